# Optimizing a Trainium2 kernel written in Bass

```python
import math
import jax, jax.numpy as jnp
from jax import lax
import numpy as np


D_MODEL = 1024
BATCH = 8
SEQ = 2048
DEPTH = 4
DEC_BATCH = 32
DEC_SEQ = 8
PAST_LEN = 8192
PAGE_SIZE = 128

MIX_WIDTH = D_MODEL
ATTN_WIDTH = MIX_WIDTH // 2
SSM_WIDTH = MIX_WIDTH - ATTN_WIDTH
HEAD_DIM = 64
N_HEADS = ATTN_WIDTH // HEAD_DIM
BRANCHES = ((128, 1), (512, 4), (2048, 16))
WIN_MAX = max(w for w, _ in BRANCHES)
SSM_GROUP_CH = 16
N_SSM_GROUPS = SSM_WIDTH // SSM_GROUP_CH
STATE_N = 64
D_FF = ((-(-8 * D_MODEL // 3) + 255) // 256) * 256
IN_WIDTH = 3 * ATTN_WIDTH + SSM_WIDTH
EPS = 1e-6
LAMBDA_RE_MAX = -1e-4

kernel_name = 'hymba_dilated_attn_s5_decoder_step'

F32 = jnp.float32


def rmsnorm(x, g):
    xf = x.astype(F32)
    y = xf * lax.rsqrt(jnp.mean(xf * xf, axis=-1, keepdims=True) + EPS)
    return (y * g.astype(F32)).astype(x.dtype)


def _branch_prompt(q, k, v, window, dil):
    b, s, h, e = q.shape
    n = window // dil
    L = s // dil
    nb = -(-L // n)
    lp = nb * n

    def sub(t):
        t = t.reshape(b, L, dil, h, e).transpose(0, 2, 1, 3, 4)
        t = jnp.pad(t, ((0, 0), (0, 0), (0, lp - L), (0, 0), (0, 0)))
        return t.reshape(b, dil, nb, n, h, e)

    def with_prev(t):
        prev = jnp.pad(t, ((0, 0), (0, 0), (1, 0), (0, 0), (0, 0), (0, 0)))[:, :, :-1]
        return jnp.concatenate([prev, t], axis=3)

    qs = sub(q)
    kk = with_prev(sub(k))
    vv = with_prev(sub(v)).astype(F32)
    scores = jnp.einsum('brcqhe,brckhe->brchqk', qs, kk, preferred_element_type=F32) * (HEAD_DIM ** -0.5)
    qi = jnp.arange(n)[:, None]
    ki = jnp.arange(2 * n)[None, :]
    dist = qi + n - ki
    band = (dist >= 0) & (dist <= n)
    has_prev = (jnp.arange(nb) > 0)[:, None, None] | (ki >= n)[None]
    mask = band[None] & has_prev
    scores = jnp.where(mask[None, None, :, None], scores, -jnp.inf)
    m = jnp.max(scores, axis=-1, keepdims=True)
    p = jnp.exp(scores - m)
    den = jnp.sum(p, axis=-1, keepdims=True)
    lse = (m + jnp.log(den))[..., 0]
    out = jnp.einsum('brchqk,brckhe->brcqhe', p / den, vv)
    out = out.reshape(b, dil, lp, h, e)[:, :, :L].transpose(0, 2, 1, 3, 4).reshape(b, s, h, e)
    lse = lse.transpose(0, 1, 2, 4, 3).reshape(b, dil, lp, h)[:, :, :L].transpose(0, 2, 1, 3).reshape(b, s, h)
    return out, lse


def _branch_sample(q, k_all, v_all, window, dil):
    b, t, h, e = q.shape
    lbuf = k_all.shape[1] - t
    n = window // dil
    idx = lbuf + jnp.arange(t)[:, None] - dil * jnp.arange(n + 1)[None, :]
    valid = idx >= 0
    idx = jnp.maximum(idx, 0)
    kg = k_all[:, idx]
    vg = v_all[:, idx].astype(F32)
    scores = jnp.einsum('bthe,btkhe->bhtk', q, kg, preferred_element_type=F32) * (HEAD_DIM ** -0.5)
    scores = jnp.where(valid[None, None], scores, -jnp.inf)
    m = jnp.max(scores, axis=-1, keepdims=True)
    p = jnp.exp(scores - m)
    den = jnp.sum(p, axis=-1, keepdims=True)
    lse = (m + jnp.log(den))[..., 0].transpose(0, 2, 1)
    out = jnp.einsum('bhtk,btkhe->bthe', p / den, vg)
    return out, lse


def _combine_branches(outs, lses):
    w = jax.nn.softmax(jnp.stack(lses, axis=0), axis=0)
    return jnp.sum(w[..., None] * jnp.stack(outs, axis=0), axis=0)


def _cmul_combine(e1, e2):
    a1r, a1i, b1r, b1i = e1
    a2r, a2i, b2r, b2i = e2
    ar = a1r * a2r - a1i * a2i
    ai = a1r * a2i + a1i * a2r
    br = a2r * b1r - a2i * b1i + b2r
    bi = a2r * b1i + a2i * b1r + b2i
    return ar, ai, br, bi


def _s5(u, h0_re, h0_im, a_re, a_im, log_dt, b_re, b_im, c_re, c_im, d_skip, glu_w, glu_b):
    bsz, t, _ = u.shape
    uf = u.astype(F32).reshape(bsz, t, N_SSM_GROUPS, SSM_GROUP_CH)
    lam_re = jnp.minimum(a_re.astype(F32), LAMBDA_RE_MAX)
    lam_im = a_im.astype(F32)
    dt = jnp.exp(log_dt.astype(F32))[:, None]
    mag = jnp.exp(lam_re * dt)
    ab_re = mag * jnp.cos(lam_im * dt)
    ab_im = mag * jnp.sin(lam_im * dt)
    den = lam_re * lam_re + lam_im * lam_im
    f_re = ((ab_re - 1.0) * lam_re + ab_im * lam_im) / den
    f_im = (ab_im * lam_re - (ab_re - 1.0) * lam_im) / den
    br = b_re.astype(F32)
    bi = b_im.astype(F32)
    bb_re = f_re[..., None] * br - f_im[..., None] * bi
    bb_im = f_re[..., None] * bi + f_im[..., None] * br
    bu_re = jnp.einsum('btgc,gnc->btgn', uf, bb_re)
    bu_im = jnp.einsum('btgc,gnc->btgn', uf, bb_im)
    h0r = h0_re.astype(F32)
    h0i = h0_im.astype(F32)
    bu_re = bu_re.at[:, 0].add(ab_re * h0r - ab_im * h0i)
    bu_im = bu_im.at[:, 0].add(ab_re * h0i + ab_im * h0r)
    a_r = jnp.broadcast_to(ab_re, bu_re.shape)
    a_i = jnp.broadcast_to(ab_im, bu_im.shape)
    _, _, h_re, h_im = lax.associative_scan(_cmul_combine, (a_r, a_i, bu_re, bu_im), axis=1)
    y = (jnp.einsum('btgn,gcn->btgc', h_re, c_re.astype(F32))
         - jnp.einsum('btgn,gcn->btgc', h_im, c_im.astype(F32))
         + d_skip.astype(F32) * uf)
    y = y.reshape(bsz, t, SSM_WIDTH)
    z = jax.nn.gelu(y)
    out = z * jax.nn.sigmoid(z @ glu_w.astype(F32) + glu_b.astype(F32))
    return out, h_re[:, -1], h_im[:, -1]


def _layer(x, k_buf, v_buf, h0_re, h0_im, norm1_g, w_in, attn_out_g, a_re, a_im, log_dt,
           b_re, b_im, c_re, c_im, d_skip, glu_w, glu_b, ssm_out_g, w_out, norm2_g,
           w_gate, w_up, w_down):
    b, t, _ = x.shape
    xn = rmsnorm(x, norm1_g)
    z = xn @ w_in
    q = z[..., :ATTN_WIDTH].reshape(b, t, N_HEADS, HEAD_DIM)
    k = z[..., ATTN_WIDTH:2 * ATTN_WIDTH].reshape(b, t, N_HEADS, HEAD_DIM)
    v = z[..., 2 * ATTN_WIDTH:3 * ATTN_WIDTH].reshape(b, t, N_HEADS, HEAD_DIM)
    u = z[..., 3 * ATTN_WIDTH:]
    if k_buf is None:
        res = [_branch_prompt(q, k, v, w, d) for (w, d) in BRANCHES]
        new_k = k[:, -WIN_MAX:]
        new_v = v[:, -WIN_MAX:]
    else:
        k_all = jnp.concatenate([k_buf.astype(k.dtype), k], axis=1)
        v_all = jnp.concatenate([v_buf.astype(v.dtype), v], axis=1)
        res = [_branch_sample(q, k_all, v_all, w, d) for (w, d) in BRANCHES]
        new_k = k
        new_v = v
    attn = _combine_branches([r[0] for r in res], [r[1] for r in res])
    attn = attn.reshape(b, t, ATTN_WIDTH).astype(x.dtype)
    ssm, h_re, h_im = _s5(u, h0_re, h0_im, a_re, a_im, log_dt, b_re, b_im, c_re, c_im,
                          d_skip, glu_w, glu_b)
    ssm = ssm.astype(x.dtype)
    mixed = jnp.concatenate([rmsnorm(attn, attn_out_g), rmsnorm(ssm, ssm_out_g)], axis=-1) @ w_out
    h = x + mixed
    hn = rmsnorm(h, norm2_g)
    y = h + (jax.nn.silu(hn @ w_gate) * (hn @ w_up)) @ w_down
    return y, new_k, new_v, h_re, h_im


def setup_inputs(seed: int = 0) -> dict:
    key = jax.random.key(seed)
    ks = jax.random.split(key, 32)

    def nrm(k, shape, scale):
        return jax.random.normal(k, shape, F32) * scale

    lbuf = min(WIN_MAX, PAST_LEN)
    G, N, C = N_SSM_GROUPS, STATE_N, SSM_GROUP_CH
    a_im = (jnp.pi * jnp.arange(N, dtype=F32))[None, None, :] + nrm(ks[8], (DEPTH, G, N), 0.01)
    return {
        'x_prompt': nrm(ks[0], (BATCH, SEQ, D_MODEL), 1.0),
        'x_sample': nrm(ks[1], (DEC_BATCH, DEC_SEQ, D_MODEL), 1.0),
        'cache_attn_k': nrm(ks[2], (DEPTH, DEC_BATCH, lbuf, N_HEADS, HEAD_DIM), 1.0),
        'cache_attn_v': nrm(ks[3], (DEPTH, DEC_BATCH, lbuf, N_HEADS, HEAD_DIM), 1.0),
        'state_ssm_re': nrm(ks[4], (DEPTH, DEC_BATCH, G, N), 1.0),
        'state_ssm_im': nrm(ks[5], (DEPTH, DEC_BATCH, G, N), 1.0),
        'norm1_g': 1.0 + nrm(ks[6], (DEPTH, D_MODEL), 0.02),
        'w_in': nrm(ks[7], (DEPTH, D_MODEL, IN_WIDTH), D_MODEL ** -0.5),
        'attn_out_g': 1.0 + nrm(ks[9], (DEPTH, ATTN_WIDTH), 0.02),
        'ssm_a_re': -0.5 + nrm(ks[10], (DEPTH, G, N), 0.01),
        'ssm_a_im': a_im,
        'ssm_log_dt': jax.random.uniform(ks[11], (DEPTH, G), F32, math.log(1e-3), math.log(1e-1)),
        'ssm_b_re': nrm(ks[12], (DEPTH, G, N, C), (2 * C) ** -0.5),
        'ssm_b_im': nrm(ks[13], (DEPTH, G, N, C), (2 * C) ** -0.5),
        'ssm_c_re': nrm(ks[14], (DEPTH, G, C, N), (2 * N) ** -0.5),
        'ssm_c_im': nrm(ks[15], (DEPTH, G, C, N), (2 * N) ** -0.5),
        'ssm_d': nrm(ks[16], (DEPTH, G, C), 1.0),
        'ssm_glu_w': nrm(ks[17], (DEPTH, SSM_WIDTH, SSM_WIDTH), SSM_WIDTH ** -0.5),
        'ssm_glu_b': nrm(ks[18], (DEPTH, SSM_WIDTH), 0.02),
        'ssm_out_g': 1.0 + nrm(ks[19], (DEPTH, SSM_WIDTH), 0.02),
        'w_out': nrm(ks[20], (DEPTH, MIX_WIDTH, D_MODEL), MIX_WIDTH ** -0.5),
        'norm2_g': 1.0 + nrm(ks[21], (DEPTH, D_MODEL), 0.02),
        'ffn_w_gate': nrm(ks[22], (DEPTH, D_MODEL, D_FF), D_MODEL ** -0.5),
        'ffn_w_up': nrm(ks[23], (DEPTH, D_MODEL, D_FF), D_MODEL ** -0.5),
        'ffn_w_down': nrm(ks[24], (DEPTH, D_FF, D_MODEL), D_FF ** -0.5),
        'final_norm_g': 1.0 + nrm(ks[25], (D_MODEL,), 0.02),
    }


def reference(x_prompt, x_sample, cache_attn_k, cache_attn_v, state_ssm_re, state_ssm_im,
              norm1_g, w_in, attn_out_g, ssm_a_re, ssm_a_im, ssm_log_dt, ssm_b_re, ssm_b_im,
              ssm_c_re, ssm_c_im, ssm_d, ssm_glu_w, ssm_glu_b, ssm_out_g, w_out, norm2_g,
              ffn_w_gate, ffn_w_up, ffn_w_down, final_norm_g):
    xp = x_prompt
    xs = x_sample
    zeros_p = jnp.zeros((x_prompt.shape[0], N_SSM_GROUPS, STATE_N), F32)
    pk, pv, pr, pim = [], [], [], []
    sk, sv, sr, sim = [], [], [], []
    for l in range(DEPTH):
        lw = (norm1_g[l], w_in[l], attn_out_g[l], ssm_a_re[l], ssm_a_im[l], ssm_log_dt[l],
              ssm_b_re[l], ssm_b_im[l], ssm_c_re[l], ssm_c_im[l], ssm_d[l], ssm_glu_w[l],
              ssm_glu_b[l], ssm_out_g[l], w_out[l], norm2_g[l], ffn_w_gate[l], ffn_w_up[l],
              ffn_w_down[l])
        xp, k_p, v_p, r_p, i_p = _layer(xp, None, None, zeros_p, zeros_p, *lw)
        pk.append(k_p)
        pv.append(v_p)
        pr.append(r_p)
        pim.append(i_p)
        xs, k_s, v_s, r_s, i_s = _layer(xs, cache_attn_k[l], cache_attn_v[l],
                                        state_ssm_re[l], state_ssm_im[l], *lw)
        sk.append(k_s)
        sv.append(v_s)
        sr.append(r_s)
        sim.append(i_s)
    y_prompt = rmsnorm(xp, final_norm_g)
    y_sample = rmsnorm(xs, final_norm_g)
    return (y_prompt, y_sample,
            jnp.stack(pk), jnp.stack(pv), jnp.stack(pr), jnp.stack(pim),
            jnp.stack(sk), jnp.stack(sv), jnp.stack(sr), jnp.stack(sim))
```

```python
import numpy as np
import concourse.bass as bass
import concourse.mybir as mybir
from concourse.bass_utils import run_bass_kernel_spmd
from contextlib import ExitStack

F32 = mybir.dt.float32
BF16 = mybir.dt.bfloat16
AF = mybir.ActivationFunctionType
ALU = mybir.AluOpType

NL = 4
NP, NSM, NT = 2048, 32, 2080
NCH = NT // 8
CTS = [(0, 512), (512, 512), (1024, 512), (1536, 512), (2048, 32)]
DFF = 2816
NFT = DFF // 128
EPS = 1e-6
MVALS = list(range(9)) + list(range(16, 129, 8)) + [256, 512, 1024]
NMV = len(MVALS)
MIDX = {m: i for i, m in enumerate(MVALS)}
MASKNEG = -30000.0
TWO_PI = 2.0 * np.pi
CW1 = 6.28125
CW2 = TWO_PI - CW1
MAGIC = 12582912.0
ARENA = 84 * 1024


def _sample_positions():
    pos = []
    for f in range(6):
        for p in range(128):
            blk, rr = p // 8, p % 8
            pos.append(16 * (16 * f + blk) + rr)
    for nt in range(4):
        for p in range(128):
            pos.append(1536 + 128 * nt + p)
    return np.array(pos, dtype=np.int64)


SPOS = _sample_positions()


def _sample_mult():
    m = np.zeros((128, 11, 2, 8), np.float32)
    branches = ((128, 1), (512, 4), (2048, 16))
    for ti in range(11):
        for p in range(128):
            if ti < 10:
                pos = SPOS[128 * ti + p]
            else:
                if p >= 8:
                    continue
                pos = 2048 + p
            for t in range(8):
                dlt = 2048 + t - pos
                c = 0
                for (w, d) in branches:
                    if dlt >= 0 and dlt % d == 0 and dlt <= w:
                        c += 1
                m[p, ti, :, t] = c
    return m


class Tok:
    __slots__ = ("w", "r", "serial")

    def __init__(self, fence=None):
        self.w = None
        self.r = dict(fence) if fence else {}
        self.serial = False


class Sem:
    __slots__ = ("h", "idx", "val")

    def __init__(self, h, idx):
        self.h = h
        self.idx = idx
        self.val = 0


class Eng:
    def __init__(self, name, h, sem, kind):
        self.name = name
        self.h = h
        self.sem = sem
        self.kind = kind
        self.n = 0
        self.seen = {}


class KB:
    def __init__(self, nc, es):
        self.nc = nc
        self.es = es
        self.nsem = 0
        self.pe = Eng("pe", nc.tensor, self.new_sem("s_pe"), "pe")
        self.act = Eng("act", nc.scalar, self.new_sem("s_act"), "act")
        self.dve = Eng("dve", nc.vector, self.new_sem("s_dve"), "dve")
        self.pool = Eng("pool", nc.gpsimd, self.new_sem("s_pool"), "pool")
        self.sp = Eng("sp", nc.sync, self.new_sem("s_sp"), "sp")
        self.dslots = {}
        self.di = {}
        for q in (self.sp, self.pool):
            self.dslots[q.name] = [self.new_sem("d_%s%d" % (q.name, i)) for i in range(24)]
            self.di[q.name] = 0
        self.ninstr = 0

    def new_sem(self, name):
        h = self.es.enter_context(self.nc.semaphore(name))
        s = Sem(h, self.nsem)
        self.nsem += 1
        return s

    def _wait(self, eng, ev):
        sem, val, src = ev
        if eng.seen.get(sem.idx, 0) >= val:
            return
        eng.h.wait_ge(sem.h, val)
        eng.seen[sem.idx] = val

    def _deps(self, eng, reads, writes, is_dma):
        for t in reads:
            if t.w is not None:
                src = t.w[2]
                if (not is_dma) and src is eng and eng.kind == "pe":
                    pass
                else:
                    self._wait(eng, t.w)
            if getattr(t, "serial", False):
                for e in t.r.values():
                    if e[2] is not eng:
                        self._wait(eng, e)
        for t in writes:
            if t.w is not None:
                if is_dma or t.w[2] is not eng or eng.kind != "pe":
                    self._wait(eng, t.w)
            for e in t.r.values():
                if is_dma or e[2] is not eng or eng.kind != "pe":
                    self._wait(eng, e)

    def op(self, eng, fn, reads=(), writes=(), inc=True):
        self._deps(eng, reads, writes, False)
        ins = fn()
        self.ninstr += 1
        if inc:
            eng.n += 1
            ins.then_inc(eng.sem.h, 1)
            ev = (eng.sem, eng.n, eng)
        else:
            ev = (eng.sem, eng.n + 1, eng)
        for t in writes:
            t.w = ev
            t.r = {}
        for t in reads:
            t.r[eng.name] = ev
        return ins

    def dma(self, q, out, in_, reads=(), writes=(), **kw):
        self._deps(q, reads, writes, True)
        slots = self.dslots[q.name]
        s = slots[self.di[q.name] % len(slots)]
        self.di[q.name] += 1
        if s.val > 0:
            self._wait(q, (s, s.val, None))
        ins = q.h.dma_start(out=out, in_=in_, **kw)
        s.val += 16
        ins.then_inc(s.h, 16)
        self.ninstr += 1
        ev = (s, s.val, None)
        for t in writes:
            t.w = ev
            t.r = {}
        for t in reads:
            t.r["dma%d" % s.idx] = ev
        return ins

    def finish(self):
        for q in (self.sp, self.pool):
            for s in self.dslots[q.name]:
                if s.val > 0:
                    self._wait(self.sp, (s, s.val, None))
        for e in (self.pe, self.act, self.dve, self.pool):
            if e.n > 0:
                self._wait(self.sp, (e.sem, e.n, e))


class _Stop(Exception):
    pass


class Buf:
    def __init__(self, t, fence=None):
        self.t = t
        self.toks = {}
        self.fence = fence

    def tok(self, key=0):
        if key not in self.toks:
            self.toks[key] = Tok(self.fence)
        return self.toks[key]

    def __getitem__(self, idx):
        return self.t[idx]


def build(cfg):
    nlayers = cfg.get("nlayers", NL)
    stage = cfg.get("stage", "full")
    dbg = cfg.get("dbg", {})
    nc = bass.Bass("TRN2", target_bir_lowering=False)

    def din(name, shape):
        return nc.dram_tensor(name, list(shape), F32, kind="ExternalInput").ap()

    def dout(name, shape):
        return nc.dram_tensor(name, list(shape), F32, kind="ExternalOutput").ap()

    d_x = din("xT", [128, 8, NT])
    d_ck = din("ckT", [NL, 16, 128, 1280])
    d_cv = din("cvN", [NL, 16, 128, 1280])
    d_pvec = din("pvec", [128, 136])
    d_are = din("aL_re", [128, NL, 16])
    d_aim = din("aL_im", [128, NL, 16])
    d_ldt = din("aL_ldt", [128, NL, 16])
    d_bre = din("bL_re", [128, NL, 256])
    d_bim = din("bL_im", [128, NL, 256])
    d_cre = din("cL_re", [128, NL, 256])
    d_cim = din("cL_im", [128, NL, 256])
    d_h0 = din("h0L", [128, NL, 2, 16, 4])
    d_win = din("w_in", [NL, 1024, 2048])
    d_wout = din("w_out", [NL, 1024, 1024])
    d_glu = din("glu_w", [NL, 512, 512])
    d_wg = din("wg", [NL, 1024, DFF])
    d_wu = din("wu", [NL, 1024, DFF])
    d_wd = din("wd", [NL, DFF, 1024])
    d_cid = din("c_ident", [128, 128])
    d_cmc = din("c_maskcur", [128, 512])
    d_cmp = din("c_maskprev", [128, 512])
    d_csm = din("c_smask", [128, 176])
    d_cmv = din("c_mv", [128, NMV * 16])
    d_crm = din("c_rowmask", [128, 2])

    o_y = dout("yT", [128, 8, NT])
    o_k = dout("pkT", [NL, 128, 4, NT])
    o_v = dout("pv", [NL, NP, 512])
    o_sv = dout("sv", [NL, NSM, 512])
    o_h = dout("hout", [128, NL, 2, 16, 5])
    dbg_out = {name: dout("dbg_" + name, shape) for name, shape in dbg.items()}

    es = ExitStack()
    with es:
        kb = KB(nc, es)
        PE, ACT, DVE, POOL, SP = kb.pe, kb.act, kb.dve, kb.pool, kb.sp
        V, S, T = nc.vector, nc.scalar, nc.tensor

        def sb(name, shape, dt=F32):
            return Buf(es.enter_context(nc.sbuf_tensor("s_" + name, list(shape), dt)))

        banks = [Buf(es.enter_context(nc.psum_tensor("bank%d" % i, [128, 512], F32))) for i in range(8)]
        for b_ in banks:
            b_.tok().serial = True

        xT = sb("xT", [128, 8, NT])
        pvec = sb("pvec", [128, 136])
        ident = sb("ident", [128, 128], BF16)
        identf = sb("identf", [128, 128])
        ones = sb("ones", [128, 128], BF16)
        maskcur = sb("maskcur", [128, 512], BF16)
        maskprev = sb("maskprev", [128, 512], BF16)
        smask = sb("smask", [128, 176], BF16)
        mvt = sb("mvt", [128, NMV, 16])
        rowmask = sb("rowmask", [128, 2])
        xnT = sb("xnT", [128, 8, NT], BF16)
        mixA = sb("mixA", [128, 4, NT], BF16)
        arena = es.enter_context(nc.sbuf_tensor("arena", [128, ARENA // 4], F32))

        class Phase:
            prev_bufs = []

            def __init__(self, base=0):
                fence = {}
                for b in Phase.prev_bufs:
                    for t in b.toks.values():
                        evs = list(t.r.values())
                        if t.w is not None:
                            evs.append(t.w)
                        for ev in evs:
                            k = ev[0].idx
                            if k not in fence or fence[k][1] < ev[1]:
                                fence[k] = ev
                self.fence = {("f", k): v for k, v in fence.items()}
                self.off = base
                self.bufs = []
                Phase.prev_bufs = self.bufs

            def take(self, shape, dt=F32):
                fshape = list(shape[1:])
                n = 1
                for s_ in fshape:
                    n *= s_
                nbytes = n * (4 if dt == F32 else 2)
                nwords = (nbytes + 3) // 4
                ap = arena[:, self.off:self.off + nwords]
                self.off += nwords
                assert self.off * 4 <= ARENA, ("arena overflow", self.off * 4)
                Phase.maxoff = max(getattr(Phase, "maxoff", 0), self.off * 4)
                if dt != F32:
                    ap = ap.bitcast(dt)[:, 0:n]
                if len(fshape) > 1:
                    names = "abcdefgh"[:len(fshape)]
                    pat = "p (" + " ".join(names) + ") -> p " + " ".join(names)
                    ap = ap.rearrange(pat, **{names[i]: fshape[i] for i in range(len(fshape))})
                b = Buf(ap, self.fence)
                self.bufs.append(b)
                return b

        for ci, (c0, cn) in enumerate(CTS):
            kb.dma(SP, xT[:, :, c0:c0 + cn], d_x[:, :, c0:c0 + cn], writes=[xT.tok(ci)])
        kb.dma(SP, pvec[:], d_pvec[:, :], writes=[pvec.tok()])
        kb.dma(SP, mvt[:], d_cmv.rearrange("p (m g) -> p m g", g=16), writes=[mvt.tok()])
        kb.dma(SP, rowmask[:], d_crm[:, :], writes=[rowmask.tok()])
        kb.dma(POOL, ident[:], d_cid[:, :], writes=[ident.tok()])
        kb.dma(SP, identf[:], d_cid[:, :], writes=[identf.tok()])
        kb.dma(POOL, maskcur[:], d_cmc[:, :], writes=[maskcur.tok()])
        kb.dma(POOL, maskprev[:], d_cmp[:, :], writes=[maskprev.tok()])
        kb.dma(POOL, smask[:], d_csm[:, :], writes=[smask.tok()])
        kb.op(DVE, lambda: V.memset(ones[:], 1.0), writes=[ones.tok()])

        def pv_col(l, which, c):
            base = {"n1": 0, "n2": 8, "ag": 16, "sg": 20, "gb": 24, "sd": 28}[which]
            o = 32 * l + base + c
            return pvec[:, o:o + 1]

        rr = {"b": 0}

        def next_bank(group):
            i = group[rr["b"] % len(group)]
            rr["b"] += 1
            return banks[i]

        def cut(n):
            if cfg.get("cut") == n:
                raise _Stop()

        def dump(name, ap, reads):
            if name in dbg_out:
                kb.dma(POOL, dbg_out[name], ap, reads=reads, max_dma_last_dim=2048)

        def rmsnorm_to_bf16(src, src_tok, nchunk, dst, dst_tok, gcol, width, ci, sq, rt, bank_group):
            c0, cn = CTS[ci]
            bk = next_bank(bank_group)
            for c in range(nchunk):
                sqb = sq[c % 2]
                kb.op(ACT, lambda c=c, sqb=sqb: S.activation(out=sqb[:, 0:cn], in_=src[:, c, c0:c0 + cn], func=AF.Square),
                      reads=[src_tok], writes=[sqb.tok()])
                kb.op(PE, lambda c=c, sqb=sqb: T.matmul(bk[:, 0:cn], ones[:], sqb[:, 0:cn], start=(c == 0), stop=(c == nchunk - 1)),
                      reads=[ones.tok(), sqb.tok()], writes=[bk.tok()], inc=True)
            kb.op(ACT, lambda: S.activation(out=rt[:, 0:cn], in_=bk[:, 0:cn], func=AF.Sqrt, scale=1.0 / width, bias=EPS),
                  reads=[bk.tok()], writes=[rt.tok()])
            kb.op(DVE, lambda: V.reciprocal(out=rt[:, 0:cn], in_=rt[:, 0:cn]), reads=[rt.tok()], writes=[rt.tok()])
            for c in range(nchunk):
                kb.op(DVE, lambda c=c: V.scalar_tensor_tensor(out=dst[:, c, c0:c0 + cn], in0=src[:, c, c0:c0 + cn], scalar=gcol(c),
                                                              in1=rt[:, 0:cn], op0=ALU.mult, op1=ALU.mult),
                      reads=[src_tok, rt.tok(), pvec.tok()], writes=[dst_tok])

        if stage == "load":
            dump("x0", xT[:, 0, :], [xT.tok(c) for c in range(5)])
            nlayers = 0
        try:
            for l in range(nlayers):
                ph = Phase()
                sq = [ph.take([128, 512], BF16), ph.take([128, 512], BF16)]
                rt = ph.take([128, 512])
                qz = ph.take([128, 2, NT], BF16)
                kTp = ph.take([128, NT], BF16)
                vaug = ph.take([128, 3, 16, 192], BF16)
                vS = ph.take([128, 4, 128], BF16)
                wsl = [ph.take([128, 8, 3, 128], BF16), ph.take([128, 8, 3, 128], BF16)]
                pT = [ph.take([128, 512], BF16) for _ in range(3)]
                rcp = [ph.take([128, 512]) for _ in range(1)]
                kst = [ph.take([128, 512]) for _ in range(2)]
                vst = [ph.take([128, 4, 128]) for _ in range(1)]
                svst = ph.take([128, 128])
                kcs = [ph.take([128, 1280], BF16) for _ in range(4)]
                vcs = [ph.take([128, 10, 128], BF16) for _ in range(4)]
                pSs = [ph.take([128, 176], BF16) for _ in range(2)]
                pSns = [ph.take([128, 16], BF16) for _ in range(2)]
                rSs = [ph.take([128, 16]) for _ in range(2)]

                if l == 0 or stage != "full":
                    for ci in range(5):
                        rmsnorm_to_bf16(xT, xT.tok(ci), 8, xnT, xnT.tok(ci), lambda c: pv_col(l, "n1", c), 1024.0, ci, sq, rt, [6, 7])
                if stage == "norm":
                    dump("xnT", xnT[:, :, :], [xnT.tok(c) for c in range(5)])
                    break
                kb.op(POOL, lambda: nc.gpsimd.memset(qz[:], 0.0), writes=[qz.tok(("z", 0)), qz.tok(("z", 1))])
                kb.op(POOL, lambda: nc.gpsimd.memset(vaug[:, :, :, 64:128], 1.0), writes=[vaug.tok("ones")])
                kb.op(POOL, lambda: nc.gpsimd.memset(vS[:], 0.0), writes=[vS.tok()])
                for pSn_i in pSns:
                    kb.op(POOL, lambda: nc.gpsimd.memset(pSn_i[:], 0.0), writes=[pSn_i.tok()])

                cut(1)
                win_v = d_win[l].rearrange("(kc p) n -> p kc n", p=128)
                IPB = [6, 7, 0, 1, 2, 3, 4, 5]
                def load_pair_w(hp_):
                    w_ = wsl[hp_ % 2]
                    for j in range(3):
                        kb.dma(POOL, w_[:, :, j, :], win_v[:, :, 512 * j + 128 * hp_: 512 * j + 128 * hp_ + 128], writes=[w_.tok(j)])
                load_pair_w(0)
                load_pair_w(1)
                for hp in range(4):
                    w = wsl[hp % 2]
                    cut(21)
                    for ci, (c0, cn) in enumerate(CTS):
                        bk = next_bank(IPB)
                        for kc in range(8):
                            kb.op(PE, lambda kc=kc: T.matmul(bk[:, 0:cn], w[:, kc, 0, :], xnT[:, kc, c0:c0 + cn], start=(kc == 0), stop=(kc == 7)),
                                  reads=[w.tok(0), xnT.tok(ci)], writes=[bk.tok()], inc=(kc == 7))
                        if ci == 1:
                            cut(31)
                        kb.op(ACT, lambda: S.copy(out=qz[0:64, 0, c0:c0 + cn], in_=bk[0:64, 0:cn]),
                              reads=[bk.tok(), qz.tok(("z", 0))], writes=[qz.tok((0, ci))])
                        if ci == 1:
                            cut(32)
                        kb.op(DVE, lambda: V.tensor_copy(out=qz[64:128, 1, c0:c0 + cn], in_=bk[64:128, 0:cn]),
                              reads=[bk.tok(), qz.tok(("z", 1))], writes=[qz.tok((1, ci))])
                        if ci == 1:
                            cut(33)
                        cut(22)
                        bk = next_bank(IPB)
                        for kc in range(8):
                            kb.op(PE, lambda kc=kc: T.matmul(bk[:, 0:cn], w[:, kc, 1, :], xnT[:, kc, c0:c0 + cn], start=(kc == 0), stop=(kc == 7)),
                                  reads=[w.tok(1), xnT.tok(ci)], writes=[bk.tok()], inc=(kc == 7))
                        ks = kst[ci % 2]
                        if ci == 1:
                            cut(34)
                        kb.op(ACT, lambda: S.copy(out=kTp[:, c0:c0 + cn], in_=bk[:, 0:cn]), reads=[bk.tok()], writes=[kTp.tok(ci)])
                        if ci == 1:
                            cut(35)
                        if cfg.get("kcopy", "dve") == "dve":
                            kb.op(DVE, lambda: V.tensor_copy(out=ks[:, 0:cn], in_=bk[:, 0:cn]), reads=[bk.tok()], writes=[ks.tok()])
                        elif cfg.get("kcopy") == "act":
                            kb.op(ACT, lambda: S.copy(out=ks[:, 0:cn], in_=bk[:, 0:cn]), reads=[bk.tok()], writes=[ks.tok()])
                        cut(23)
                        if ci == 1:
                            cut(36)
                        kb.dma(SP, o_k[l, :, hp, c0:c0 + cn], ks[:, 0:cn], reads=[ks.tok()])
                        cut(24)
                        if ci == 1:
                            cut(25)
                        if ci == 3:
                            cut(26)
                    cut(2)
                    def tok_ap(o, ti, kc):
                        if o == 0:
                            return xnT[:, kc, 128 * ti:128 * ti + 128]
                        if o == 1:
                            G, r = ti // 4, ti % 4
                            return xnT[:, kc, 512 * G + r:512 * G + 512:4]
                        return xnT[:, kc, ti:2048:16]
                    for o in range(3):
                        cut(3 + o)
                        for tb in range(4):
                            bk = next_bank(IPB)
                            for j in range(4):
                                ti = 4 * tb + j
                                for kc in range(8):
                                    kb.op(PE, lambda kc=kc, ti=ti, j=j: T.matmul(bk[:, 128 * j:128 * j + 128], tok_ap(o, ti, kc), w[:, kc, 2, :],
                                                                                  start=(kc == 0), stop=(kc == 7)),
                                          reads=[w.tok(2)] + [xnT.tok(c) for c in range(4)], writes=[bk.tok()], inc=(kc == 7 and j == 3))
                            bv = bk[:].rearrange("p (j c) -> p j c", c=128)
                            kb.op(ACT, lambda: S.copy(out=vaug[:, o, 4 * tb:4 * tb + 4, 0:64], in_=bv[:, :, 0:64]),
                                  reads=[bk.tok()], writes=[vaug.tok((o, tb, 0))])
                            kb.op(DVE, lambda: V.tensor_copy(out=vaug[:, o, 4 * tb:4 * tb + 4, 128:192], in_=bv[:, :, 64:128]),
                                  reads=[bk.tok()], writes=[vaug.tok((o, tb, 1))])
                            if o == 0:
                                vs_ = vst[0]
                                kb.op(DVE, lambda: V.tensor_copy(out=vs_[:], in_=bv), reads=[bk.tok()], writes=[vs_.tok()])
                                kb.dma(SP, o_v[l, 512 * tb:512 * tb + 512, 128 * hp:128 * hp + 128].rearrange("(j p) c -> p j c", p=128), vs_[:],
                                       reads=[vs_.tok()])
                    cut(6)
                    bk = next_bank([6, 7])
                    for kc in range(8):
                        kb.op(PE, lambda kc=kc: T.matmul(bk[0:32, 0:128], xnT[:, kc, 2048:2080], w[:, kc, 2, :], start=(kc == 0), stop=(kc == 7)),
                              reads=[w.tok(2), xnT.tok(4)], writes=[bk.tok()], inc=(kc == 7))
                    kb.op(DVE, lambda: V.tensor_copy(out=svst[0:32, :], in_=bk[0:32, 0:128]), reads=[bk.tok()], writes=[svst.tok()])
                    kb.dma(SP, o_sv[l, :, 128 * hp:128 * hp + 128], svst[0:32, :], reads=[svst.tok()])
                    bk = next_bank([6, 7])
                    for b in range(4):
                        for kc in range(8):
                            kb.op(PE, lambda kc=kc, b=b: T.matmul(bk[0:8, 128 * b:128 * b + 128], xnT[:, kc, 2048 + 8 * b:2056 + 8 * b], w[:, kc, 2, :],
                                                                  start=(kc == 0), stop=(kc == 7)),
                                  reads=[w.tok(2), xnT.tok(4)], writes=[bk.tok()], inc=(kc == 7 and b == 3))
                    kb.op(DVE, lambda: V.tensor_copy(out=vS[0:8, :, :], in_=bk[0:8, :].rearrange("p (b c) -> p b c", c=128)),
                          reads=[bk.tok()], writes=[vS.tok()])

                    cut(7)
                    if hp + 2 < 4:
                        load_pair_w(hp + 2)
                    if stage == "inproj" and hp == 0:
                        dump("qz", qz[:], [qz.tok((0, c)) for c in range(5)] + [qz.tok((1, c)) for c in range(5)])
                        dump("kTp", kTp[:], [kTp.tok(c) for c in range(5)])
                        dump("vaug", vaug[:].rearrange("p o t c -> p (o t c)"), [vaug.tok((o, tb, a)) for o in range(3) for tb in range(4) for a in range(2)] + [vaug.tok("ones")])
                        dump("vS", vS[:].rearrange("p b c -> p (b c)"), [vS.tok()])
                        break

                    qz_reads = lambda a: [qz.tok((a, c)) for c in range(4)] + [qz.tok(("z", 1 - a))]
                    kT_reads = [kTp.tok(c) for c in range(4)]
                    for a in range(2):
                        tiles = []
                        for c in range(16):
                            tiles.append(("cur", slice(128 * c, 128 * c + 128), slice(128 * c, 128 * c + 128), (0, c),
                                          [(c // 4, slice(128 * (c % 4), 128 * (c % 4) + 128), slice(0, 128))]))
                        for r in range(4):
                            for j in range(4):
                                s_ = slice(512 * j + r, 512 * j + 512, 4)
                                tiles.append(("cur", s_, s_, (1, 4 * j + r), [(j, slice(r, 512, 4), slice(0, 128))]))
                        for r in range(16):
                            s_ = slice(r, 2048, 16)
                            tiles.append(("cur", s_, s_, (2, r), [(G, slice(r, 512, 16), slice(32 * G, 32 * G + 32)) for G in range(4)]))
                        for c in range(1, 16):
                            tiles.append(("prev", slice(128 * (c - 1), 128 * c), slice(128 * c, 128 * c + 128), (0, c - 1),
                                          [(c // 4, slice(128 * (c % 4), 128 * (c % 4) + 128), slice(0, 128))]))
                        for r in range(4):
                            for j in range(1, 4):
                                ks_ = slice(512 * (j - 1) + r, 512 * j, 4)
                                qs_ = slice(512 * j + r, 512 * j + 512, 4)
                                tiles.append(("prev", ks_, qs_, (1, 4 * (j - 1) + r), [(j, slice(r, 512, 4), slice(0, 128))]))
                        started = [False] * 4
                        nb_ = 0
                        i0 = 0

                        def emit_pv(batch, pt_):
                            nbt = len(batch)
                            for j, tl in enumerate(batch):
                                o, vt = tl[3]
                                nd = len(tl[4])
                                for di, (G, ocols, pcols) in enumerate(tl[4]):
                                    xb = banks[G]
                                    st = not started[G]
                                    started[G] = True
                                    pc = slice(128 * j + pcols.start, 128 * j + pcols.stop)
                                    lastpv = (j == nbt - 1 and di == nd - 1)
                                    kb.op(PE, lambda o=o, vt=vt, ocols=ocols, pc=pc, xb=xb, st=st:
                                          T.matmul(xb[:, ocols], vaug[:, o, vt, 64 * a:64 * a + 128], pt_[:, pc], start=st, stop=False, skip_group_check=True),
                                          reads=[pt_.tok(), vaug.tok((o, vt // 4, a)), vaug.tok("ones")], writes=[xb.tok()], inc=lastpv)
                        pending = None
                        while i0 < len(tiles):
                            mk = tiles[i0][0]
                            batch = [tiles[i0]]
                            while len(batch) < 4 and i0 + len(batch) < len(tiles) and tiles[i0 + len(batch)][0] == mk:
                                batch.append(tiles[i0 + len(batch)])
                            i0 += len(batch)
                            nbt = len(batch)
                            sb_ = banks[4 + (nb_ % 2)]
                            pt_ = pT[nb_ % 3]
                            nb_ += 1
                            mt = maskcur if mk == "cur" else maskprev
                            kb.op(PE, lambda: T.matmul(sb_[:, 0:128 * nbt], ident[:], mt[:, 0:128 * nbt], start=True, stop=False),
                                  reads=[ident.tok(), mt.tok()], writes=[sb_.tok()], inc=False)
                            for j, tl in enumerate(batch):
                                kb.op(PE, lambda j=j, tl=tl: T.matmul(sb_[:, 128 * j:128 * j + 128], kTp[:, tl[1]], qz[:, a, tl[2]], start=False, stop=(j == nbt - 1)),
                                      reads=kT_reads + qz_reads(a), writes=[sb_.tok()], inc=(j == nbt - 1))
                            kb.op(ACT, lambda: S.activation(out=pt_[:, 0:128 * nbt], in_=sb_[:, 0:128 * nbt], func=AF.Exp, scale=0.125),
                                  reads=[sb_.tok()], writes=[pt_.tok()])
                            if pending is not None:
                                emit_pv(*pending)
                            pending = (batch, pt_)
                        emit_pv(*pending)
                        for G in range(4):
                            xb = banks[G]
                            rc = rcp[0]
                            if a == 0:
                                kb.op(DVE, lambda: V.reciprocal(out=rc[0:64, :], in_=xb[64:128, :]), reads=[xb.tok()], writes=[rc.tok()])
                                kb.op(DVE, lambda: V.tensor_tensor(out=mixA[0:64, hp, 512 * G:512 * G + 512], in0=xb[0:64, :], in1=rc[0:64, :], op=ALU.mult),
                                      reads=[xb.tok(), rc.tok()], writes=[mixA.tok((hp, G, 0))])
                            else:
                                kb.op(DVE, lambda: V.reciprocal(out=rc[64:128, :], in_=xb[0:64, :]), reads=[xb.tok()], writes=[rc.tok()])
                                kb.op(DVE, lambda: V.tensor_tensor(out=mixA[64:128, hp, 512 * G:512 * G + 512], in0=xb[64:128, :], in1=rc[64:128, :], op=ALU.mult),
                                      reads=[xb.tok(), rc.tok()], writes=[mixA.tok((hp, G, 1))])

                    def samp_scores(b):
                        kc_ = kcs[b]
                        vc_ = vcs[b]
                        pS_ = pSs[b % 2]
                        pSn_ = pSns[b % 2]
                        kb.dma(POOL, kc_[:], d_ck[l, 4 * b + hp, :, :], writes=[kc_.tok()])
                        kb.dma(POOL, vc_[:], d_cv[l, 4 * b + hp, :, :].rearrange("p (t c) -> p t c", c=128), writes=[vc_.tok()])
                        sbk = next_bank([4, 5])
                        qs = qz[:, :, 2048 + 8 * b:2056 + 8 * b]
                        qrd = [qz.tok((0, 4)), qz.tok((1, 4)), qz.tok(("z", 0)), qz.tok(("z", 1))]
                        for ti in range(10):
                            kb.op(PE, lambda ti=ti: T.matmul(sbk[:, 16 * ti:16 * ti + 16].rearrange("p (a q) -> p a q", a=2), kc_[:, 128 * ti:128 * ti + 128], qs,
                                                             start=True, stop=True),
                                  reads=[kc_.tok()] + qrd, writes=[sbk.tok()], inc=False)
                        kb.op(PE, lambda: T.matmul(sbk[0:8, 160:176].rearrange("p (a q) -> p a q", a=2), kTp[:, 2048 + 8 * b:2056 + 8 * b], qs, start=True, stop=True),
                              reads=[kTp.tok(4)] + qrd, writes=[sbk.tok()], inc=True)
                        kb.op(ACT, lambda: S.activation(out=pS_[:, 0:160], in_=sbk[:, 0:160], func=AF.Exp, scale=0.125), reads=[sbk.tok()], writes=[pS_.tok()])
                        kb.op(ACT, lambda: S.activation(out=pSn_[0:8, :], in_=sbk[0:8, 160:176], func=AF.Exp, scale=0.125), reads=[sbk.tok()], writes=[pSn_.tok()])
                        kb.op(DVE, lambda: V.tensor_tensor(out=pS_[:, 0:160], in0=pS_[:, 0:160], in1=smask[:, 0:160], op=ALU.mult),
                              reads=[pS_.tok(), smask.tok()], writes=[pS_.tok()])
                        kb.op(DVE, lambda: V.tensor_tensor(out=pSn_[0:8, :], in0=pSn_[0:8, :], in1=smask[0:8, 160:176], op=ALU.mult),
                              reads=[pSn_.tok(), smask.tok()], writes=[pSn_.tok()])

                    def samp_pv(b):
                        vc_ = vcs[b]
                        pS_ = pSs[b % 2]
                        pSn_ = pSns[b % 2]
                        rS_ = rSs[b % 2]
                        nb = next_bank([6, 7])
                        db = next_bank([6, 7])
                        for ti in range(10):
                            kb.op(PE, lambda ti=ti: T.matmul(nb[:, 0:16], vc_[:, ti, :], pS_[:, 16 * ti:16 * ti + 16], start=(ti == 0), stop=False),
                                  reads=[vc_.tok(), pS_.tok()], writes=[nb.tok()], inc=False)
                        kb.op(PE, lambda: T.matmul(nb[:, 0:16], vS[:, b, :], pSn_[:, :], start=False, stop=True), reads=[vS.tok(), pSn_.tok()], writes=[nb.tok()], inc=True)
                        for ti in range(10):
                            kb.op(PE, lambda ti=ti: T.matmul(db[:, 0:16], ones[:], pS_[:, 16 * ti:16 * ti + 16], start=(ti == 0), stop=False),
                                  reads=[ones.tok(), pS_.tok()], writes=[db.tok()], inc=False)
                        kb.op(PE, lambda: T.matmul(db[:, 0:16], ones[:], pSn_[:, :], start=False, stop=True), reads=[ones.tok(), pSn_.tok()], writes=[db.tok()], inc=True)
                        kb.op(DVE, lambda: V.reciprocal(out=rS_[:, :], in_=db[:, 0:16]), reads=[db.tok()], writes=[rS_.tok()])
                        kb.op(DVE, lambda: V.tensor_tensor(out=mixA[0:64, hp, 2048 + 8 * b:2056 + 8 * b], in0=nb[0:64, 0:8], in1=rS_[0:64, 0:8], op=ALU.mult),
                              reads=[nb.tok(), rS_.tok()], writes=[mixA.tok((hp, 4, 0))])
                        kb.op(DVE, lambda: V.tensor_tensor(out=mixA[64:128, hp, 2048 + 8 * b:2056 + 8 * b], in0=nb[64:128, 8:16], in1=rS_[64:128, 8:16], op=ALU.mult),
                              reads=[nb.tok(), rS_.tok()], writes=[mixA.tok((hp, 4, 1))])
                    samp_scores(0)
                    for b in range(1, 4):
                        samp_scores(b)
                        samp_pv(b - 1)
                    samp_pv(3)
                if stage == "inproj":
                    break
                if stage == "attn":
                    dump("mixA", mixA[:].rearrange("p c t -> p (c t)"), [mixA.tok((hp, G, a)) for hp in range(4) for G in range(5) for a in range(2)])
                    break

                UZW = (4 * NT * 2) // 4
                ph = Phase()
                uz = ph.take([128, 4, NT], BF16)
                wub = ph.take([128, 8, 512], BF16)
                kb.dma(POOL, wub[:], win_v[:, :, 1536:2048], writes=[wub.tok()])
                flip = 0
                for oc in range(4):
                    for ci, (c0, cn) in enumerate(CTS):
                        bk = next_bank([0, 1, 2, 3])
                        for kc in range(8):
                            kb.op(PE, lambda kc=kc: T.matmul(bk[:, 0:cn], wub[:, kc, 128 * oc:128 * oc + 128], xnT[:, kc, c0:c0 + cn], start=(kc == 0), stop=(kc == 7)),
                                  reads=[wub.tok(), xnT.tok(ci)], writes=[bk.tok()], inc=(kc == 7))
                        if flip % 2 == 0:
                            kb.op(ACT, lambda: S.copy(out=uz[:, oc, c0:c0 + cn], in_=bk[:, 0:cn]), reads=[bk.tok()], writes=[uz.tok((oc, ci))])
                        else:
                            kb.op(DVE, lambda: V.tensor_copy(out=uz[:, oc, c0:c0 + cn], in_=bk[:, 0:cn]), reads=[bk.tok()], writes=[uz.tok((oc, ci))])
                        flip += 1
                uz_all = [uz.tok((oc, ci)) for oc in range(4) for ci in range(5)]

                for hh in range(2):
                    ph2 = Phase(base=UZW)
                    ph2.bufs.append(uz)
                    g0 = 8 * hh

                    def tk(shape, dt=F32):
                        return ph2.take(shape, dt)
                    are = tk([128, 8]); aim = tk([128, 8]); ldt = tk([128, 8])
                    bre = tk([128, 8, 16]); bim = tk([128, 8, 16]); cre = tk([128, 8, 16]); cim = tk([128, 8, 16])
                    h0 = tk([128, 2, 8, 4])
                    kb.dma(SP, are[:], d_are[:, l, g0:g0 + 8], writes=[are.tok()])
                    kb.dma(SP, aim[:], d_aim[:, l, g0:g0 + 8], writes=[aim.tok()])
                    kb.dma(SP, ldt[:], d_ldt[:, l, g0:g0 + 8], writes=[ldt.tok()])
                    kb.dma(SP, bre[:], d_bre[:, l, 16 * g0:16 * g0 + 128].rearrange("p (g c) -> p g c", c=16), writes=[bre.tok()])
                    kb.dma(SP, bim[:], d_bim[:, l, 16 * g0:16 * g0 + 128].rearrange("p (g c) -> p g c", c=16), writes=[bim.tok()])
                    kb.dma(SP, cre[:], d_cre[:, l, 16 * g0:16 * g0 + 128].rearrange("p (g c) -> p g c", c=16), writes=[cre.tok()])
                    kb.dma(SP, cim[:], d_cim[:, l, 16 * g0:16 * g0 + 128].rearrange("p (g c) -> p g c", c=16), writes=[cim.tok()])
                    kb.dma(SP, h0[:], d_h0[:, l, :, g0:g0 + 8, :], writes=[h0.tok()])
                    dtt = tk([128, 8]); lr = tk([128, 8]); rho = tk([128, 8]); th = tk([128, 8])
                    t_a = tk([128, NMV, 8]); t_b = tk([128, NMV, 8]); t_c = tk([128, NMV, 8]); t_d = tk([128, NMV, 8])
                    Er = tk([128, NMV, 8]); Ei = tk([128, NMV, 8])
                    s1 = tk([128, 8]); s2 = tk([128, 8]); s3 = tk([128, 8]); s4 = tk([128, 8]); fr = tk([128, 8]); fi = tk([128, 8])
                    bbr = tk([128, 8, 16]); bbi = tk([128, 8, 16]); tb1 = tk([128, 8, 16])
                    scr = tk([128, 2, 1152])
                    RW = tk([128, 3072])
                    Xr = tk([128, 9, 8, 16], BF16); XiN = tk([128, 9, 8, 16], BF16)
                    Kblk = tk([128, 2, 8, 128], BF16)
                    Harr = tk([128, 2, 8, 257])
                    Ss = tk([128, 2, 8, 4]); Hs = tk([128, 2, 8, 4]); tS = tk([128, 2, 8, 4])
                    tl1 = tk([128, 8, 16]); tl2 = tk([128, 8, 16]); tl3 = tk([128, 8, 16]); tl4 = tk([128, 8, 16])
                    Hb = [tk([128, 2, 8, 64], BF16) for _ in range(2)]
                    ytmp = [tk([128, 512]) for _ in range(2)]
                    rwb = RW[:].bitcast(BF16)
                    WT = [rwb[:, 2048 * e:2048 * e + 2048].rearrange("p (m r c) -> p m r c", m=8, r=2) for e in range(2)]
                    BbPad = rwb[:, 4096:6144].rearrange("p (g r c) -> p g r c", g=8, r=2)
                    PT = rwb[:, 0:4096].rearrange("p (t g i c) -> p t g i c", t=8, g=2, i=8)
                    rw = RW.tok()

                    def bc_m(x):
                        return x[:].unsqueeze(1).broadcast_to([128, NMV, 8])

                    def dv(fn, reads, writes):
                        kb.op(DVE, fn, reads=reads, writes=writes)

                    kb.op(ACT, lambda: S.activation(out=dtt[:], in_=ldt[:], func=AF.Exp), reads=[ldt.tok()], writes=[dtt.tok()])
                    dv(lambda: V.tensor_scalar(out=lr[:], in0=are[:], scalar1=-1e-4, scalar2=None, op0=ALU.min), [are.tok()], [lr.tok()])
                    dv(lambda: V.tensor_tensor(out=rho[:], in0=lr[:], in1=dtt[:], op=ALU.mult), [lr.tok(), dtt.tok()], [rho.tok()])
                    dv(lambda: V.tensor_tensor(out=th[:], in0=aim[:], in1=dtt[:], op=ALU.mult), [aim.tok(), dtt.tok()], [th.tok()])
                    dv(lambda: V.tensor_tensor(out=t_a[:], in0=bc_m(rho), in1=mvt[:, :, 0:8], op=ALU.mult), [rho.tok(), mvt.tok()], [t_a.tok()])
                    dv(lambda: V.tensor_tensor(out=t_b[:], in0=bc_m(th), in1=mvt[:, :, 0:8], op=ALU.mult), [th.tok(), mvt.tok()], [t_b.tok()])
                    kb.op(ACT, lambda: S.activation(out=t_a[:], in_=t_a[:], func=AF.Exp), reads=[t_a.tok()], writes=[t_a.tok()])
                    dv(lambda: V.tensor_scalar(out=t_c[:], in0=t_b[:], scalar1=1.0 / TWO_PI, scalar2=MAGIC, op0=ALU.mult, op1=ALU.add), [t_b.tok()], [t_c.tok()])
                    dv(lambda: V.tensor_scalar(out=t_c[:], in0=t_c[:], scalar1=-MAGIC, scalar2=None, op0=ALU.add), [t_c.tok()], [t_c.tok()])
                    dv(lambda: V.scalar_tensor_tensor(out=t_b[:], in0=t_c[:], scalar=-CW1, in1=t_b[:], op0=ALU.mult, op1=ALU.add), [t_c.tok(), t_b.tok()], [t_b.tok()])
                    dv(lambda: V.scalar_tensor_tensor(out=t_b[:], in0=t_c[:], scalar=-CW2, in1=t_b[:], op0=ALU.mult, op1=ALU.add), [t_c.tok(), t_b.tok()], [t_b.tok()])
                    dv(lambda: V.tensor_scalar(out=t_b[:], in0=t_b[:], scalar1=-np.pi, scalar2=np.pi, op0=ALU.max, op1=ALU.min), [t_b.tok()], [t_b.tok()])
                    kb.op(ACT, lambda: S.activation(out=t_c[:], in_=t_b[:], func=AF.Sin), reads=[t_b.tok()], writes=[t_c.tok()])
                    kb.op(ACT, lambda: S.activation(out=t_d[:], in_=t_b[:], func=AF.Sin, scale=0.5), reads=[t_b.tok()], writes=[t_d.tok()])
                    dv(lambda: V.tensor_tensor(out=t_d[:], in0=t_d[:], in1=t_d[:], op=ALU.mult), [t_d.tok()], [t_d.tok()])
                    dv(lambda: V.tensor_scalar(out=t_d[:], in0=t_d[:], scalar1=-2.0, scalar2=1.0, op0=ALU.mult, op1=ALU.add), [t_d.tok()], [t_d.tok()])
                    dv(lambda: V.tensor_tensor(out=Er[:], in0=t_a[:], in1=t_d[:], op=ALU.mult), [t_a.tok(), t_d.tok()], [Er.tok()])
                    dv(lambda: V.tensor_tensor(out=Ei[:], in0=t_a[:], in1=t_c[:], op=ALU.mult), [t_a.tok(), t_c.tok()], [Ei.tok()])
                    dv(lambda: V.tensor_scalar(out=s1[:], in0=Er[:, 1, :], scalar1=-1.0, scalar2=None, op0=ALU.add), [Er.tok()], [s1.tok()])
                    dv(lambda: V.tensor_tensor(out=s2[:], in0=lr[:], in1=lr[:], op=ALU.mult), [lr.tok()], [s2.tok()])
                    dv(lambda: V.tensor_tensor(out=s3[:], in0=aim[:], in1=aim[:], op=ALU.mult), [aim.tok()], [s3.tok()])
                    dv(lambda: V.tensor_tensor(out=s2[:], in0=s2[:], in1=s3[:], op=ALU.add), [s2.tok(), s3.tok()], [s2.tok()])
                    dv(lambda: V.reciprocal(out=s2[:], in_=s2[:]), [s2.tok()], [s2.tok()])
                    dv(lambda: V.tensor_tensor(out=fr[:], in0=s1[:], in1=lr[:], op=ALU.mult), [s1.tok(), lr.tok()], [fr.tok()])
                    dv(lambda: V.tensor_tensor(out=s3[:], in0=Ei[:, 1, :], in1=aim[:], op=ALU.mult), [Ei.tok(), aim.tok()], [s3.tok()])
                    dv(lambda: V.tensor_tensor(out=fr[:], in0=fr[:], in1=s3[:], op=ALU.add), [fr.tok(), s3.tok()], [fr.tok()])
                    dv(lambda: V.tensor_tensor(out=fr[:], in0=fr[:], in1=s2[:], op=ALU.mult), [fr.tok(), s2.tok()], [fr.tok()])
                    dv(lambda: V.tensor_tensor(out=fi[:], in0=Ei[:, 1, :], in1=lr[:], op=ALU.mult), [Ei.tok(), lr.tok()], [fi.tok()])
                    dv(lambda: V.tensor_tensor(out=s3[:], in0=s1[:], in1=aim[:], op=ALU.mult), [s1.tok(), aim.tok()], [s3.tok()])
                    dv(lambda: V.tensor_tensor(out=fi[:], in0=fi[:], in1=s3[:], op=ALU.subtract), [fi.tok(), s3.tok()], [fi.tok()])
                    dv(lambda: V.tensor_tensor(out=fi[:], in0=fi[:], in1=s2[:], op=ALU.mult), [fi.tok(), s2.tok()], [fi.tok()])

                    def bc_c(x):
                        return x[:].unsqueeze(2).broadcast_to([128, 8, 16])
                    dv(lambda: V.tensor_tensor(out=bbr[:], in0=bc_c(fr), in1=bre[:], op=ALU.mult), [fr.tok(), bre.tok()], [bbr.tok()])
                    dv(lambda: V.tensor_tensor(out=tb1[:], in0=bc_c(fi), in1=bim[:], op=ALU.mult), [fi.tok(), bim.tok()], [tb1.tok()])
                    dv(lambda: V.tensor_tensor(out=bbr[:], in0=bbr[:], in1=tb1[:], op=ALU.subtract), [bbr.tok(), tb1.tok()], [bbr.tok()])
                    dv(lambda: V.tensor_tensor(out=bbi[:], in0=bc_c(fr), in1=bim[:], op=ALU.mult), [fr.tok(), bim.tok()], [bbi.tok()])
                    dv(lambda: V.tensor_tensor(out=tb1[:], in0=bc_c(fi), in1=bre[:], op=ALU.mult), [fi.tok(), bre.tok()], [tb1.tok()])
                    dv(lambda: V.tensor_tensor(out=bbi[:], in0=bbi[:], in1=tb1[:], op=ALU.add), [bbi.tok(), tb1.tok()], [bbi.tok()])

                    def Em(E_, n_m):
                        return E_[:, 0:n_m, :].unsqueeze(3).broadcast_to([128, n_m, 8, 16])

                    def Bm(b_, n_m):
                        return b_[:].unsqueeze(1).broadcast_to([128, n_m, 8, 16])
                    Wr = scr[:, 0, 0:1024].rearrange("p (m g c) -> p m g c", m=8, g=8)
                    Wi = scr[:, 1, 0:1024].rearrange("p (m g c) -> p m g c", m=8, g=8)
                    tmpW = Harr[:, 0, :, 0:128].rearrange("p g (m c) -> p m g c", m=8)
                    st = scr.tok()
                    ht = Harr.tok()
                    dv(lambda: V.tensor_tensor(out=Wr, in0=Em(Er, 8), in1=Bm(bbr, 8), op=ALU.mult), [Er.tok(), bbr.tok()], [st])
                    dv(lambda: V.tensor_tensor(out=tmpW, in0=Em(Ei, 8), in1=Bm(bbi, 8), op=ALU.mult), [Ei.tok(), bbi.tok()], [ht])
                    dv(lambda: V.tensor_tensor(out=Wr, in0=Wr, in1=tmpW, op=ALU.subtract), [st, ht], [st])
                    dv(lambda: V.tensor_tensor(out=Wi, in0=Em(Er, 8), in1=Bm(bbi, 8), op=ALU.mult), [Er.tok(), bbi.tok()], [st])
                    dv(lambda: V.tensor_tensor(out=tmpW, in0=Em(Ei, 8), in1=Bm(bbr, 8), op=ALU.mult), [Ei.tok(), bbr.tok(), st], [ht])
                    dv(lambda: V.tensor_tensor(out=Wi, in0=Wi, in1=tmpW, op=ALU.add), [st, ht], [st])
                    for ri in range(2):
                        Wsrc = Wr if ri == 0 else Wi
                        for mb in range(2):
                            bk = next_bank([0, 1, 2, 3])
                            for mm in range(4):
                                m_ = 4 * mb + mm
                                kb.op(PE, lambda m_=m_, mm=mm: T.transpose(bk[:, 128 * mm:128 * mm + 128], Wsrc[:, m_, :, :].rearrange("p g c -> p (g c)"), identf[:]),
                                      reads=[st, identf.tok()], writes=[bk.tok()], inc=(mm == 3))
                            for e in range(2):
                                kb.op(DVE, lambda e=e: V.tensor_scalar(out=WT[e][:, 4 * mb:4 * mb + 4, ri, :], in0=bk[:].rearrange("p (m c) -> p m c", c=128),
                                                                       scalar1=rowmask[:, e:e + 1], scalar2=None, op0=ALU.mult),
                                      reads=[bk.tok(), rowmask.tok()], writes=[rw])
                    X1 = scr[:, 0, 0:1152].rearrange("p (m g c) -> p m g c", m=9, g=8)
                    X2 = scr[:, 1, 0:1152].rearrange("p (m g c) -> p m g c", m=9, g=8)
                    dv(lambda: V.tensor_tensor(out=X1, in0=Em(Er, 9), in1=Bm(cre, 9), op=ALU.mult), [Er.tok(), cre.tok()], [st])
                    dv(lambda: V.tensor_tensor(out=X2, in0=Em(Ei, 9), in1=Bm(cim, 9), op=ALU.mult), [Ei.tok(), cim.tok()], [st])
                    dv(lambda: V.tensor_tensor(out=Xr[:], in0=X1, in1=X2, op=ALU.subtract), [st], [Xr.tok()])
                    dv(lambda: V.tensor_tensor(out=X1, in0=Em(Ei, 9), in1=Bm(cre, 9), op=ALU.mult), [Ei.tok(), cre.tok(), Xr.tok()], [st])
                    dv(lambda: V.tensor_tensor(out=X2, in0=Em(Er, 9), in1=Bm(cim, 9), op=ALU.mult), [Er.tok(), cim.tok()], [st])
                    dv(lambda: V.tensor_tensor(out=X1, in0=X1, in1=X2, op=ALU.add), [st], [st])
                    dv(lambda: V.tensor_scalar(out=XiN[:], in0=X1, scalar1=-1.0, scalar2=None, op0=ALU.mult), [st], [XiN.tok()])
                    bpt = Tok(ph2.fence)
                    kb.op(POOL, lambda: nc.gpsimd.memset(BbPad, 0.0), reads=[], writes=[bpt])
                    for i in range(8):
                        kb.op(ACT, lambda i=i: S.copy(out=BbPad[:, i, 0, 16 * i:16 * i + 16], in_=bbr[:, i, :]), reads=[bbr.tok()], writes=[bpt])
                        kb.op(ACT, lambda i=i: S.copy(out=BbPad[:, i, 1, 16 * i:16 * i + 16], in_=bbi[:, i, :]), reads=[bbi.tok()], writes=[bpt])
                    for g2 in range(2):
                        for lh in range(2):
                            bk = next_bank([0, 1, 2, 3])
                            bv = bk[:].rearrange("p (m c) -> p m c", c=128)
                            for i in range(8):
                                for lq in range(4):
                                    kb.op(PE, lambda i=i, lq=lq: T.matmul(bv[:, lq, 16 * i:16 * i + 16], BbPad[64 * g2:64 * g2 + 64, i, 0, :], Xr[64 * g2:64 * g2 + 64, 4 * lh + lq, i, :],
                                                                          start=True, stop=False, tile_position=(64 * g2, 0)),
                                          reads=[bpt, Xr.tok()], writes=[bk.tok()], inc=False)
                                    kb.op(PE, lambda i=i, lq=lq: T.matmul(bv[:, lq, 16 * i:16 * i + 16], BbPad[64 * g2:64 * g2 + 64, i, 1, :], XiN[64 * g2:64 * g2 + 64, 4 * lh + lq, i, :],
                                                                          start=False, stop=True, tile_position=(64 * g2, 0)),
                                          reads=[bpt, XiN.tok()], writes=[bk.tok()], inc=(i == 7 and lq == 3))
                            kb.op(ACT, lambda: S.copy(out=Kblk[:, g2, 4 * lh:4 * lh + 4, :], in_=bv), reads=[bk.tok()], writes=[Kblk.tok()])
                    grp = 0
                    for ri in range(2):
                        for e_ in range(2):
                            bset = [banks[4 * (grp % 2) + j_] for j_ in range(4)]
                            grp += 1
                            for g2 in range(2):
                                oc = 2 * g2 + hh
                                for sg in range(8):
                                    for j_ in range(4):
                                        bk = bset[j_]
                                        uv = uz[32 * j_:32 * j_ + 32, oc, :].rearrange("p (k s) -> p k s", s=8)
                                        kb.op(PE, lambda sg=sg, g2=g2, uv=uv, bk=bk, j_=j_: T.matmul(bk[64 * g2:64 * g2 + 64, 0:NCH], WT[e_][32 * j_:32 * j_ + 32, 7 - sg, ri, 64 * g2:64 * g2 + 64],
                                                                                                      uv[:, :, sg], start=(sg == 0), stop=(sg == 7), tile_position=(32 * j_, 64 * g2),
                                                                                                      skip_group_check=True),
                                              reads=[rw] + uz_all, writes=[bk.tok()], inc=(sg == 7 and g2 == 1))
                            for j_ in range(4):
                                i = 2 * j_ + e_
                                bk = bset[j_]
                                kb.op(ACT, lambda: S.copy(out=Harr[:, ri, i, 1:257], in_=bk[:, 0:256]), reads=[bk.tok(), ht], writes=[Harr.tok((ri, i))])
                                kb.op(DVE, lambda: V.tensor_copy(out=Ss[:, ri, i, :], in_=bk[:, 256:260]), reads=[bk.tok()], writes=[Ss.tok()])
                    hall = [Harr.tok((ri, i)) for ri in range(2) for i in range(8)]
                    hsc = Harr.tok("scan")
                    dv(lambda: V.memset(Harr[:, :, :, 0:1], 0.0), hall + [ht], [hsc])
                    kb.op(POOL, lambda: nc.gpsimd.memset(rwb[:, 0:4096], 0.0), reads=[], writes=[rw])
                    for par in range(2):
                        for ri in range(2):
                            Xs = Xr if ri == 0 else XiN
                            for g2 in range(2):
                                kb.op(ACT, lambda g2=g2: S.copy(out=PT[64 * ri:64 * ri + 64, :, g2, par:8:2, 16 * par:16 * par + 16], in_=Xs[64 * g2:64 * g2 + 64, 1:9, par:8:2, :]),
                                      reads=[Xs.tok()], writes=[rw])
                    A8r = Er[:, MIDX[8], :].unsqueeze(2).broadcast_to([128, 8, 16])
                    A8i = Ei[:, MIDX[8], :].unsqueeze(2).broadcast_to([128, 8, 16])
                    et = [Er.tok(), Ei.tok()]

                    def Hv(ri, j):
                        return Harr[:, ri, :, 1 + j:257:16]
                    for j in range(1, 16):
                        dv(lambda: V.tensor_tensor(out=tl1[:], in0=A8r, in1=Hv(0, j - 1), op=ALU.mult), et + [hsc], [tl1.tok()])
                        dv(lambda: V.tensor_tensor(out=tl2[:], in0=A8i, in1=Hv(1, j - 1), op=ALU.mult), et + [hsc], [tl2.tok()])
                        dv(lambda: V.tensor_tensor(out=tl3[:], in0=A8r, in1=Hv(1, j - 1), op=ALU.mult), et + [hsc], [tl3.tok()])
                        dv(lambda: V.tensor_tensor(out=tl4[:], in0=A8i, in1=Hv(0, j - 1), op=ALU.mult), et + [hsc], [tl4.tok()])
                        dv(lambda: V.tensor_tensor(out=tl1[:], in0=tl1[:], in1=tl2[:], op=ALU.subtract), [tl1.tok(), tl2.tok()], [tl1.tok()])
                        dv(lambda: V.tensor_tensor(out=tl3[:], in0=tl3[:], in1=tl4[:], op=ALU.add), [tl3.tok(), tl4.tok()], [tl3.tok()])
                        dv(lambda: V.tensor_tensor(out=Hv(0, j), in0=Hv(0, j), in1=tl1[:], op=ALU.add), [tl1.tok(), hsc], [hsc])
                        dv(lambda: V.tensor_tensor(out=Hv(1, j), in0=Hv(1, j), in1=tl3[:], op=ALU.add), [tl3.tok(), hsc], [hsc])
                    A128r = Er[:, MIDX[128], :]
                    A128i = Ei[:, MIDX[128], :]

                    def Ce(ri, b):
                        return Harr[:, ri, :, 16 * b + 16]
                    for sft in (1, 2, 4, 8):
                        nn = 16 - sft
                        Ar_ = Er[:, MIDX[128 * sft], :].unsqueeze(2).broadcast_to([128, 8, nn])
                        Ai_ = Ei[:, MIDX[128 * sft], :].unsqueeze(2).broadcast_to([128, 8, nn])

                        def Plo(ri):
                            return Harr[:, ri, :, 16:16 + 16 * nn:16]

                        def Phi(ri):
                            return Harr[:, ri, :, 16 + 16 * sft:257:16]
                        dv(lambda: V.tensor_tensor(out=tl1[:, :, 0:nn], in0=Ar_, in1=Plo(0), op=ALU.mult), et + [hsc], [tl1.tok()])
                        dv(lambda: V.tensor_tensor(out=tl2[:, :, 0:nn], in0=Ai_, in1=Plo(1), op=ALU.mult), et + [hsc], [tl2.tok()])
                        dv(lambda: V.tensor_tensor(out=tl3[:, :, 0:nn], in0=Ar_, in1=Plo(1), op=ALU.mult), et + [hsc], [tl3.tok()])
                        dv(lambda: V.tensor_tensor(out=tl4[:, :, 0:nn], in0=Ai_, in1=Plo(0), op=ALU.mult), et + [hsc], [tl4.tok()])
                        dv(lambda: V.tensor_tensor(out=tl1[:, :, 0:nn], in0=tl1[:, :, 0:nn], in1=tl2[:, :, 0:nn], op=ALU.subtract), [tl1.tok(), tl2.tok()], [tl1.tok()])
                        dv(lambda: V.tensor_tensor(out=tl3[:, :, 0:nn], in0=tl3[:, :, 0:nn], in1=tl4[:, :, 0:nn], op=ALU.add), [tl3.tok(), tl4.tok()], [tl3.tok()])
                        dv(lambda: V.tensor_tensor(out=Phi(0), in0=Phi(0), in1=tl1[:, :, 0:nn], op=ALU.add), [tl1.tok(), hsc], [hsc])
                        dv(lambda: V.tensor_tensor(out=Phi(1), in0=Phi(1), in1=tl3[:, :, 0:nn], op=ALU.add), [tl3.tok(), hsc], [hsc])
                    F1 = scr[:, 0, 0:1800].rearrange("p (g b j) -> p g b j", g=8, b=15) if False else None
                    fx = scr[:].rearrange("p a w -> p (a w)")
                    F1 = fx[:, 0:1800].rearrange("p (g b j) -> p g b j", g=8, b=15)

                    def Ep(E_):
                        return E_[:, 8:23, :].rearrange("p j g -> p g j").unsqueeze(2).broadcast_to([128, 8, 15, 15])

                    def Cb(ri):
                        return Harr[:, ri, :, 16:241:16].unsqueeze(3).broadcast_to([128, 8, 15, 15])

                    def Hf(ri):
                        return Harr[:, ri, :, 17:257].rearrange("p g (b j) -> p g b j", j=16)[:, :, :, 0:15]
                    dv(lambda: V.tensor_tensor(out=F1, in0=Ep(Er), in1=Cb(0), op=ALU.mult), et + [hsc], [st])
                    dv(lambda: V.tensor_tensor(out=Hf(0), in0=Hf(0), in1=F1, op=ALU.add), [st, hsc], [hsc])
                    dv(lambda: V.tensor_tensor(out=F1, in0=Ep(Ei), in1=Cb(1), op=ALU.mult), et + [hsc], [st])
                    dv(lambda: V.tensor_tensor(out=Hf(0), in0=Hf(0), in1=F1, op=ALU.subtract), [st, hsc], [hsc])
                    dv(lambda: V.tensor_tensor(out=F1, in0=Ep(Er), in1=Cb(1), op=ALU.mult), et + [hsc], [st])
                    dv(lambda: V.tensor_tensor(out=Hf(1), in0=Hf(1), in1=F1, op=ALU.add), [st, hsc], [hsc])
                    dv(lambda: V.tensor_tensor(out=F1, in0=Ep(Ei), in1=Cb(0), op=ALU.mult), et + [hsc], [st])
                    dv(lambda: V.tensor_tensor(out=Hf(1), in0=Hf(1), in1=F1, op=ALU.add), [st, hsc], [hsc])
                    A8r4 = Er[:, MIDX[8], :].unsqueeze(2).broadcast_to([128, 8, 4])
                    A8i4 = Ei[:, MIDX[8], :].unsqueeze(2).broadcast_to([128, 8, 4])
                    dv(lambda: V.tensor_tensor(out=Hs[:, 0], in0=A8r4, in1=h0[:, 0], op=ALU.mult), et + [h0.tok()], [Hs.tok()])
                    dv(lambda: V.tensor_tensor(out=tS[:, 0], in0=A8i4, in1=h0[:, 1], op=ALU.mult), et + [h0.tok()], [tS.tok()])
                    dv(lambda: V.tensor_tensor(out=Hs[:, 0], in0=Hs[:, 0], in1=tS[:, 0], op=ALU.subtract), [Hs.tok(), tS.tok()], [Hs.tok()])
                    dv(lambda: V.tensor_tensor(out=Hs[:, 1], in0=A8r4, in1=h0[:, 1], op=ALU.mult), et + [h0.tok()], [Hs.tok()])
                    dv(lambda: V.tensor_tensor(out=tS[:, 1], in0=A8i4, in1=h0[:, 0], op=ALU.mult), et + [h0.tok()], [tS.tok()])
                    dv(lambda: V.tensor_tensor(out=Hs[:, 1], in0=Hs[:, 1], in1=tS[:, 1], op=ALU.add), [Hs.tok(), tS.tok()], [Hs.tok()])
                    dv(lambda: V.tensor_tensor(out=Hs[:], in0=Hs[:], in1=Ss[:], op=ALU.add), [Hs.tok(), Ss.tok()], [Hs.tok()])
                    for ri in range(2):
                        kb.dma(SP, o_h[:, l, ri, g0:g0 + 8, 0:4], Hs[:, ri], reads=[Hs.tok()])
                        kb.dma(SP, o_h[:, l, ri, g0:g0 + 8, 4:5], Harr[:, ri, :, 256:257], reads=[hsc], allow_slow_non_contiguous=True)
                    if stage == "s5scan":
                        dump("Harr%d" % hh, Harr[:].rearrange("p r g k -> p (r g k)"), [hsc])
                        dump("Kblk%d" % hh, Kblk[:].rearrange("p a m c -> p (a m c)"), [Kblk.tok()])
                    for ci, (c0, cn) in enumerate(CTS):
                        k0, kn = c0 // 8, cn // 8
                        hb = Hb[ci % 2]
                        for ri in range(2):
                            for g2 in range(2):
                                if ci < 4:
                                    kb.op(ACT, lambda ri=ri, g2=g2: S.copy(out=hb[64 * ri:64 * ri + 64, g2, :, 0:kn], in_=Harr[64 * g2:64 * g2 + 64, ri, :, k0:k0 + kn]),
                                          reads=[hsc], writes=[hb.tok()])
                                else:
                                    kb.op(ACT, lambda ri=ri, g2=g2: S.copy(out=hb[64 * ri:64 * ri + 64, g2, :, 0:4], in_=h0[64 * g2:64 * g2 + 64, ri, :, :]),
                                          reads=[h0.tok()], writes=[hb.tok()])
                        for g2 in range(2):
                            oc = 2 * g2 + hh
                            bk = next_bank([0, 1, 2, 3])
                            kb.op(PE, lambda: T.matmul(bk[:, 0:cn], Kblk[:, g2, 0, :], uz[:, oc, c0:c0 + cn], start=True, stop=False),
                                  reads=[Kblk.tok(), uz.tok((oc, ci))], writes=[bk.tok()], inc=False)
                            uvv = uz[:, oc, c0:c0 + cn].rearrange("p (k s) -> p k s", s=8)
                            bvv = bk[:, 0:cn].rearrange("p (k s) -> p k s", s=8)
                            for ta in range(8):
                                for i in (0, 2, 4, 6, 1, 3, 5, 7):
                                    kb.op(PE, lambda i=i, ta=ta: T.matmul(bvv[32 * (i // 2):32 * (i // 2) + 32, :, ta], PT[:, ta, g2, i, :],
                                                                          hb[:, g2, i, 0:kn], start=False, stop=False,
                                                                          tile_position=(0, 32 * (i // 2))),
                                          reads=[rw, hb.tok()], writes=[bk.tok()], inc=False)
                            for lg in range(1, 8):
                                for ta in range(lg, 8):
                                    kb.op(PE, lambda lg=lg, ta=ta: T.matmul(bvv[:, :, ta], Kblk[:, g2, lg, :], uvv[:, :, ta - lg], start=False, stop=(lg == 7 and ta == 7)),
                                          reads=[Kblk.tok(), uz.tok((oc, ci))], writes=[bk.tok()], inc=(lg == 7 and ta == 7))
                            yt = ytmp[(2 * ci + g2) % 2]
                            dv(lambda: V.scalar_tensor_tensor(out=yt[:, 0:cn], in0=uz[:, oc, c0:c0 + cn], scalar=pv_col(l, "sd", oc), in1=bk[:, 0:cn], op0=ALU.mult, op1=ALU.add),
                               [uz.tok((oc, ci)), bk.tok(), pvec.tok()], [yt.tok()])
                            kb.op(ACT, lambda: S.activation(out=uz[:, oc, c0:c0 + cn], in_=yt[:, 0:cn], func=AF.Gelu_apprx_tanh), reads=[yt.tok()], writes=[uz.tok((oc, ci))])
                if stage == "s5scan":
                    break
                if stage == "s5":
                    dump("zT", uz[:].rearrange("p c t -> p (c t)"), uz_all)
                    break

                ph = Phase(base=UZW)
                ph.bufs.append(uz)
                gw = ph.take([128, 4, 512], BF16)
                sgt = [ph.take([128, 512]) for _ in range(2)]
                sq = [ph.take([128, 512], BF16), ph.take([128, 512], BF16)]
                rt = ph.take([128, 512])
                wo = ph.take([128, 8, 1024], BF16)
                kb.dma(POOL, gw[:], d_glu[l].rearrange("(kc p) n -> p kc n", p=128), writes=[gw.tok()])
                wout_v = d_wout[l].rearrange("(kc p) n -> p kc n", p=128)
                for hhalf in range(2):
                    kb.dma(POOL, wo[:, :, 512 * hhalf:512 * hhalf + 512], wout_v[:, :, 512 * hhalf:512 * hhalf + 512], writes=[wo.tok(hhalf)])
                wgu0 = ph.take([128, 8, 2, 512], BF16)
                wdn0 = ph.take([128, 4, 1024], BF16)
                actb = [ph.take([128, 4, 512], BF16) for _ in range(2)]
                sil = [ph.take([128, 512]) for _ in range(2)]
                ph_mix = ph
                wg_v = d_wg[l].rearrange("(kc p) n -> p kc n", p=128)
                wu_v = d_wu[l].rearrange("(kc p) n -> p kc n", p=128)
                fgs = [(0, 4), (4, 4), (8, 4), (12, 4), (16, 4), (20, 2)]

                def load_ffn_group(gi, wA, wD):
                    f0, fn_ = fgs[gi]
                    kb.dma(POOL, wA[:, :, 0, 0:128 * fn_], wg_v[:, :, 128 * f0:128 * (f0 + fn_)], writes=[wA.tok(0)])
                    kb.dma(POOL, wA[:, :, 1, 0:128 * fn_], wu_v[:, :, 128 * f0:128 * (f0 + fn_)], writes=[wA.tok(1)])
                    kb.dma(POOL, wD[:, 0:fn_, :], d_wd[l, 128 * f0:128 * (f0 + fn_), :].rearrange("(ft p) n -> p ft n", p=128), writes=[wD.tok()])
                load_ffn_group(0, wgu0, wdn0)
                mixS = Buf(xnT.t[:, 0:4, :])
                def stageA(ci):
                    c0, cn = CTS[ci]
                    for oc in range(4):
                        bk = next_bank([0, 1, 2, 3])
                        for kc in range(4):
                            kb.op(PE, lambda kc=kc: T.matmul(bk[:, 0:cn], gw[:, kc, 128 * oc:128 * oc + 128], uz[:, kc, c0:c0 + cn], start=(kc == 0), stop=(kc == 3)),
                                  reads=[gw.tok(), uz.tok((kc, ci))], writes=[bk.tok()], inc=(kc == 3))
                        sg_ = sgt[oc % 2]
                        kb.op(ACT, lambda: S.activation(out=sg_[:, 0:cn], in_=bk[:, 0:cn], func=AF.Sigmoid, bias=pv_col(l, "gb", oc)), reads=[bk.tok(), pvec.tok()], writes=[sg_.tok()])
                        kb.op(DVE, lambda: V.tensor_tensor(out=mixS[:, oc, c0:c0 + cn], in0=uz[:, oc, c0:c0 + cn], in1=sg_[:, 0:cn], op=ALU.mult),
                              reads=[uz.tok((oc, ci)), sg_.tok()], writes=[xnT.tok(ci)])
                    rmsnorm_to_bf16(mixS, xnT.tok(ci), 4, mixS, xnT.tok(ci), lambda c: pv_col(l, "sg", c), 512.0, ci, sq, rt, [4, 5])
                    ma_toks = [mixA.tok((hp_, ci, a_)) for hp_ in range(4) for a_ in range(2)]
                    mat = mixA.tok(("n", ci))
                    kb.op(DVE, lambda: V.tensor_copy(out=rt[:, 0:1], in_=rt[:, 0:1]), reads=ma_toks + [rt.tok()], writes=[mat, rt.tok()])
                    rmsnorm_to_bf16(mixA, mat, 4, mixA, mat, lambda c: pv_col(l, "ag", c), 512.0, ci, sq, rt, [4, 5])

                def stageB(ci):
                    c0, cn = CTS[ci]
                    mat = mixA.tok(("n", ci))
                    for dc in range(8):
                        bk = next_bank([0, 1, 2, 3, 6, 7])
                        for kc in range(8):
                            src = mixA[:, kc, c0:c0 + cn] if kc < 4 else mixS[:, kc - 4, c0:c0 + cn]
                            stok = mat if kc < 4 else xnT.tok(ci)
                            kb.op(PE, lambda kc=kc, src=src: T.matmul(bk[:, 0:cn], wo[:, kc, 128 * dc:128 * dc + 128], src, start=(kc == 0), stop=(kc == 7)),
                                  reads=[wo.tok(dc // 4), stok], writes=[bk.tok()], inc=(kc == 7))
                        kb.op(DVE, lambda: V.tensor_tensor(out=xT[:, dc, c0:c0 + cn], in0=bk[:, 0:cn], in1=xT[:, dc, c0:c0 + cn], op=ALU.add),
                              reads=[bk.tok(), xT.tok(ci)], writes=[xT.tok(ci)])
                def norm2(ci):
                    rmsnorm_to_bf16(xT, xT.tok(ci), 8, xnT, xnT.tok(ci), lambda c: pv_col(l, "n2", c), 1024.0, ci, sq, rt, [4, 5])
                stageA(0)
                for ci in range(1, 5):
                    stageA(ci)
                    stageB(ci - 1)
                    norm2(ci - 1)
                stageB(4)
                norm2(4)
                if stage == "mix":
                    dump("hT", xT[:].rearrange("p c t -> p (c t)"), [xT.tok(c) for c in range(5)])
                    break

                ph = Phase(base=0)
                wgu1 = ph.take([128, 8, 2, 512], BF16)
                wdn1 = ph.take([128, 4, 1024], BF16)
                assert ph.off * 4 <= UZW * 4 + 4 * 512 * 2 + 2 * 512 * 4 - 0, ph.off
                ph.bufs.extend(ph_mix.bufs)
                wgu = [wgu0, wgu1]
                wdn = [wdn0, wdn1]
                na = 0
                for gi, (f0, fn_) in enumerate(fgs):
                    wA = wgu[gi % 2]
                    wD = wdn[gi % 2]
                    if gi > 0:
                        load_ffn_group(gi, wA, wD)
                    for ci, (c0, cn) in enumerate(CTS):
                        ab = actb[na % 2]
                        na += 1
                        for ft in range(fn_):
                            bg = next_bank([0, 1, 2, 3])
                            bu = next_bank([0, 1, 2, 3])
                            for kc in range(8):
                                kb.op(PE, lambda kc=kc: T.matmul(bg[:, 0:cn], wA[:, kc, 0, 128 * ft:128 * ft + 128], xnT[:, kc, c0:c0 + cn], start=(kc == 0), stop=(kc == 7)),
                                      reads=[wA.tok(0), xnT.tok(ci)], writes=[bg.tok()], inc=(kc == 7))
                            for kc in range(8):
                                kb.op(PE, lambda kc=kc: T.matmul(bu[:, 0:cn], wA[:, kc, 1, 128 * ft:128 * ft + 128], xnT[:, kc, c0:c0 + cn], start=(kc == 0), stop=(kc == 7)),
                                      reads=[wA.tok(1), xnT.tok(ci)], writes=[bu.tok()], inc=(kc == 7))
                            sl = sil[ft % 2]
                            kb.op(ACT, lambda: S.activation(out=sl[:, 0:cn], in_=bg[:, 0:cn], func=AF.Silu), reads=[bg.tok()], writes=[sl.tok()])
                            kb.op(DVE, lambda: V.tensor_tensor(out=ab[:, ft, 0:cn], in0=bu[:, 0:cn], in1=sl[:, 0:cn], op=ALU.mult),
                                  reads=[bu.tok(), sl.tok()], writes=[ab.tok(ft)])
                        for dc in range(8):
                            bk = next_bank([4, 5, 6, 7])
                            for ft in range(fn_):
                                kb.op(PE, lambda ft=ft: T.matmul(bk[:, 0:cn], wD[:, ft, 128 * dc:128 * dc + 128], ab[:, ft, 0:cn], start=(ft == 0), stop=(ft == fn_ - 1)),
                                      reads=[wD.tok(), ab.tok(ft)], writes=[bk.tok()], inc=(ft == fn_ - 1))
                            kb.op(DVE, lambda: V.tensor_tensor(out=xT[:, dc, c0:c0 + cn], in0=bk[:, 0:cn], in1=xT[:, dc, c0:c0 + cn], op=ALU.add),
                                  reads=[bk.tok(), xT.tok(ci)], writes=[xT.tok(ci)])
                        if gi == len(fgs) - 1 and l + 1 < nlayers and stage == "full":
                            rmsnorm_to_bf16(xT, xT.tok(ci), 8, xnT, xnT.tok(ci), lambda c: pv_col(l + 1, "n1", c), 1024.0, ci, sq, rt, [4, 5, 6, 7])
                if stage == "layer":
                    dump("yT1", xT[:].rearrange("p c t -> p (c t)"), [xT.tok(c) for c in range(5)])
                    break

            if stage == "full":
                ph = Phase()
                sq = [ph.take([128, 512], BF16), ph.take([128, 512], BF16)]
                rt = ph.take([128, 512])
                yst = [ph.take([128, 8, 512]) for _ in range(2)]
                for ci, (c0, cn) in enumerate(CTS):
                    ys = yst[ci % 2]
                    ysv = Buf(ys.t[:, :, 0:cn])

                    class _Shift:
                        def __getitem__(self, idx):
                            p, c, cols = idx
                            return ys.t[p, c, cols.start - c0:cols.stop - c0]
                    rmsnorm_to_bf16(xT, xT.tok(ci), 8, _Shift(), ys.tok(), lambda c: pvec[:, 128 + c:129 + c], 1024.0, ci, sq, rt, [4, 5])
                    kb.dma(SP, o_y[:, :, c0:c0 + cn], ys[:, :, 0:cn], reads=[ys.tok()])

        except _Stop:
            pass
        kb.finish()
    return nc


def _consts():
    c = {}
    c["c_ident"] = np.eye(128, dtype=np.float32)
    k = np.arange(128)[:, None]
    q = np.arange(128)[None, :]
    cur = np.where(q >= k, 0.0, MASKNEG).astype(np.float32)
    prev = np.where(k >= q, 0.0, MASKNEG).astype(np.float32)
    c["c_maskcur"] = np.tile(cur, (1, 4))
    c["c_maskprev"] = np.tile(prev, (1, 4))
    c["c_smask"] = _sample_mult().reshape(128, 176)
    mv = np.zeros((128, NMV, 16), np.float32)
    mv[:, :, :] = np.array(MVALS, np.float32)[None, :, None]
    c["c_mv"] = mv.reshape(128, NMV * 16)
    rm = np.zeros((128, 2), np.float32)
    par = (np.arange(128) // 16) % 2
    rm[:, 0] = (par == 0)
    rm[:, 1] = (par == 1)
    c["c_rowmask"] = rm
    return c


def _vecT(v):
    L_, F_ = v.shape
    return v.reshape(L_, F_ // 128, 128).transpose(2, 0, 1)


def prep_core(inp, core, consts):
    m = dict(consts)
    xp = inp["x_prompt"][core]
    xs = inp["x_sample"][4 * core:4 * core + 4].reshape(32, 1024)
    x = np.concatenate([xp, xs], axis=0)
    m["xT"] = np.ascontiguousarray(x.reshape(NT, 8, 128).transpose(2, 1, 0))
    ck = inp["cache_attn_k"][:, 4 * core:4 * core + 4]
    cv = inp["cache_attn_v"][:, 4 * core:4 * core + 4]
    ckg = ck[:, :, SPOS].reshape(NL, 4, 1280, 4, 128)
    m["ckT"] = np.ascontiguousarray(ckg.transpose(0, 1, 3, 4, 2).reshape(NL, 16, 128, 1280))
    cvg = cv[:, :, SPOS].reshape(NL, 4, 10, 128, 4, 128)
    m["cvN"] = np.ascontiguousarray(cvg.transpose(0, 1, 4, 3, 2, 5).reshape(NL, 16, 128, 1280))
    pv = np.zeros((128, 136), np.float32)
    pvl = pv[:, :128].reshape(128, NL, 32)
    pvl[:, :, 0:8] = _vecT(inp["norm1_g"])
    pvl[:, :, 8:16] = _vecT(inp["norm2_g"])
    pvl[:, :, 16:20] = _vecT(inp["attn_out_g"])
    pvl[:, :, 20:24] = _vecT(inp["ssm_out_g"])
    pvl[:, :, 24:28] = _vecT(inp["ssm_glu_b"])
    pvl[:, :, 28:32] = _vecT(inp["ssm_d"].reshape(NL, 512))
    pv[:, 128:136] = inp["final_norm_g"].reshape(8, 128).T
    m["pvec"] = pv

    def gn(a):
        return np.ascontiguousarray(a.reshape(NL, 2, 16, 64).transpose(1, 3, 0, 2).reshape(128, NL, 16))
    m["aL_re"] = gn(inp["ssm_a_re"])
    m["aL_im"] = gn(inp["ssm_a_im"])
    m["aL_ldt"] = gn(np.broadcast_to(inp["ssm_log_dt"][:, :, None], (NL, 32, 64)))
    m["bL_re"] = np.ascontiguousarray(inp["ssm_b_re"].reshape(NL, 2, 16, 64, 16).transpose(1, 3, 0, 2, 4).reshape(128, NL, 256))
    m["bL_im"] = np.ascontiguousarray(inp["ssm_b_im"].reshape(NL, 2, 16, 64, 16).transpose(1, 3, 0, 2, 4).reshape(128, NL, 256))
    m["cL_re"] = np.ascontiguousarray(inp["ssm_c_re"].reshape(NL, 2, 16, 16, 64).transpose(1, 4, 0, 2, 3).reshape(128, NL, 256))
    m["cL_im"] = np.ascontiguousarray(inp["ssm_c_im"].reshape(NL, 2, 16, 16, 64).transpose(1, 4, 0, 2, 3).reshape(128, NL, 256))
    h0 = np.stack([inp["state_ssm_re"][:, 4 * core:4 * core + 4], inp["state_ssm_im"][:, 4 * core:4 * core + 4]], axis=0)
    m["h0L"] = np.ascontiguousarray(h0.reshape(2, NL, 4, 2, 16, 64).transpose(3, 5, 1, 0, 4, 2).reshape(128, NL, 2, 16, 4))
    m["w_in"] = inp["w_in"]
    m["w_out"] = inp["w_out"]
    m["glu_w"] = inp["ssm_glu_w"]
    m["wg"] = inp["ffn_w_gate"]
    m["wu"] = inp["ffn_w_up"]
    m["wd"] = inp["ffn_w_down"]
    return m


_CACHE = {}


def kernel(**inputs):
    inp = {k: np.asarray(v) for k, v in inputs.items()}
    if "nc" not in _CACHE:
        _CACHE["nc"] = build({})
    nc = _CACHE["nc"]
    consts = _consts()
    in_maps = [prep_core(inp, c, consts) for c in range(8)]
    res = run_bass_kernel_spmd(nc, in_maps, core_ids=list(range(8)))
    R = res.results
    y_p = np.zeros((8, NP, 1024), np.float32)
    y_s = np.zeros((32, 8, 1024), np.float32)
    pk = np.zeros((NL, 8, NP, 8, 64), np.float32)
    pvv = np.zeros((NL, 8, NP, 8, 64), np.float32)
    pr = np.zeros((NL, 8, 32, 64), np.float32)
    pi = np.zeros((NL, 8, 32, 64), np.float32)
    sk = np.zeros((NL, 32, 8, 8, 64), np.float32)
    sv = np.zeros((NL, 32, 8, 8, 64), np.float32)
    sr = np.zeros((NL, 32, 32, 64), np.float32)
    si = np.zeros((NL, 32, 32, 64), np.float32)
    for c in range(8):
        r = R[c]
        y = r["yT"].transpose(2, 1, 0).reshape(NT, 1024)
        y_p[c] = y[:NP]
        y_s[4 * c:4 * c + 4] = y[NP:].reshape(4, 8, 1024)
        kT = r["pkT"]
        kk = kT.transpose(0, 3, 2, 1).reshape(NL, NT, 512)
        pk[:, c] = kk[:, :NP].reshape(NL, NP, 8, 64)
        sk[:, 4 * c:4 * c + 4] = kk[:, NP:].reshape(NL, 4, 8, 8, 64)
        pvv[:, c] = r["pv"].reshape(NL, NP, 8, 64)
        sv[:, 4 * c:4 * c + 4] = r["sv"].reshape(NL, 4, 8, 8, 64)
        h = r["hout"].reshape(2, 64, NL, 2, 16, 5)
        hh = h.transpose(2, 3, 5, 0, 4, 1).reshape(NL, 2, 5, 32, 64)
        sr[:, 4 * c:4 * c + 4] = hh[:, 0, 0:4]
        si[:, 4 * c:4 * c + 4] = hh[:, 1, 0:4]
        pr[:, c] = hh[:, 0, 4]
        pi[:, c] = hh[:, 1, 4]
    return (y_p, y_s, pk, pvv, pr, pi, sk, sv, sr, si)
```

```python
import numpy as np
import concourse.bass as bass
import concourse.mybir as mybir
from concourse.bass_utils import run_bass_kernel_spmd
from contextlib import ExitStack

F32 = mybir.dt.float32
BF16 = mybir.dt.bfloat16
AF = mybir.ActivationFunctionType
ALU = mybir.AluOpType

NL = 4
NP, NSM, NT = 2048, 32, 2080
NCH = NT // 8
CTS = [(0, 512), (512, 512), (1024, 512), (1536, 512), (2048, 32)]
DFF = 2816
NFT = DFF // 128
EPS = 1e-6
MVALS = list(range(9)) + list(range(16, 129, 8)) + [256, 512, 1024]
NMV = len(MVALS)
MIDX = {m: i for i, m in enumerate(MVALS)}
MASKNEG = -30000.0
TWO_PI = 2.0 * np.pi
CW1 = 6.28125
CW2 = TWO_PI - CW1
MAGIC = 12582912.0
ARENA = 84 * 1024


def _sample_positions():
    pos = []
    for f in range(6):
        for p in range(128):
            blk, rr = p // 8, p % 8
            pos.append(16 * (16 * f + blk) + rr)
    for nt in range(4):
        for p in range(128):
            pos.append(1536 + 128 * nt + p)
    return np.array(pos, dtype=np.int64)


SPOS = _sample_positions()


def _sample_mult():
    m = np.zeros((128, 11, 2, 8), np.float32)
    branches = ((128, 1), (512, 4), (2048, 16))
    for ti in range(11):
        for p in range(128):
            if ti < 10:
                pos = SPOS[128 * ti + p]
            else:
                if p >= 8:
                    continue
                pos = 2048 + p
            for t in range(8):
                dlt = 2048 + t - pos
                c = 0
                for (w, d) in branches:
                    if dlt >= 0 and dlt % d == 0 and dlt <= w:
                        c += 1
                m[p, ti, :, t] = c
    return m


class Tok:
    __slots__ = ("w", "r", "serial")

    def __init__(self, fence=None):
        self.w = None
        self.r = dict(fence) if fence else {}
        self.serial = False


class Sem:
    __slots__ = ("h", "idx", "val")

    def __init__(self, h, idx):
        self.h = h
        self.idx = idx
        self.val = 0


class Eng:
    def __init__(self, name, h, sem, kind):
        self.name = name
        self.h = h
        self.sem = sem
        self.kind = kind
        self.n = 0
        self.seen = {}


class KB:
    def __init__(self, nc, es):
        self.nc = nc
        self.es = es
        self.nsem = 0
        self.pe = Eng("pe", nc.tensor, self.new_sem("s_pe"), "pe")
        self.act = Eng("act", nc.scalar, self.new_sem("s_act"), "act")
        self.dve = Eng("dve", nc.vector, self.new_sem("s_dve"), "dve")
        self.pool = Eng("pool", nc.gpsimd, self.new_sem("s_pool"), "pool")
        self.sp = Eng("sp", nc.sync, self.new_sem("s_sp"), "sp")
        self.dslots = {}
        self.di = {}
        for q in (self.sp, self.pool):
            self.dslots[q.name] = [self.new_sem("d_%s%d" % (q.name, i)) for i in range(24)]
            self.di[q.name] = 0
        self.ninstr = 0

    def new_sem(self, name):
        h = self.es.enter_context(self.nc.semaphore(name))
        s = Sem(h, self.nsem)
        self.nsem += 1
        return s

    def _wait(self, eng, ev):
        sem, val, src = ev
        if eng.seen.get(sem.idx, 0) >= val:
            return
        eng.h.wait_ge(sem.h, val)
        eng.seen[sem.idx] = val

    def _deps(self, eng, reads, writes, is_dma):
        for t in reads:
            if t.w is not None:
                src = t.w[2]
                if (not is_dma) and src is eng and eng.kind == "pe":
                    pass
                else:
                    self._wait(eng, t.w)
            if getattr(t, "serial", False):
                for e in t.r.values():
                    if e[2] is not eng:
                        self._wait(eng, e)
        for t in writes:
            if t.w is not None:
                if is_dma or t.w[2] is not eng or eng.kind != "pe":
                    self._wait(eng, t.w)
            for e in t.r.values():
                if is_dma or e[2] is not eng or eng.kind != "pe":
                    self._wait(eng, e)

    def op(self, eng, fn, reads=(), writes=(), inc=True):
        self._deps(eng, reads, writes, False)
        ins = fn()
        self.ninstr += 1
        if inc:
            eng.n += 1
            ins.then_inc(eng.sem.h, 1)
            ev = (eng.sem, eng.n, eng)
        else:
            ev = (eng.sem, eng.n + 1, eng)
        for t in writes:
            t.w = ev
            t.r = {}
        for t in reads:
            t.r[eng.name] = ev
        return ins

    def dma(self, q, out, in_, reads=(), writes=(), **kw):
        self._deps(q, reads, writes, True)
        slots = self.dslots[q.name]
        s = slots[self.di[q.name] % len(slots)]
        self.di[q.name] += 1
        if s.val > 0:
            self._wait(q, (s, s.val, None))
        ins = q.h.dma_start(out=out, in_=in_, **kw)
        s.val += 16
        ins.then_inc(s.h, 16)
        self.ninstr += 1
        ev = (s, s.val, None)
        for t in writes:
            t.w = ev
            t.r = {}
        for t in reads:
            t.r["dma%d" % s.idx] = ev
        return ins

    def finish(self):
        for q in (self.sp, self.pool):
            for s in self.dslots[q.name]:
                if s.val > 0:
                    self._wait(self.sp, (s, s.val, None))
        for e in (self.pe, self.act, self.dve, self.pool):
            if e.n > 0:
                self._wait(self.sp, (e.sem, e.n, e))


class _Stop(Exception):
    pass


class Buf:
    def __init__(self, t, fence=None):
        self.t = t
        self.toks = {}
        self.fence = fence

    def tok(self, key=0):
        if key not in self.toks:
            self.toks[key] = Tok(self.fence)
        return self.toks[key]

    def __getitem__(self, idx):
        return self.t[idx]


def build(cfg):
    nlayers = cfg.get("nlayers", NL)
    stage = cfg.get("stage", "full")
    dbg = cfg.get("dbg", {})
    nc = bass.Bass("TRN2", target_bir_lowering=False)

    def din(name, shape):
        return nc.dram_tensor(name, list(shape), F32, kind="ExternalInput").ap()

    def dout(name, shape):
        return nc.dram_tensor(name, list(shape), F32, kind="ExternalOutput").ap()

    d_x = din("xT", [128, 8, NT])
    d_ck = din("ckT", [NL, 16, 128, 1280])
    d_cv = din("cvN", [NL, 16, 128, 1280])
    d_pvec = din("pvec", [128, 136])
    d_are = din("aL_re", [128, NL, 16])
    d_aim = din("aL_im", [128, NL, 16])
    d_ldt = din("aL_ldt", [128, NL, 16])
    d_bre = din("bL_re", [128, NL, 256])
    d_bim = din("bL_im", [128, NL, 256])
    d_cre = din("cL_re", [128, NL, 256])
    d_cim = din("cL_im", [128, NL, 256])
    d_h0 = din("h0L", [128, NL, 2, 16, 4])
    d_win = din("w_in", [NL, 1024, 2048])
    d_wout = din("w_out", [NL, 1024, 1024])
    d_glu = din("glu_w", [NL, 512, 512])
    d_wg = din("wg", [NL, 1024, DFF])
    d_wu = din("wu", [NL, 1024, DFF])
    d_wd = din("wd", [NL, DFF, 1024])
    d_cid = din("c_ident", [128, 128])
    d_cmc = din("c_maskcur", [128, 512])
    d_cmp = din("c_maskprev", [128, 512])
    d_csm = din("c_smask", [128, 176])
    d_cmv = din("c_mv", [128, NMV * 16])
    d_crm = din("c_rowmask", [128, 2])

    o_y = dout("yT", [128, 8, NT])
    o_k = dout("pkT", [NL, 128, 4, NT])
    o_v = dout("pv", [NL, NP, 512])
    o_sv = dout("sv", [NL, NSM, 512])
    o_h = dout("hout", [128, NL, 2, 16, 5])
    dbg_out = {name: dout("dbg_" + name, shape) for name, shape in dbg.items()}

    es = ExitStack()
    with es:
        kb = KB(nc, es)
        PE, ACT, DVE, POOL, SP = kb.pe, kb.act, kb.dve, kb.pool, kb.sp
        V, S, T = nc.vector, nc.scalar, nc.tensor

        def sb(name, shape, dt=F32):
            return Buf(es.enter_context(nc.sbuf_tensor("s_" + name, list(shape), dt)))

        banks = [Buf(es.enter_context(nc.psum_tensor("bank%d" % i, [128, 512], F32))) for i in range(8)]
        for b_ in banks:
            b_.tok().serial = True

        xT = sb("xT", [128, 8, NT])
        pvec = sb("pvec", [128, 136])
        ident = sb("ident", [128, 128], BF16)
        identf = sb("identf", [128, 128])
        ones = sb("ones", [128, 128], BF16)
        maskcur = sb("maskcur", [128, 512], BF16)
        maskprev = sb("maskprev", [128, 512], BF16)
        smask = sb("smask", [128, 176], BF16)
        mvt = sb("mvt", [128, NMV, 16])
        rowmask = sb("rowmask", [128, 2])
        xnT = sb("xnT", [128, 8, NT], BF16)
        mixA = sb("mixA", [128, 4, NT], BF16)
        arena = es.enter_context(nc.sbuf_tensor("arena", [128, ARENA // 4], F32))

        class Phase:
            prev_bufs = []

            def __init__(self, base=0):
                fence = {}
                for b in Phase.prev_bufs:
                    for t in b.toks.values():
                        evs = list(t.r.values())
                        if t.w is not None:
                            evs.append(t.w)
                        for ev in evs:
                            k = ev[0].idx
                            if k not in fence or fence[k][1] < ev[1]:
                                fence[k] = ev
                self.fence = {("f", k): v for k, v in fence.items()}
                self.off = base
                self.bufs = []
                Phase.prev_bufs = self.bufs

            def take(self, shape, dt=F32):
                fshape = list(shape[1:])
                n = 1
                for s_ in fshape:
                    n *= s_
                nbytes = n * (4 if dt == F32 else 2)
                nwords = (nbytes + 3) // 4
                ap = arena[:, self.off:self.off + nwords]
                self.off += nwords
                assert self.off * 4 <= ARENA, ("arena overflow", self.off * 4)
                Phase.maxoff = max(getattr(Phase, "maxoff", 0), self.off * 4)
                if dt != F32:
                    ap = ap.bitcast(dt)[:, 0:n]
                if len(fshape) > 1:
                    names = "abcdefgh"[:len(fshape)]
                    pat = "p (" + " ".join(names) + ") -> p " + " ".join(names)
                    ap = ap.rearrange(pat, **{names[i]: fshape[i] for i in range(len(fshape))})
                b = Buf(ap, self.fence)
                self.bufs.append(b)
                return b

        for ci, (c0, cn) in enumerate(CTS):
            kb.dma(SP, xT[:, :, c0:c0 + cn], d_x[:, :, c0:c0 + cn], writes=[xT.tok(ci)])
        kb.dma(SP, pvec[:], d_pvec[:, :], writes=[pvec.tok()])
        kb.dma(SP, mvt[:], d_cmv.rearrange("p (m g) -> p m g", g=16), writes=[mvt.tok()])
        kb.dma(SP, rowmask[:], d_crm[:, :], writes=[rowmask.tok()])
        kb.dma(POOL, ident[:], d_cid[:, :], writes=[ident.tok()])
        kb.dma(SP, identf[:], d_cid[:, :], writes=[identf.tok()])
        kb.dma(POOL, maskcur[:], d_cmc[:, :], writes=[maskcur.tok()])
        kb.dma(POOL, maskprev[:], d_cmp[:, :], writes=[maskprev.tok()])
        kb.dma(POOL, smask[:], d_csm[:, :], writes=[smask.tok()])
        kb.op(DVE, lambda: V.memset(ones[:], 1.0), writes=[ones.tok()])

        def pv_col(l, which, c):
            base = {"n1": 0, "n2": 8, "ag": 16, "sg": 20, "gb": 24, "sd": 28}[which]
            o = 32 * l + base + c
            return pvec[:, o:o + 1]

        rr = {"b": 0}

        def next_bank(group):
            i = group[rr["b"] % len(group)]
            rr["b"] += 1
            return banks[i]

        def cut(n):
            if cfg.get("cut") == n:
                raise _Stop()

        def dump(name, ap, reads):
            if name in dbg_out:
                kb.dma(POOL, dbg_out[name], ap, reads=reads, max_dma_last_dim=2048)

        def rmsnorm_to_bf16(src, src_tok, nchunk, dst, dst_tok, gcol, width, ci, sq, rt, bank_group):
            c0, cn = CTS[ci]
            bk = next_bank(bank_group)
            for c in range(nchunk):
                sqb = sq[c % 2]
                kb.op(ACT, lambda c=c, sqb=sqb: S.activation(out=sqb[:, 0:cn], in_=src[:, c, c0:c0 + cn], func=AF.Square),
                      reads=[src_tok], writes=[sqb.tok()])
                kb.op(PE, lambda c=c, sqb=sqb: T.matmul(bk[:, 0:cn], ones[:], sqb[:, 0:cn], start=(c == 0), stop=(c == nchunk - 1)),
                      reads=[ones.tok(), sqb.tok()], writes=[bk.tok()], inc=True)
            kb.op(ACT, lambda: S.activation(out=rt[:, 0:cn], in_=bk[:, 0:cn], func=AF.Sqrt, scale=1.0 / width, bias=EPS),
                  reads=[bk.tok()], writes=[rt.tok()])
            kb.op(DVE, lambda: V.reciprocal(out=rt[:, 0:cn], in_=rt[:, 0:cn]), reads=[rt.tok()], writes=[rt.tok()])
            for c in range(nchunk):
                kb.op(DVE, lambda c=c: V.scalar_tensor_tensor(out=dst[:, c, c0:c0 + cn], in0=src[:, c, c0:c0 + cn], scalar=gcol(c),
                                                              in1=rt[:, 0:cn], op0=ALU.mult, op1=ALU.mult),
                      reads=[src_tok, rt.tok(), pvec.tok()], writes=[dst_tok])

        if stage == "load":
            dump("x0", xT[:, 0, :], [xT.tok(c) for c in range(5)])
            nlayers = 0
        try:
            for l in range(nlayers):
                ph = Phase()
                sq = [ph.take([128, 512], BF16), ph.take([128, 512], BF16)]
                rt = ph.take([128, 512])
                qz = ph.take([128, 2, NT], BF16)
                kTp = ph.take([128, NT], BF16)
                vaug = ph.take([128, 3, 16, 192], BF16)
                vS = ph.take([128, 4, 128], BF16)
                wsl = [ph.take([128, 8, 3, 128], BF16), ph.take([128, 8, 3, 128], BF16)]
                pT = [ph.take([128, 512], BF16) for _ in range(3)]
                rcp = [ph.take([128, 512]) for _ in range(1)]
                kst = [ph.take([128, 512]) for _ in range(2)]
                vst = [ph.take([128, 4, 128]) for _ in range(1)]
                svst = ph.take([128, 128])
                kcs = [ph.take([128, 1280], BF16) for _ in range(4)]
                vcs = [ph.take([128, 10, 128], BF16) for _ in range(4)]
                pSs = [ph.take([128, 176], BF16) for _ in range(2)]
                pSns = [ph.take([128, 16], BF16) for _ in range(2)]
                rSs = [ph.take([128, 16]) for _ in range(2)]

                if l == 0 or stage != "full":
                    for ci in range(5):
                        rmsnorm_to_bf16(xT, xT.tok(ci), 8, xnT, xnT.tok(ci), lambda c: pv_col(l, "n1", c), 1024.0, ci, sq, rt, [6, 7])
                if stage == "norm":
                    dump("xnT", xnT[:, :, :], [xnT.tok(c) for c in range(5)])
                    break
                kb.op(POOL, lambda: nc.gpsimd.memset(qz[:], 0.0), writes=[qz.tok(("z", 0)), qz.tok(("z", 1))])
                kb.op(POOL, lambda: nc.gpsimd.memset(vaug[:, :, :, 64:128], 1.0), writes=[vaug.tok("ones")])
                kb.op(POOL, lambda: nc.gpsimd.memset(vS[:], 0.0), writes=[vS.tok()])
                for pSn_i in pSns:
                    kb.op(POOL, lambda: nc.gpsimd.memset(pSn_i[:], 0.0), writes=[pSn_i.tok()])

                cut(1)
                win_v = d_win[l].rearrange("(kc p) n -> p kc n", p=128)
                IPB = [6, 7, 0, 1, 2, 3, 4, 5]
                def load_pair_w(hp_):
                    w_ = wsl[hp_ % 2]
                    for j in range(3):
                        kb.dma(POOL, w_[:, :, j, :], win_v[:, :, 512 * j + 128 * hp_: 512 * j + 128 * hp_ + 128], writes=[w_.tok(j)])
                load_pair_w(0)
                load_pair_w(1)
                for hp in range(4):
                    w = wsl[hp % 2]
                    cut(21)
                    for ci, (c0, cn) in enumerate(CTS):
                        bk = next_bank(IPB)
                        for kc in range(8):
                            kb.op(PE, lambda kc=kc: T.matmul(bk[:, 0:cn], w[:, kc, 0, :], xnT[:, kc, c0:c0 + cn], start=(kc == 0), stop=(kc == 7)),
                                  reads=[w.tok(0), xnT.tok(ci)], writes=[bk.tok()], inc=(kc == 7))
                        if ci == 1:
                            cut(31)
                        kb.op(ACT, lambda: S.copy(out=qz[0:64, 0, c0:c0 + cn], in_=bk[0:64, 0:cn]),
                              reads=[bk.tok(), qz.tok(("z", 0))], writes=[qz.tok((0, ci))])
                        if ci == 1:
                            cut(32)
                        kb.op(DVE, lambda: V.tensor_copy(out=qz[64:128, 1, c0:c0 + cn], in_=bk[64:128, 0:cn]),
                              reads=[bk.tok(), qz.tok(("z", 1))], writes=[qz.tok((1, ci))])
                        if ci == 1:
                            cut(33)
                        cut(22)
                        bk = next_bank(IPB)
                        for kc in range(8):
                            kb.op(PE, lambda kc=kc: T.matmul(bk[:, 0:cn], w[:, kc, 1, :], xnT[:, kc, c0:c0 + cn], start=(kc == 0), stop=(kc == 7)),
                                  reads=[w.tok(1), xnT.tok(ci)], writes=[bk.tok()], inc=(kc == 7))
                        ks = kst[ci % 2]
                        if ci == 1:
                            cut(34)
                        kb.op(ACT, lambda: S.copy(out=kTp[:, c0:c0 + cn], in_=bk[:, 0:cn]), reads=[bk.tok()], writes=[kTp.tok(ci)])
                        if ci == 1:
                            cut(35)
                        if cfg.get("kcopy", "dve") == "dve":
                            kb.op(DVE, lambda: V.tensor_copy(out=ks[:, 0:cn], in_=bk[:, 0:cn]), reads=[bk.tok()], writes=[ks.tok()])
                        elif cfg.get("kcopy") == "act":
                            kb.op(ACT, lambda: S.copy(out=ks[:, 0:cn], in_=bk[:, 0:cn]), reads=[bk.tok()], writes=[ks.tok()])
                        cut(23)
                        if ci == 1:
                            cut(36)
                        kb.dma(SP, o_k[l, :, hp, c0:c0 + cn], ks[:, 0:cn], reads=[ks.tok()])
                        cut(24)
                        if ci == 1:
                            cut(25)
                        if ci == 3:
                            cut(26)
                    cut(2)
                    def tok_ap(o, ti, kc):
                        if o == 0:
                            return xnT[:, kc, 128 * ti:128 * ti + 128]
                        if o == 1:
                            G, r = ti // 4, ti % 4
                            return xnT[:, kc, 512 * G + r:512 * G + 512:4]
                        return xnT[:, kc, ti:2048:16]
                    for o in range(3):
                        cut(3 + o)
                        for tb in range(4):
                            bk = next_bank(IPB)
                            for j in range(4):
                                ti = 4 * tb + j
                                for kc in range(8):
                                    kb.op(PE, lambda kc=kc, ti=ti, j=j: T.matmul(bk[:, 128 * j:128 * j + 128], tok_ap(o, ti, kc), w[:, kc, 2, :],
                                                                                  start=(kc == 0), stop=(kc == 7)),
                                          reads=[w.tok(2)] + [xnT.tok(c) for c in range(4)], writes=[bk.tok()], inc=(kc == 7 and j == 3))
                            bv = bk[:].rearrange("p (j c) -> p j c", c=128)
                            kb.op(ACT, lambda: S.copy(out=vaug[:, o, 4 * tb:4 * tb + 4, 0:64], in_=bv[:, :, 0:64]),
                                  reads=[bk.tok()], writes=[vaug.tok((o, tb, 0))])
                            kb.op(DVE, lambda: V.tensor_copy(out=vaug[:, o, 4 * tb:4 * tb + 4, 128:192], in_=bv[:, :, 64:128]),
                                  reads=[bk.tok()], writes=[vaug.tok((o, tb, 1))])
                            if o == 0:
                                vs_ = vst[0]
                                kb.op(DVE, lambda: V.tensor_copy(out=vs_[:], in_=bv), reads=[bk.tok()], writes=[vs_.tok()])
                                kb.dma(SP, o_v[l, 512 * tb:512 * tb + 512, 128 * hp:128 * hp + 128].rearrange("(j p) c -> p j c", p=128), vs_[:],
                                       reads=[vs_.tok()])
                    cut(6)
                    bk = next_bank([6, 7])
                    for kc in range(8):
                        kb.op(PE, lambda kc=kc: T.matmul(bk[0:32, 0:128], xnT[:, kc, 2048:2080], w[:, kc, 2, :], start=(kc == 0), stop=(kc == 7)),
                              reads=[w.tok(2), xnT.tok(4)], writes=[bk.tok()], inc=(kc == 7))
                    kb.op(DVE, lambda: V.tensor_copy(out=svst[0:32, :], in_=bk[0:32, 0:128]), reads=[bk.tok()], writes=[svst.tok()])
                    kb.dma(SP, o_sv[l, :, 128 * hp:128 * hp + 128], svst[0:32, :], reads=[svst.tok()])
                    bk = next_bank([6, 7])
                    for b in range(4):
                        for kc in range(8):
                            kb.op(PE, lambda kc=kc, b=b: T.matmul(bk[0:8, 128 * b:128 * b + 128], xnT[:, kc, 2048 + 8 * b:2056 + 8 * b], w[:, kc, 2, :],
                                                                  start=(kc == 0), stop=(kc == 7)),
                                  reads=[w.tok(2), xnT.tok(4)], writes=[bk.tok()], inc=(kc == 7 and b == 3))
                    kb.op(DVE, lambda: V.tensor_copy(out=vS[0:8, :, :], in_=bk[0:8, :].rearrange("p (b c) -> p b c", c=128)),
                          reads=[bk.tok()], writes=[vS.tok()])

                    cut(7)
                    if hp + 2 < 4:
                        load_pair_w(hp + 2)
                    if stage == "inproj" and hp == 0:
                        dump("qz", qz[:], [qz.tok((0, c)) for c in range(5)] + [qz.tok((1, c)) for c in range(5)])
                        dump("kTp", kTp[:], [kTp.tok(c) for c in range(5)])
                        dump("vaug", vaug[:].rearrange("p o t c -> p (o t c)"), [vaug.tok((o, tb, a)) for o in range(3) for tb in range(4) for a in range(2)] + [vaug.tok("ones")])
                        dump("vS", vS[:].rearrange("p b c -> p (b c)"), [vS.tok()])
                        break

                    qz_reads = lambda a: [qz.tok((a, c)) for c in range(4)] + [qz.tok(("z", 1 - a))]
                    kT_reads = [kTp.tok(c) for c in range(4)]
                    for a in range(2):
                        tiles = []
                        for c in range(16):
                            tiles.append(("cur", slice(128 * c, 128 * c + 128), slice(128 * c, 128 * c + 128), (0, c),
                                          [(c // 4, slice(128 * (c % 4), 128 * (c % 4) + 128), slice(0, 128))]))
                        for r in range(4):
                            for j in range(4):
                                s_ = slice(512 * j + r, 512 * j + 512, 4)
                                tiles.append(("cur", s_, s_, (1, 4 * j + r), [(j, slice(r, 512, 4), slice(0, 128))]))
                        for r in range(16):
                            s_ = slice(r, 2048, 16)
                            tiles.append(("cur", s_, s_, (2, r), [(G, slice(r, 512, 16), slice(32 * G, 32 * G + 32)) for G in range(4)]))
                        for c in range(1, 16):
                            tiles.append(("prev", slice(128 * (c - 1), 128 * c), slice(128 * c, 128 * c + 128), (0, c - 1),
                                          [(c // 4, slice(128 * (c % 4), 128 * (c % 4) + 128), slice(0, 128))]))
                        for r in range(4):
                            for j in range(1, 4):
                                ks_ = slice(512 * (j - 1) + r, 512 * j, 4)
                                qs_ = slice(512 * j + r, 512 * j + 512, 4)
                                tiles.append(("prev", ks_, qs_, (1, 4 * (j - 1) + r), [(j, slice(r, 512, 4), slice(0, 128))]))
                        started = [False] * 4
                        nb_ = 0
                        i0 = 0

                        def emit_pv(batch, pt_):
                            nbt = len(batch)
                            for j, tl in enumerate(batch):
                                o, vt = tl[3]
                                nd = len(tl[4])
                                for di, (G, ocols, pcols) in enumerate(tl[4]):
                                    xb = banks[G]
                                    st = not started[G]
                                    started[G] = True
                                    pc = slice(128 * j + pcols.start, 128 * j + pcols.stop)
                                    lastpv = (j == nbt - 1 and di == nd - 1)
                                    kb.op(PE, lambda o=o, vt=vt, ocols=ocols, pc=pc, xb=xb, st=st:
                                          T.matmul(xb[:, ocols], vaug[:, o, vt, 64 * a:64 * a + 128], pt_[:, pc], start=st, stop=False, skip_group_check=True),
                                          reads=[pt_.tok(), vaug.tok((o, vt // 4, a)), vaug.tok("ones")], writes=[xb.tok()], inc=lastpv)
                        pending = None
                        while i0 < len(tiles):
                            mk = tiles[i0][0]
                            batch = [tiles[i0]]
                            while len(batch) < 4 and i0 + len(batch) < len(tiles) and tiles[i0 + len(batch)][0] == mk:
                                batch.append(tiles[i0 + len(batch)])
                            i0 += len(batch)
                            nbt = len(batch)
                            sb_ = banks[4 + (nb_ % 2)]
                            pt_ = pT[nb_ % 3]
                            nb_ += 1
                            mt = maskcur if mk == "cur" else maskprev
                            kb.op(PE, lambda: T.matmul(sb_[:, 0:128 * nbt], ident[:], mt[:, 0:128 * nbt], start=True, stop=False),
                                  reads=[ident.tok(), mt.tok()], writes=[sb_.tok()], inc=False)
                            for j, tl in enumerate(batch):
                                kb.op(PE, lambda j=j, tl=tl: T.matmul(sb_[:, 128 * j:128 * j + 128], kTp[:, tl[1]], qz[:, a, tl[2]], start=False, stop=(j == nbt - 1)),
                                      reads=kT_reads + qz_reads(a), writes=[sb_.tok()], inc=(j == nbt - 1))
                            kb.op(ACT, lambda: S.activation(out=pt_[:, 0:128 * nbt], in_=sb_[:, 0:128 * nbt], func=AF.Exp, scale=0.125),
                                  reads=[sb_.tok()], writes=[pt_.tok()])
                            if pending is not None:
                                emit_pv(*pending)
                            pending = (batch, pt_)
                        emit_pv(*pending)
                        for G in range(4):
                            xb = banks[G]
                            rc = rcp[0]
                            if a == 0:
                                kb.op(DVE, lambda: V.reciprocal(out=rc[0:64, :], in_=xb[64:128, :]), reads=[xb.tok()], writes=[rc.tok()])
                                kb.op(DVE, lambda: V.tensor_tensor(out=mixA[0:64, hp, 512 * G:512 * G + 512], in0=xb[0:64, :], in1=rc[0:64, :], op=ALU.mult),
                                      reads=[xb.tok(), rc.tok()], writes=[mixA.tok((hp, G, 0))])
                            else:
                                kb.op(DVE, lambda: V.reciprocal(out=rc[64:128, :], in_=xb[0:64, :]), reads=[xb.tok()], writes=[rc.tok()])
                                kb.op(DVE, lambda: V.tensor_tensor(out=mixA[64:128, hp, 512 * G:512 * G + 512], in0=xb[64:128, :], in1=rc[64:128, :], op=ALU.mult),
                                      reads=[xb.tok(), rc.tok()], writes=[mixA.tok((hp, G, 1))])

                    def samp_scores(b):
                        kc_ = kcs[b]
                        vc_ = vcs[b]
                        pS_ = pSs[b % 2]
                        pSn_ = pSns[b % 2]
                        kb.dma(POOL, kc_[:], d_ck[l, 4 * b + hp, :, :], writes=[kc_.tok()])
                        kb.dma(POOL, vc_[:], d_cv[l, 4 * b + hp, :, :].rearrange("p (t c) -> p t c", c=128), writes=[vc_.tok()])
                        sbk = next_bank([4, 5])
                        qs = qz[:, :, 2048 + 8 * b:2056 + 8 * b]
                        qrd = [qz.tok((0, 4)), qz.tok((1, 4)), qz.tok(("z", 0)), qz.tok(("z", 1))]
                        for ti in range(10):
                            kb.op(PE, lambda ti=ti: T.matmul(sbk[:, 16 * ti:16 * ti + 16].rearrange("p (a q) -> p a q", a=2), kc_[:, 128 * ti:128 * ti + 128], qs,
                                                             start=True, stop=True),
                                  reads=[kc_.tok()] + qrd, writes=[sbk.tok()], inc=False)
                        kb.op(PE, lambda: T.matmul(sbk[0:8, 160:176].rearrange("p (a q) -> p a q", a=2), kTp[:, 2048 + 8 * b:2056 + 8 * b], qs, start=True, stop=True),
                              reads=[kTp.tok(4)] + qrd, writes=[sbk.tok()], inc=True)
                        kb.op(ACT, lambda: S.activation(out=pS_[:, 0:160], in_=sbk[:, 0:160], func=AF.Exp, scale=0.125), reads=[sbk.tok()], writes=[pS_.tok()])
                        kb.op(ACT, lambda: S.activation(out=pSn_[0:8, :], in_=sbk[0:8, 160:176], func=AF.Exp, scale=0.125), reads=[sbk.tok()], writes=[pSn_.tok()])
                        kb.op(DVE, lambda: V.tensor_tensor(out=pS_[:, 0:160], in0=pS_[:, 0:160], in1=smask[:, 0:160], op=ALU.mult),
                              reads=[pS_.tok(), smask.tok()], writes=[pS_.tok()])
                        kb.op(DVE, lambda: V.tensor_tensor(out=pSn_[0:8, :], in0=pSn_[0:8, :], in1=smask[0:8, 160:176], op=ALU.mult),
                              reads=[pSn_.tok(), smask.tok()], writes=[pSn_.tok()])

                    def samp_pv(b):
                        vc_ = vcs[b]
                        pS_ = pSs[b % 2]
                        pSn_ = pSns[b % 2]
                        rS_ = rSs[b % 2]
                        nb = next_bank([6, 7])
                        db = next_bank([6, 7])
                        for ti in range(10):
                            kb.op(PE, lambda ti=ti: T.matmul(nb[:, 0:16], vc_[:, ti, :], pS_[:, 16 * ti:16 * ti + 16], start=(ti == 0), stop=False),
                                  reads=[vc_.tok(), pS_.tok()], writes=[nb.tok()], inc=False)
                        kb.op(PE, lambda: T.matmul(nb[:, 0:16], vS[:, b, :], pSn_[:, :], start=False, stop=True), reads=[vS.tok(), pSn_.tok()], writes=[nb.tok()], inc=True)
                        for ti in range(10):
                            kb.op(PE, lambda ti=ti: T.matmul(db[:, 0:16], ones[:], pS_[:, 16 * ti:16 * ti + 16], start=(ti == 0), stop=False),
                                  reads=[ones.tok(), pS_.tok()], writes=[db.tok()], inc=False)
                        kb.op(PE, lambda: T.matmul(db[:, 0:16], ones[:], pSn_[:, :], start=False, stop=True), reads=[ones.tok(), pSn_.tok()], writes=[db.tok()], inc=True)
                        kb.op(DVE, lambda: V.reciprocal(out=rS_[:, :], in_=db[:, 0:16]), reads=[db.tok()], writes=[rS_.tok()])
                        kb.op(DVE, lambda: V.tensor_tensor(out=mixA[0:64, hp, 2048 + 8 * b:2056 + 8 * b], in0=nb[0:64, 0:8], in1=rS_[0:64, 0:8], op=ALU.mult),
                              reads=[nb.tok(), rS_.tok()], writes=[mixA.tok((hp, 4, 0))])
                        kb.op(DVE, lambda: V.tensor_tensor(out=mixA[64:128, hp, 2048 + 8 * b:2056 + 8 * b], in0=nb[64:128, 8:16], in1=rS_[64:128, 8:16], op=ALU.mult),
                              reads=[nb.tok(), rS_.tok()], writes=[mixA.tok((hp, 4, 1))])
                    samp_scores(0)
                    for b in range(1, 4):
                        samp_scores(b)
                        samp_pv(b - 1)
                    samp_pv(3)
                if stage == "inproj":
                    break
                if stage == "attn":
                    dump("mixA", mixA[:].rearrange("p c t -> p (c t)"), [mixA.tok((hp, G, a)) for hp in range(4) for G in range(5) for a in range(2)])
                    break

                UZW = (4 * NT * 2) // 4
                ph = Phase()
                uz = ph.take([128, 4, NT], BF16)
                wub = ph.take([128, 8, 512], BF16)
                kb.dma(POOL, wub[:], win_v[:, :, 1536:2048], writes=[wub.tok()])
                flip = 0
                for oc in range(4):
                    for ci, (c0, cn) in enumerate(CTS):
                        bk = next_bank([0, 1, 2, 3])
                        for kc in range(8):
                            kb.op(PE, lambda kc=kc: T.matmul(bk[:, 0:cn], wub[:, kc, 128 * oc:128 * oc + 128], xnT[:, kc, c0:c0 + cn], start=(kc == 0), stop=(kc == 7)),
                                  reads=[wub.tok(), xnT.tok(ci)], writes=[bk.tok()], inc=(kc == 7))
                        if flip % 2 == 0:
                            kb.op(ACT, lambda: S.copy(out=uz[:, oc, c0:c0 + cn], in_=bk[:, 0:cn]), reads=[bk.tok()], writes=[uz.tok((oc, ci))])
                        else:
                            kb.op(DVE, lambda: V.tensor_copy(out=uz[:, oc, c0:c0 + cn], in_=bk[:, 0:cn]), reads=[bk.tok()], writes=[uz.tok((oc, ci))])
                        flip += 1
                uz_all = [uz.tok((oc, ci)) for oc in range(4) for ci in range(5)]

                for hh in range(2):
                    ph2 = Phase(base=UZW)
                    ph2.bufs.append(uz)
                    g0 = 8 * hh

                    def tk(shape, dt=F32):
                        return ph2.take(shape, dt)
                    are = tk([128, 8]); aim = tk([128, 8]); ldt = tk([128, 8])
                    bre = tk([128, 8, 16]); bim = tk([128, 8, 16]); cre = tk([128, 8, 16]); cim = tk([128, 8, 16])
                    h0 = tk([128, 2, 8, 4])
                    kb.dma(SP, are[:], d_are[:, l, g0:g0 + 8], writes=[are.tok()])
                    kb.dma(SP, aim[:], d_aim[:, l, g0:g0 + 8], writes=[aim.tok()])
                    kb.dma(SP, ldt[:], d_ldt[:, l, g0:g0 + 8], writes=[ldt.tok()])
                    kb.dma(SP, bre[:], d_bre[:, l, 16 * g0:16 * g0 + 128].rearrange("p (g c) -> p g c", c=16), writes=[bre.tok()])
                    kb.dma(SP, bim[:], d_bim[:, l, 16 * g0:16 * g0 + 128].rearrange("p (g c) -> p g c", c=16), writes=[bim.tok()])
                    kb.dma(SP, cre[:], d_cre[:, l, 16 * g0:16 * g0 + 128].rearrange("p (g c) -> p g c", c=16), writes=[cre.tok()])
                    kb.dma(SP, cim[:], d_cim[:, l, 16 * g0:16 * g0 + 128].rearrange("p (g c) -> p g c", c=16), writes=[cim.tok()])
                    kb.dma(SP, h0[:], d_h0[:, l, :, g0:g0 + 8, :], writes=[h0.tok()])
                    dtt = tk([128, 8]); lr = tk([128, 8]); rho = tk([128, 8]); th = tk([128, 8])
                    t_a = tk([128, NMV, 8]); t_b = tk([128, NMV, 8]); t_c = tk([128, NMV, 8]); t_d = tk([128, NMV, 8])
                    Er = tk([128, NMV, 8]); Ei = tk([128, NMV, 8])
                    s1 = tk([128, 8]); s2 = tk([128, 8]); s3 = tk([128, 8]); s4 = tk([128, 8]); fr = tk([128, 8]); fi = tk([128, 8])
                    bbr = tk([128, 8, 16]); bbi = tk([128, 8, 16]); tb1 = tk([128, 8, 16])
                    scr = tk([128, 2, 1152])
                    RW = tk([128, 3072])
                    Xr = tk([128, 9, 8, 16], BF16); XiN = tk([128, 9, 8, 16], BF16)
                    Kblk = tk([128, 2, 8, 128], BF16)
                    Harr = tk([128, 2, 8, 257])
                    Ss = tk([128, 2, 8, 4]); Hs = tk([128, 2, 8, 4]); tS = tk([128, 2, 8, 4])
                    tl1 = tk([128, 8, 16]); tl2 = tk([128, 8, 16]); tl3 = tk([128, 8, 16]); tl4 = tk([128, 8, 16])
                    Hb = [tk([128, 2, 8, 64], BF16) for _ in range(2)]
                    ytmp = [tk([128, 512]) for _ in range(2)]
                    rwb = RW[:].bitcast(BF16)
                    WT = [rwb[:, 2048 * e:2048 * e + 2048].rearrange("p (m r c) -> p m r c", m=8, r=2) for e in range(2)]
                    BbPad = rwb[:, 4096:6144].rearrange("p (g r c) -> p g r c", g=8, r=2)
                    PT = rwb[:, 0:4096].rearrange("p (t g i c) -> p t g i c", t=8, g=2, i=8)
                    rw = RW.tok()

                    def bc_m(x):
                        return x[:].unsqueeze(1).broadcast_to([128, NMV, 8])

                    def dv(fn, reads, writes):
                        kb.op(DVE, fn, reads=reads, writes=writes)

                    kb.op(ACT, lambda: S.activation(out=dtt[:], in_=ldt[:], func=AF.Exp), reads=[ldt.tok()], writes=[dtt.tok()])
                    dv(lambda: V.tensor_scalar(out=lr[:], in0=are[:], scalar1=-1e-4, scalar2=None, op0=ALU.min), [are.tok()], [lr.tok()])
                    dv(lambda: V.tensor_tensor(out=rho[:], in0=lr[:], in1=dtt[:], op=ALU.mult), [lr.tok(), dtt.tok()], [rho.tok()])
                    dv(lambda: V.tensor_tensor(out=th[:], in0=aim[:], in1=dtt[:], op=ALU.mult), [aim.tok(), dtt.tok()], [th.tok()])
                    dv(lambda: V.tensor_tensor(out=t_a[:], in0=bc_m(rho), in1=mvt[:, :, 0:8], op=ALU.mult), [rho.tok(), mvt.tok()], [t_a.tok()])
                    dv(lambda: V.tensor_tensor(out=t_b[:], in0=bc_m(th), in1=mvt[:, :, 0:8], op=ALU.mult), [th.tok(), mvt.tok()], [t_b.tok()])
                    kb.op(ACT, lambda: S.activation(out=t_a[:], in_=t_a[:], func=AF.Exp), reads=[t_a.tok()], writes=[t_a.tok()])
                    dv(lambda: V.tensor_scalar(out=t_c[:], in0=t_b[:], scalar1=1.0 / TWO_PI, scalar2=MAGIC, op0=ALU.mult, op1=ALU.add), [t_b.tok()], [t_c.tok()])
                    dv(lambda: V.tensor_scalar(out=t_c[:], in0=t_c[:], scalar1=-MAGIC, scalar2=None, op0=ALU.add), [t_c.tok()], [t_c.tok()])
                    dv(lambda: V.scalar_tensor_tensor(out=t_b[:], in0=t_c[:], scalar=-CW1, in1=t_b[:], op0=ALU.mult, op1=ALU.add), [t_c.tok(), t_b.tok()], [t_b.tok()])
                    dv(lambda: V.scalar_tensor_tensor(out=t_b[:], in0=t_c[:], scalar=-CW2, in1=t_b[:], op0=ALU.mult, op1=ALU.add), [t_c.tok(), t_b.tok()], [t_b.tok()])
                    dv(lambda: V.tensor_scalar(out=t_b[:], in0=t_b[:], scalar1=-np.pi, scalar2=np.pi, op0=ALU.max, op1=ALU.min), [t_b.tok()], [t_b.tok()])
                    kb.op(ACT, lambda: S.activation(out=t_c[:], in_=t_b[:], func=AF.Sin), reads=[t_b.tok()], writes=[t_c.tok()])
                    kb.op(ACT, lambda: S.activation(out=t_d[:], in_=t_b[:], func=AF.Sin, scale=0.5), reads=[t_b.tok()], writes=[t_d.tok()])
                    dv(lambda: V.tensor_tensor(out=t_d[:], in0=t_d[:], in1=t_d[:], op=ALU.mult), [t_d.tok()], [t_d.tok()])
                    dv(lambda: V.tensor_scalar(out=t_d[:], in0=t_d[:], scalar1=-2.0, scalar2=1.0, op0=ALU.mult, op1=ALU.add), [t_d.tok()], [t_d.tok()])
                    dv(lambda: V.tensor_tensor(out=Er[:], in0=t_a[:], in1=t_d[:], op=ALU.mult), [t_a.tok(), t_d.tok()], [Er.tok()])
                    dv(lambda: V.tensor_tensor(out=Ei[:], in0=t_a[:], in1=t_c[:], op=ALU.mult), [t_a.tok(), t_c.tok()], [Ei.tok()])
                    dv(lambda: V.tensor_scalar(out=s1[:], in0=Er[:, 1, :], scalar1=-1.0, scalar2=None, op0=ALU.add), [Er.tok()], [s1.tok()])
                    dv(lambda: V.tensor_tensor(out=s2[:], in0=lr[:], in1=lr[:], op=ALU.mult), [lr.tok()], [s2.tok()])
                    dv(lambda: V.tensor_tensor(out=s3[:], in0=aim[:], in1=aim[:], op=ALU.mult), [aim.tok()], [s3.tok()])
                    dv(lambda: V.tensor_tensor(out=s2[:], in0=s2[:], in1=s3[:], op=ALU.add), [s2.tok(), s3.tok()], [s2.tok()])
                    dv(lambda: V.reciprocal(out=s2[:], in_=s2[:]), [s2.tok()], [s2.tok()])
                    dv(lambda: V.tensor_tensor(out=fr[:], in0=s1[:], in1=lr[:], op=ALU.mult), [s1.tok(), lr.tok()], [fr.tok()])
                    dv(lambda: V.tensor_tensor(out=s3[:], in0=Ei[:, 1, :], in1=aim[:], op=ALU.mult), [Ei.tok(), aim.tok()], [s3.tok()])
                    dv(lambda: V.tensor_tensor(out=fr[:], in0=fr[:], in1=s3[:], op=ALU.add), [fr.tok(), s3.tok()], [fr.tok()])
                    dv(lambda: V.tensor_tensor(out=fr[:], in0=fr[:], in1=s2[:], op=ALU.mult), [fr.tok(), s2.tok()], [fr.tok()])
                    dv(lambda: V.tensor_tensor(out=fi[:], in0=Ei[:, 1, :], in1=lr[:], op=ALU.mult), [Ei.tok(), lr.tok()], [fi.tok()])
                    dv(lambda: V.tensor_tensor(out=s3[:], in0=s1[:], in1=aim[:], op=ALU.mult), [s1.tok(), aim.tok()], [s3.tok()])
                    dv(lambda: V.tensor_tensor(out=fi[:], in0=fi[:], in1=s3[:], op=ALU.subtract), [fi.tok(), s3.tok()], [fi.tok()])
                    dv(lambda: V.tensor_tensor(out=fi[:], in0=fi[:], in1=s2[:], op=ALU.mult), [fi.tok(), s2.tok()], [fi.tok()])

                    def bc_c(x):
                        return x[:].unsqueeze(2).broadcast_to([128, 8, 16])
                    dv(lambda: V.tensor_tensor(out=bbr[:], in0=bc_c(fr), in1=bre[:], op=ALU.mult), [fr.tok(), bre.tok()], [bbr.tok()])
                    dv(lambda: V.tensor_tensor(out=tb1[:], in0=bc_c(fi), in1=bim[:], op=ALU.mult), [fi.tok(), bim.tok()], [tb1.tok()])
                    dv(lambda: V.tensor_tensor(out=bbr[:], in0=bbr[:], in1=tb1[:], op=ALU.subtract), [bbr.tok(), tb1.tok()], [bbr.tok()])
                    dv(lambda: V.tensor_tensor(out=bbi[:], in0=bc_c(fr), in1=bim[:], op=ALU.mult), [fr.tok(), bim.tok()], [bbi.tok()])
                    dv(lambda: V.tensor_tensor(out=tb1[:], in0=bc_c(fi), in1=bre[:], op=ALU.mult), [fi.tok(), bre.tok()], [tb1.tok()])
                    dv(lambda: V.tensor_tensor(out=bbi[:], in0=bbi[:], in1=tb1[:], op=ALU.add), [bbi.tok(), tb1.tok()], [bbi.tok()])

                    def Em(E_, n_m):
                        return E_[:, 0:n_m, :].unsqueeze(3).broadcast_to([128, n_m, 8, 16])

                    def Bm(b_, n_m):
                        return b_[:].unsqueeze(1).broadcast_to([128, n_m, 8, 16])
                    Wr = scr[:, 0, 0:1024].rearrange("p (m g c) -> p m g c", m=8, g=8)
                    Wi = scr[:, 1, 0:1024].rearrange("p (m g c) -> p m g c", m=8, g=8)
                    tmpW = Harr[:, 0, :, 0:128].rearrange("p g (m c) -> p m g c", m=8)
                    st = scr.tok()
                    ht = Harr.tok()
                    dv(lambda: V.tensor_tensor(out=Wr, in0=Em(Er, 8), in1=Bm(bbr, 8), op=ALU.mult), [Er.tok(), bbr.tok()], [st])
                    dv(lambda: V.tensor_tensor(out=tmpW, in0=Em(Ei, 8), in1=Bm(bbi, 8), op=ALU.mult), [Ei.tok(), bbi.tok()], [ht])
                    dv(lambda: V.tensor_tensor(out=Wr, in0=Wr, in1=tmpW, op=ALU.subtract), [st, ht], [st])
                    dv(lambda: V.tensor_tensor(out=Wi, in0=Em(Er, 8), in1=Bm(bbi, 8), op=ALU.mult), [Er.tok(), bbi.tok()], [st])
                    dv(lambda: V.tensor_tensor(out=tmpW, in0=Em(Ei, 8), in1=Bm(bbr, 8), op=ALU.mult), [Ei.tok(), bbr.tok(), st], [ht])
                    dv(lambda: V.tensor_tensor(out=Wi, in0=Wi, in1=tmpW, op=ALU.add), [st, ht], [st])
                    for ri in range(2):
                        Wsrc = Wr if ri == 0 else Wi
                        for mb in range(2):
                            bk = next_bank([0, 1, 2, 3])
                            for mm in range(4):
                                m_ = 4 * mb + mm
                                kb.op(PE, lambda m_=m_, mm=mm: T.transpose(bk[:, 128 * mm:128 * mm + 128], Wsrc[:, m_, :, :].rearrange("p g c -> p (g c)"), identf[:]),
                                      reads=[st, identf.tok()], writes=[bk.tok()], inc=(mm == 3))
                            for e in range(2):
                                kb.op(DVE, lambda e=e: V.tensor_scalar(out=WT[e][:, 4 * mb:4 * mb + 4, ri, :], in0=bk[:].rearrange("p (m c) -> p m c", c=128),
                                                                       scalar1=rowmask[:, e:e + 1], scalar2=None, op0=ALU.mult),
                                      reads=[bk.tok(), rowmask.tok()], writes=[rw])
                    X1 = scr[:, 0, 0:1152].rearrange("p (m g c) -> p m g c", m=9, g=8)
                    X2 = scr[:, 1, 0:1152].rearrange("p (m g c) -> p m g c", m=9, g=8)
                    dv(lambda: V.tensor_tensor(out=X1, in0=Em(Er, 9), in1=Bm(cre, 9), op=ALU.mult), [Er.tok(), cre.tok()], [st])
                    dv(lambda: V.tensor_tensor(out=X2, in0=Em(Ei, 9), in1=Bm(cim, 9), op=ALU.mult), [Ei.tok(), cim.tok()], [st])
                    dv(lambda: V.tensor_tensor(out=Xr[:], in0=X1, in1=X2, op=ALU.subtract), [st], [Xr.tok()])
                    dv(lambda: V.tensor_tensor(out=X1, in0=Em(Ei, 9), in1=Bm(cre, 9), op=ALU.mult), [Ei.tok(), cre.tok(), Xr.tok()], [st])
                    dv(lambda: V.tensor_tensor(out=X2, in0=Em(Er, 9), in1=Bm(cim, 9), op=ALU.mult), [Er.tok(), cim.tok()], [st])
                    dv(lambda: V.tensor_tensor(out=X1, in0=X1, in1=X2, op=ALU.add), [st], [st])
                    dv(lambda: V.tensor_scalar(out=XiN[:], in0=X1, scalar1=-1.0, scalar2=None, op0=ALU.mult), [st], [XiN.tok()])
                    bpt = Tok(ph2.fence)
                    kb.op(POOL, lambda: nc.gpsimd.memset(BbPad, 0.0), reads=[], writes=[bpt])
                    for i in range(8):
                        kb.op(ACT, lambda i=i: S.copy(out=BbPad[:, i, 0, 16 * i:16 * i + 16], in_=bbr[:, i, :]), reads=[bbr.tok()], writes=[bpt])
                        kb.op(ACT, lambda i=i: S.copy(out=BbPad[:, i, 1, 16 * i:16 * i + 16], in_=bbi[:, i, :]), reads=[bbi.tok()], writes=[bpt])
                    for g2 in range(2):
                        for lh in range(2):
                            bk = next_bank([0, 1, 2, 3])
                            bv = bk[:].rearrange("p (m c) -> p m c", c=128)
                            for i in range(8):
                                for lq in range(4):
                                    kb.op(PE, lambda i=i, lq=lq: T.matmul(bv[:, lq, 16 * i:16 * i + 16], BbPad[64 * g2:64 * g2 + 64, i, 0, :], Xr[64 * g2:64 * g2 + 64, 4 * lh + lq, i, :],
                                                                          start=True, stop=False, tile_position=(64 * g2, 0)),
                                          reads=[bpt, Xr.tok()], writes=[bk.tok()], inc=False)
                                    kb.op(PE, lambda i=i, lq=lq: T.matmul(bv[:, lq, 16 * i:16 * i + 16], BbPad[64 * g2:64 * g2 + 64, i, 1, :], XiN[64 * g2:64 * g2 + 64, 4 * lh + lq, i, :],
                                                                          start=False, stop=True, tile_position=(64 * g2, 0)),
                                          reads=[bpt, XiN.tok()], writes=[bk.tok()], inc=(i == 7 and lq == 3))
                            kb.op(ACT, lambda: S.copy(out=Kblk[:, g2, 4 * lh:4 * lh + 4, :], in_=bv), reads=[bk.tok()], writes=[Kblk.tok()])
                    grp = 0
                    for ri in range(2):
                        for e_ in range(2):
                            bset = [banks[4 * (grp % 2) + j_] for j_ in range(4)]
                            grp += 1
                            for g2 in range(2):
                                oc = 2 * g2 + hh
                                for sg in range(8):
                                    for j_ in range(4):
                                        bk = bset[j_]
                                        uv = uz[32 * j_:32 * j_ + 32, oc, :].rearrange("p (k s) -> p k s", s=8)
                                        kb.op(PE, lambda sg=sg, g2=g2, uv=uv, bk=bk, j_=j_: T.matmul(bk[64 * g2:64 * g2 + 64, 0:NCH], WT[e_][32 * j_:32 * j_ + 32, 7 - sg, ri, 64 * g2:64 * g2 + 64],
                                                                                                      uv[:, :, sg], start=(sg == 0), stop=(sg == 7), tile_position=(32 * j_, 64 * g2),
                                                                                                      skip_group_check=True),
                                              reads=[rw] + uz_all, writes=[bk.tok()], inc=(sg == 7 and g2 == 1))
                            for j_ in range(4):
                                i = 2 * j_ + e_
                                bk = bset[j_]
                                kb.op(ACT, lambda: S.copy(out=Harr[:, ri, i, 1:257], in_=bk[:, 0:256]), reads=[bk.tok(), ht], writes=[Harr.tok((ri, i))])
                                kb.op(DVE, lambda: V.tensor_copy(out=Ss[:, ri, i, :], in_=bk[:, 256:260]), reads=[bk.tok()], writes=[Ss.tok()])
                    hall = [Harr.tok((ri, i)) for ri in range(2) for i in range(8)]
                    hsc = Harr.tok("scan")
                    dv(lambda: V.memset(Harr[:, :, :, 0:1], 0.0), hall + [ht], [hsc])
                    kb.op(POOL, lambda: nc.gpsimd.memset(rwb[:, 0:4096], 0.0), reads=[], writes=[rw])
                    for par in range(2):
                        for ri in range(2):
                            Xs = Xr if ri == 0 else XiN
                            for g2 in range(2):
                                kb.op(ACT, lambda g2=g2: S.copy(out=PT[64 * ri:64 * ri + 64, :, g2, par:8:2, 16 * par:16 * par + 16], in_=Xs[64 * g2:64 * g2 + 64, 1:9, par:8:2, :]),
                                      reads=[Xs.tok()], writes=[rw])
                    A8r = Er[:, MIDX[8], :].unsqueeze(2).broadcast_to([128, 8, 16])
                    A8i = Ei[:, MIDX[8], :].unsqueeze(2).broadcast_to([128, 8, 16])
                    et = [Er.tok(), Ei.tok()]

                    def Hv(ri, j):
                        return Harr[:, ri, :, 1 + j:257:16]
                    for j in range(1, 16):
                        dv(lambda: V.tensor_tensor(out=tl1[:], in0=A8r, in1=Hv(0, j - 1), op=ALU.mult), et + [hsc], [tl1.tok()])
                        dv(lambda: V.tensor_tensor(out=tl2[:], in0=A8i, in1=Hv(1, j - 1), op=ALU.mult), et + [hsc], [tl2.tok()])
                        dv(lambda: V.tensor_tensor(out=tl3[:], in0=A8r, in1=Hv(1, j - 1), op=ALU.mult), et + [hsc], [tl3.tok()])
                        dv(lambda: V.tensor_tensor(out=tl4[:], in0=A8i, in1=Hv(0, j - 1), op=ALU.mult), et + [hsc], [tl4.tok()])
                        dv(lambda: V.tensor_tensor(out=tl1[:], in0=tl1[:], in1=tl2[:], op=ALU.subtract), [tl1.tok(), tl2.tok()], [tl1.tok()])
                        dv(lambda: V.tensor_tensor(out=tl3[:], in0=tl3[:], in1=tl4[:], op=ALU.add), [tl3.tok(), tl4.tok()], [tl3.tok()])
                        dv(lambda: V.tensor_tensor(out=Hv(0, j), in0=Hv(0, j), in1=tl1[:], op=ALU.add), [tl1.tok(), hsc], [hsc])
                        dv(lambda: V.tensor_tensor(out=Hv(1, j), in0=Hv(1, j), in1=tl3[:], op=ALU.add), [tl3.tok(), hsc], [hsc])
                    A128r = Er[:, MIDX[128], :]
                    A128i = Ei[:, MIDX[128], :]

                    def Ce(ri, b):
                        return Harr[:, ri, :, 16 * b + 16]
                    for sft in (1, 2, 4, 8):
                        nn = 16 - sft
                        Ar_ = Er[:, MIDX[128 * sft], :].unsqueeze(2).broadcast_to([128, 8, nn])
                        Ai_ = Ei[:, MIDX[128 * sft], :].unsqueeze(2).broadcast_to([128, 8, nn])

                        def Plo(ri):
                            return Harr[:, ri, :, 16:16 + 16 * nn:16]

                        def Phi(ri):
                            return Harr[:, ri, :, 16 + 16 * sft:257:16]
                        dv(lambda: V.tensor_tensor(out=tl1[:, :, 0:nn], in0=Ar_, in1=Plo(0), op=ALU.mult), et + [hsc], [tl1.tok()])
                        dv(lambda: V.tensor_tensor(out=tl2[:, :, 0:nn], in0=Ai_, in1=Plo(1), op=ALU.mult), et + [hsc], [tl2.tok()])
                        dv(lambda: V.tensor_tensor(out=tl3[:, :, 0:nn], in0=Ar_, in1=Plo(1), op=ALU.mult), et + [hsc], [tl3.tok()])
                        dv(lambda: V.tensor_tensor(out=tl4[:, :, 0:nn], in0=Ai_, in1=Plo(0), op=ALU.mult), et + [hsc], [tl4.tok()])
                        dv(lambda: V.tensor_tensor(out=tl1[:, :, 0:nn], in0=tl1[:, :, 0:nn], in1=tl2[:, :, 0:nn], op=ALU.subtract), [tl1.tok(), tl2.tok()], [tl1.tok()])
                        dv(lambda: V.tensor_tensor(out=tl3[:, :, 0:nn], in0=tl3[:, :, 0:nn], in1=tl4[:, :, 0:nn], op=ALU.add), [tl3.tok(), tl4.tok()], [tl3.tok()])
                        dv(lambda: V.tensor_tensor(out=Phi(0), in0=Phi(0), in1=tl1[:, :, 0:nn], op=ALU.add), [tl1.tok(), hsc], [hsc])
                        dv(lambda: V.tensor_tensor(out=Phi(1), in0=Phi(1), in1=tl3[:, :, 0:nn], op=ALU.add), [tl3.tok(), hsc], [hsc])
                    F1 = scr[:, 0, 0:1800].rearrange("p (g b j) -> p g b j", g=8, b=15) if False else None
                    fx = scr[:].rearrange("p a w -> p (a w)")
                    F1 = fx[:, 0:1800].rearrange("p (g b j) -> p g b j", g=8, b=15)

                    def Ep(E_):
                        return E_[:, 8:23, :].rearrange("p j g -> p g j").unsqueeze(2).broadcast_to([128, 8, 15, 15])

                    def Cb(ri):
                        return Harr[:, ri, :, 16:241:16].unsqueeze(3).broadcast_to([128, 8, 15, 15])

                    def Hf(ri):
                        return Harr[:, ri, :, 17:257].rearrange("p g (b j) -> p g b j", j=16)[:, :, :, 0:15]
                    dv(lambda: V.tensor_tensor(out=F1, in0=Ep(Er), in1=Cb(0), op=ALU.mult), et + [hsc], [st])
                    dv(lambda: V.tensor_tensor(out=Hf(0), in0=Hf(0), in1=F1, op=ALU.add), [st, hsc], [hsc])
                    dv(lambda: V.tensor_tensor(out=F1, in0=Ep(Ei), in1=Cb(1), op=ALU.mult), et + [hsc], [st])
                    dv(lambda: V.tensor_tensor(out=Hf(0), in0=Hf(0), in1=F1, op=ALU.subtract), [st, hsc], [hsc])
                    dv(lambda: V.tensor_tensor(out=F1, in0=Ep(Er), in1=Cb(1), op=ALU.mult), et + [hsc], [st])
                    dv(lambda: V.tensor_tensor(out=Hf(1), in0=Hf(1), in1=F1, op=ALU.add), [st, hsc], [hsc])
                    dv(lambda: V.tensor_tensor(out=F1, in0=Ep(Ei), in1=Cb(0), op=ALU.mult), et + [hsc], [st])
                    dv(lambda: V.tensor_tensor(out=Hf(1), in0=Hf(1), in1=F1, op=ALU.add), [st, hsc], [hsc])
                    A8r4 = Er[:, MIDX[8], :].unsqueeze(2).broadcast_to([128, 8, 4])
                    A8i4 = Ei[:, MIDX[8], :].unsqueeze(2).broadcast_to([128, 8, 4])
                    dv(lambda: V.tensor_tensor(out=Hs[:, 0], in0=A8r4, in1=h0[:, 0], op=ALU.mult), et + [h0.tok()], [Hs.tok()])
                    dv(lambda: V.tensor_tensor(out=tS[:, 0], in0=A8i4, in1=h0[:, 1], op=ALU.mult), et + [h0.tok()], [tS.tok()])
                    dv(lambda: V.tensor_tensor(out=Hs[:, 0], in0=Hs[:, 0], in1=tS[:, 0], op=ALU.subtract), [Hs.tok(), tS.tok()], [Hs.tok()])
                    dv(lambda: V.tensor_tensor(out=Hs[:, 1], in0=A8r4, in1=h0[:, 1], op=ALU.mult), et + [h0.tok()], [Hs.tok()])
                    dv(lambda: V.tensor_tensor(out=tS[:, 1], in0=A8i4, in1=h0[:, 0], op=ALU.mult), et + [h0.tok()], [tS.tok()])
                    dv(lambda: V.tensor_tensor(out=Hs[:, 1], in0=Hs[:, 1], in1=tS[:, 1], op=ALU.add), [Hs.tok(), tS.tok()], [Hs.tok()])
                    dv(lambda: V.tensor_tensor(out=Hs[:], in0=Hs[:], in1=Ss[:], op=ALU.add), [Hs.tok(), Ss.tok()], [Hs.tok()])
                    for ri in range(2):
                        kb.dma(SP, o_h[:, l, ri, g0:g0 + 8, 0:4], Hs[:, ri], reads=[Hs.tok()])
                        kb.dma(SP, o_h[:, l, ri, g0:g0 + 8, 4:5], Harr[:, ri, :, 256:257], reads=[hsc], allow_slow_non_contiguous=True)
                    if stage == "s5scan":
                        dump("Harr%d" % hh, Harr[:].rearrange("p r g k -> p (r g k)"), [hsc])
                        dump("Kblk%d" % hh, Kblk[:].rearrange("p a m c -> p (a m c)"), [Kblk.tok()])
                    for ci, (c0, cn) in enumerate(CTS):
                        k0, kn = c0 // 8, cn // 8
                        hb = Hb[ci % 2]
                        for ri in range(2):
                            for g2 in range(2):
                                if ci < 4:
                                    kb.op(ACT, lambda ri=ri, g2=g2: S.copy(out=hb[64 * ri:64 * ri + 64, g2, :, 0:kn], in_=Harr[64 * g2:64 * g2 + 64, ri, :, k0:k0 + kn]),
                                          reads=[hsc], writes=[hb.tok()])
                                else:
                                    kb.op(ACT, lambda ri=ri, g2=g2: S.copy(out=hb[64 * ri:64 * ri + 64, g2, :, 0:4], in_=h0[64 * g2:64 * g2 + 64, ri, :, :]),
                                          reads=[h0.tok()], writes=[hb.tok()])
                        for g2 in range(2):
                            oc = 2 * g2 + hh
                            bk = next_bank([0, 1, 2, 3])
                            kb.op(PE, lambda: T.matmul(bk[:, 0:cn], Kblk[:, g2, 0, :], uz[:, oc, c0:c0 + cn], start=True, stop=False),
                                  reads=[Kblk.tok(), uz.tok((oc, ci))], writes=[bk.tok()], inc=False)
                            uvv = uz[:, oc, c0:c0 + cn].rearrange("p (k s) -> p k s", s=8)
                            bvv = bk[:, 0:cn].rearrange("p (k s) -> p k s", s=8)
                            for ta in range(8):
                                for i in (0, 2, 4, 6, 1, 3, 5, 7):
                                    kb.op(PE, lambda i=i, ta=ta: T.matmul(bvv[32 * (i // 2):32 * (i // 2) + 32, :, ta], PT[:, ta, g2, i, :],
                                                                          hb[:, g2, i, 0:kn], start=False, stop=False,
                                                                          tile_position=(0, 32 * (i // 2))),
                                          reads=[rw, hb.tok()], writes=[bk.tok()], inc=False)
                            for lg in range(1, 8):
                                for ta in range(lg, 8):
                                    kb.op(PE, lambda lg=lg, ta=ta: T.matmul(bvv[:, :, ta], Kblk[:, g2, lg, :], uvv[:, :, ta - lg], start=False, stop=(lg == 7 and ta == 7)),
                                          reads=[Kblk.tok(), uz.tok((oc, ci))], writes=[bk.tok()], inc=(lg == 7 and ta == 7))
                            yt = ytmp[(2 * ci + g2) % 2]
                            dv(lambda: V.scalar_tensor_tensor(out=yt[:, 0:cn], in0=uz[:, oc, c0:c0 + cn], scalar=pv_col(l, "sd", oc), in1=bk[:, 0:cn], op0=ALU.mult, op1=ALU.add),
                               [uz.tok((oc, ci)), bk.tok(), pvec.tok()], [yt.tok()])
                            kb.op(ACT, lambda: S.activation(out=uz[:, oc, c0:c0 + cn], in_=yt[:, 0:cn], func=AF.Gelu_apprx_tanh), reads=[yt.tok()], writes=[uz.tok((oc, ci))])
                if stage == "s5scan":
                    break
                if stage == "s5":
                    dump("zT", uz[:].rearrange("p c t -> p (c t)"), uz_all)
                    break

                ph = Phase(base=UZW)
                ph.bufs.append(uz)
                gw = ph.take([128, 4, 512], BF16)
                sgt = [ph.take([128, 512]) for _ in range(2)]
                sq = [ph.take([128, 512], BF16), ph.take([128, 512], BF16)]
                rt = ph.take([128, 512])
                wo = ph.take([128, 8, 1024], BF16)
                kb.dma(POOL, gw[:], d_glu[l].rearrange("(kc p) n -> p kc n", p=128), writes=[gw.tok()])
                wout_v = d_wout[l].rearrange("(kc p) n -> p kc n", p=128)
                for hhalf in range(2):
                    kb.dma(POOL, wo[:, :, 512 * hhalf:512 * hhalf + 512], wout_v[:, :, 512 * hhalf:512 * hhalf + 512], writes=[wo.tok(hhalf)])
                wgu0 = ph.take([128, 8, 2, 512], BF16)
                wdn0 = ph.take([128, 4, 1024], BF16)
                actb = [ph.take([128, 4, 512], BF16) for _ in range(2)]
                sil = [ph.take([128, 512]) for _ in range(2)]
                ph_mix = ph
                wg_v = d_wg[l].rearrange("(kc p) n -> p kc n", p=128)
                wu_v = d_wu[l].rearrange("(kc p) n -> p kc n", p=128)
                fgs = [(0, 4), (4, 4), (8, 4), (12, 4), (16, 4), (20, 2)]

                def load_ffn_group(gi, wA, wD):
                    f0, fn_ = fgs[gi]
                    kb.dma(POOL, wA[:, :, 0, 0:128 * fn_], wg_v[:, :, 128 * f0:128 * (f0 + fn_)], writes=[wA.tok(0)])
                    kb.dma(POOL, wA[:, :, 1, 0:128 * fn_], wu_v[:, :, 128 * f0:128 * (f0 + fn_)], writes=[wA.tok(1)])
                    kb.dma(POOL, wD[:, 0:fn_, :], d_wd[l, 128 * f0:128 * (f0 + fn_), :].rearrange("(ft p) n -> p ft n", p=128), writes=[wD.tok()])
                load_ffn_group(0, wgu0, wdn0)
                mixS = Buf(xnT.t[:, 0:4, :])
                def stageA(ci):
                    c0, cn = CTS[ci]
                    for oc in range(4):
                        bk = next_bank([0, 1, 2, 3])
                        for kc in range(4):
                            kb.op(PE, lambda kc=kc: T.matmul(bk[:, 0:cn], gw[:, kc, 128 * oc:128 * oc + 128], uz[:, kc, c0:c0 + cn], start=(kc == 0), stop=(kc == 3)),
                                  reads=[gw.tok(), uz.tok((kc, ci))], writes=[bk.tok()], inc=(kc == 3))
                        sg_ = sgt[oc % 2]
                        kb.op(ACT, lambda: S.activation(out=sg_[:, 0:cn], in_=bk[:, 0:cn], func=AF.Sigmoid, bias=pv_col(l, "gb", oc)), reads=[bk.tok(), pvec.tok()], writes=[sg_.tok()])
                        kb.op(DVE, lambda: V.tensor_tensor(out=mixS[:, oc, c0:c0 + cn], in0=uz[:, oc, c0:c0 + cn], in1=sg_[:, 0:cn], op=ALU.mult),
                              reads=[uz.tok((oc, ci)), sg_.tok()], writes=[xnT.tok(ci)])
                    rmsnorm_to_bf16(mixS, xnT.tok(ci), 4, mixS, xnT.tok(ci), lambda c: pv_col(l, "sg", c), 512.0, ci, sq, rt, [4, 5])
                    ma_toks = [mixA.tok((hp_, ci, a_)) for hp_ in range(4) for a_ in range(2)]
                    mat = mixA.tok(("n", ci))
                    kb.op(DVE, lambda: V.tensor_copy(out=rt[:, 0:1], in_=rt[:, 0:1]), reads=ma_toks + [rt.tok()], writes=[mat, rt.tok()])
                    rmsnorm_to_bf16(mixA, mat, 4, mixA, mat, lambda c: pv_col(l, "ag", c), 512.0, ci, sq, rt, [4, 5])

                def stageB(ci):
                    c0, cn = CTS[ci]
                    mat = mixA.tok(("n", ci))
                    for dc in range(8):
                        bk = next_bank([0, 1, 2, 3, 6, 7])
                        for kc in range(8):
                            src = mixA[:, kc, c0:c0 + cn] if kc < 4 else mixS[:, kc - 4, c0:c0 + cn]
                            stok = mat if kc < 4 else xnT.tok(ci)
                            kb.op(PE, lambda kc=kc, src=src: T.matmul(bk[:, 0:cn], wo[:, kc, 128 * dc:128 * dc + 128], src, start=(kc == 0), stop=(kc == 7)),
                                  reads=[wo.tok(dc // 4), stok], writes=[bk.tok()], inc=(kc == 7))
                        kb.op(DVE, lambda: V.tensor_tensor(out=xT[:, dc, c0:c0 + cn], in0=bk[:, 0:cn], in1=xT[:, dc, c0:c0 + cn], op=ALU.add),
                              reads=[bk.tok(), xT.tok(ci)], writes=[xT.tok(ci)])
                def norm2(ci):
                    rmsnorm_to_bf16(xT, xT.tok(ci), 8, xnT, xnT.tok(ci), lambda c: pv_col(l, "n2", c), 1024.0, ci, sq, rt, [4, 5])
                stageA(0)
                for ci in range(1, 5):
                    stageA(ci)
                    stageB(ci - 1)
                stageB(4)
                if stage == "mix":
                    dump("hT", xT[:].rearrange("p c t -> p (c t)"), [xT.tok(c) for c in range(5)])
                    break

                ph = Phase(base=0)
                wgu1 = ph.take([128, 8, 2, 512], BF16)
                wdn1 = ph.take([128, 4, 1024], BF16)
                assert ph.off * 4 <= UZW * 4 + 4 * 512 * 2 + 2 * 512 * 4 - 0, ph.off
                ph.bufs.extend(ph_mix.bufs)
                wgu = [wgu0, wgu1]
                wdn = [wdn0, wdn1]
                for ci in range(5):
                    norm2(ci)
                na = 0
                for gi, (f0, fn_) in enumerate(fgs):
                    wA = wgu[gi % 2]
                    wD = wdn[gi % 2]
                    if gi > 0:
                        load_ffn_group(gi, wA, wD)
                    for ci, (c0, cn) in enumerate(CTS):
                        ab = actb[na % 2]
                        na += 1
                        for ft in range(fn_):
                            bg = next_bank([0, 1, 2, 3])
                            bu = next_bank([0, 1, 2, 3])
                            for kc in range(8):
                                kb.op(PE, lambda kc=kc: T.matmul(bg[:, 0:cn], wA[:, kc, 0, 128 * ft:128 * ft + 128], xnT[:, kc, c0:c0 + cn], start=(kc == 0), stop=(kc == 7)),
                                      reads=[wA.tok(0), xnT.tok(ci)], writes=[bg.tok()], inc=(kc == 7))
                            for kc in range(8):
                                kb.op(PE, lambda kc=kc: T.matmul(bu[:, 0:cn], wA[:, kc, 1, 128 * ft:128 * ft + 128], xnT[:, kc, c0:c0 + cn], start=(kc == 0), stop=(kc == 7)),
                                      reads=[wA.tok(1), xnT.tok(ci)], writes=[bu.tok()], inc=(kc == 7))
                            sl = sil[ft % 2]
                            kb.op(ACT, lambda: S.activation(out=sl[:, 0:cn], in_=bg[:, 0:cn], func=AF.Silu), reads=[bg.tok()], writes=[sl.tok()])
                            kb.op(DVE, lambda: V.tensor_tensor(out=ab[:, ft, 0:cn], in0=bu[:, 0:cn], in1=sl[:, 0:cn], op=ALU.mult),
                                  reads=[bu.tok(), sl.tok()], writes=[ab.tok(ft)])
                        for dc in range(8):
                            bk = next_bank([4, 5, 6, 7])
                            for ft in range(fn_):
                                kb.op(PE, lambda ft=ft: T.matmul(bk[:, 0:cn], wD[:, ft, 128 * dc:128 * dc + 128], ab[:, ft, 0:cn], start=(ft == 0), stop=(ft == fn_ - 1)),
                                      reads=[wD.tok(), ab.tok(ft)], writes=[bk.tok()], inc=(ft == fn_ - 1))
                            kb.op(DVE, lambda: V.tensor_tensor(out=xT[:, dc, c0:c0 + cn], in0=bk[:, 0:cn], in1=xT[:, dc, c0:c0 + cn], op=ALU.add),
                                  reads=[bk.tok(), xT.tok(ci)], writes=[xT.tok(ci)])
                        if gi == len(fgs) - 1 and l + 1 < nlayers and stage == "full":
                            rmsnorm_to_bf16(xT, xT.tok(ci), 8, xnT, xnT.tok(ci), lambda c: pv_col(l + 1, "n1", c), 1024.0, ci, sq, rt, [4, 5, 6, 7])
                if stage == "layer":
                    dump("yT1", xT[:].rearrange("p c t -> p (c t)"), [xT.tok(c) for c in range(5)])
                    break

            if stage == "full":
                ph = Phase()
                sq = [ph.take([128, 512], BF16), ph.take([128, 512], BF16)]
                rt = ph.take([128, 512])
                yst = [ph.take([128, 8, 512]) for _ in range(2)]
                for ci, (c0, cn) in enumerate(CTS):
                    ys = yst[ci % 2]
                    ysv = Buf(ys.t[:, :, 0:cn])

                    class _Shift:
                        def __getitem__(self, idx):
                            p, c, cols = idx
                            return ys.t[p, c, cols.start - c0:cols.stop - c0]
                    rmsnorm_to_bf16(xT, xT.tok(ci), 8, _Shift(), ys.tok(), lambda c: pvec[:, 128 + c:129 + c], 1024.0, ci, sq, rt, [4, 5])
                    kb.dma(SP, o_y[:, :, c0:c0 + cn], ys[:, :, 0:cn], reads=[ys.tok()])

        except _Stop:
            pass
        kb.finish()
    return nc


def _consts():
    c = {}
    c["c_ident"] = np.eye(128, dtype=np.float32)
    k = np.arange(128)[:, None]
    q = np.arange(128)[None, :]
    cur = np.where(q >= k, 0.0, MASKNEG).astype(np.float32)
    prev = np.where(k >= q, 0.0, MASKNEG).astype(np.float32)
    c["c_maskcur"] = np.tile(cur, (1, 4))
    c["c_maskprev"] = np.tile(prev, (1, 4))
    c["c_smask"] = _sample_mult().reshape(128, 176)
    mv = np.zeros((128, NMV, 16), np.float32)
    mv[:, :, :] = np.array(MVALS, np.float32)[None, :, None]
    c["c_mv"] = mv.reshape(128, NMV * 16)
    rm = np.zeros((128, 2), np.float32)
    par = (np.arange(128) // 16) % 2
    rm[:, 0] = (par == 0)
    rm[:, 1] = (par == 1)
    c["c_rowmask"] = rm
    return c


def _vecT(v):
    L_, F_ = v.shape
    return v.reshape(L_, F_ // 128, 128).transpose(2, 0, 1)


def prep_core(inp, core, consts):
    m = dict(consts)
    xp = inp["x_prompt"][core]
    xs = inp["x_sample"][4 * core:4 * core + 4].reshape(32, 1024)
    x = np.concatenate([xp, xs], axis=0)
    m["xT"] = np.ascontiguousarray(x.reshape(NT, 8, 128).transpose(2, 1, 0))
    ck = inp["cache_attn_k"][:, 4 * core:4 * core + 4]
    cv = inp["cache_attn_v"][:, 4 * core:4 * core + 4]
    ckg = ck[:, :, SPOS].reshape(NL, 4, 1280, 4, 128)
    m["ckT"] = np.ascontiguousarray(ckg.transpose(0, 1, 3, 4, 2).reshape(NL, 16, 128, 1280))
    cvg = cv[:, :, SPOS].reshape(NL, 4, 10, 128, 4, 128)
    m["cvN"] = np.ascontiguousarray(cvg.transpose(0, 1, 4, 3, 2, 5).reshape(NL, 16, 128, 1280))
    pv = np.zeros((128, 136), np.float32)
    pvl = pv[:, :128].reshape(128, NL, 32)
    pvl[:, :, 0:8] = _vecT(inp["norm1_g"])
    pvl[:, :, 8:16] = _vecT(inp["norm2_g"])
    pvl[:, :, 16:20] = _vecT(inp["attn_out_g"])
    pvl[:, :, 20:24] = _vecT(inp["ssm_out_g"])
    pvl[:, :, 24:28] = _vecT(inp["ssm_glu_b"])
    pvl[:, :, 28:32] = _vecT(inp["ssm_d"].reshape(NL, 512))
    pv[:, 128:136] = inp["final_norm_g"].reshape(8, 128).T
    m["pvec"] = pv

    def gn(a):
        return np.ascontiguousarray(a.reshape(NL, 2, 16, 64).transpose(1, 3, 0, 2).reshape(128, NL, 16))
    m["aL_re"] = gn(inp["ssm_a_re"])
    m["aL_im"] = gn(inp["ssm_a_im"])
    m["aL_ldt"] = gn(np.broadcast_to(inp["ssm_log_dt"][:, :, None], (NL, 32, 64)))
    m["bL_re"] = np.ascontiguousarray(inp["ssm_b_re"].reshape(NL, 2, 16, 64, 16).transpose(1, 3, 0, 2, 4).reshape(128, NL, 256))
    m["bL_im"] = np.ascontiguousarray(inp["ssm_b_im"].reshape(NL, 2, 16, 64, 16).transpose(1, 3, 0, 2, 4).reshape(128, NL, 256))
    m["cL_re"] = np.ascontiguousarray(inp["ssm_c_re"].reshape(NL, 2, 16, 16, 64).transpose(1, 4, 0, 2, 3).reshape(128, NL, 256))
    m["cL_im"] = np.ascontiguousarray(inp["ssm_c_im"].reshape(NL, 2, 16, 16, 64).transpose(1, 4, 0, 2, 3).reshape(128, NL, 256))
    h0 = np.stack([inp["state_ssm_re"][:, 4 * core:4 * core + 4], inp["state_ssm_im"][:, 4 * core:4 * core + 4]], axis=0)
    m["h0L"] = np.ascontiguousarray(h0.reshape(2, NL, 4, 2, 16, 64).transpose(3, 5, 1, 0, 4, 2).reshape(128, NL, 2, 16, 4))
    m["w_in"] = inp["w_in"]
    m["w_out"] = inp["w_out"]
    m["glu_w"] = inp["ssm_glu_w"]
    m["wg"] = inp["ffn_w_gate"]
    m["wu"] = inp["ffn_w_up"]
    m["wd"] = inp["ffn_w_down"]
    return m


_CACHE = {}


def kernel(**inputs):
    inp = {k: np.asarray(v) for k, v in inputs.items()}
    if "nc" not in _CACHE:
        _CACHE["nc"] = build({})
    nc = _CACHE["nc"]
    consts = _consts()
    in_maps = [prep_core(inp, c, consts) for c in range(8)]
    res = run_bass_kernel_spmd(nc, in_maps, core_ids=list(range(8)))
    R = res.results
    y_p = np.zeros((8, NP, 1024), np.float32)
    y_s = np.zeros((32, 8, 1024), np.float32)
    pk = np.zeros((NL, 8, NP, 8, 64), np.float32)
    pvv = np.zeros((NL, 8, NP, 8, 64), np.float32)
    pr = np.zeros((NL, 8, 32, 64), np.float32)
    pi = np.zeros((NL, 8, 32, 64), np.float32)
    sk = np.zeros((NL, 32, 8, 8, 64), np.float32)
    sv = np.zeros((NL, 32, 8, 8, 64), np.float32)
    sr = np.zeros((NL, 32, 32, 64), np.float32)
    si = np.zeros((NL, 32, 32, 64), np.float32)
    for c in range(8):
        r = R[c]
        y = r["yT"].transpose(2, 1, 0).reshape(NT, 1024)
        y_p[c] = y[:NP]
        y_s[4 * c:4 * c + 4] = y[NP:].reshape(4, 8, 1024)
        kT = r["pkT"]
        kk = kT.transpose(0, 3, 2, 1).reshape(NL, NT, 512)
        pk[:, c] = kk[:, :NP].reshape(NL, NP, 8, 64)
        sk[:, 4 * c:4 * c + 4] = kk[:, NP:].reshape(NL, 4, 8, 8, 64)
        pvv[:, c] = r["pv"].reshape(NL, NP, 8, 64)
        sv[:, 4 * c:4 * c + 4] = r["sv"].reshape(NL, 4, 8, 8, 64)
        h = r["hout"].reshape(2, 64, NL, 2, 16, 5)
        hh = h.transpose(2, 3, 5, 0, 4, 1).reshape(NL, 2, 5, 32, 64)
        sr[:, 4 * c:4 * c + 4] = hh[:, 0, 0:4]
        si[:, 4 * c:4 * c + 4] = hh[:, 1, 0:4]
        pr[:, c] = hh[:, 0, 4]
        pi[:, c] = hh[:, 1, 4]
    return (y_p, y_s, pk, pvv, pr, pi, sk, sv, sr, si)
```

```python
import numpy as np
import concourse.bass as bass
import concourse.mybir as mybir
from concourse.bass_utils import run_bass_kernel_spmd
from contextlib import ExitStack

F32 = mybir.dt.float32
BF16 = mybir.dt.bfloat16
AF = mybir.ActivationFunctionType
ALU = mybir.AluOpType

NL = 4
NP, NSM, NT = 2048, 32, 2080
NCH = NT // 8
CTS = [(0, 512), (512, 512), (1024, 512), (1536, 512), (2048, 32)]
DFF = 2816
NFT = DFF // 128
EPS = 1e-6
MVALS = list(range(9)) + list(range(16, 129, 8)) + [256, 512, 1024]
NMV = len(MVALS)
MIDX = {m: i for i, m in enumerate(MVALS)}
MASKNEG = -30000.0
TWO_PI = 2.0 * np.pi
CW1 = 6.28125
CW2 = TWO_PI - CW1
MAGIC = 12582912.0
ARENA = 84 * 1024


def _sample_positions():
    pos = []
    for f in range(6):
        for p in range(128):
            blk, rr = p // 8, p % 8
            pos.append(16 * (16 * f + blk) + rr)
    for nt in range(4):
        for p in range(128):
            pos.append(1536 + 128 * nt + p)
    return np.array(pos, dtype=np.int64)


SPOS = _sample_positions()


def _sample_mult():
    m = np.zeros((128, 11, 2, 8), np.float32)
    branches = ((128, 1), (512, 4), (2048, 16))
    for ti in range(11):
        for p in range(128):
            if ti < 10:
                pos = SPOS[128 * ti + p]
            else:
                if p >= 8:
                    continue
                pos = 2048 + p
            for t in range(8):
                dlt = 2048 + t - pos
                c = 0
                for (w, d) in branches:
                    if dlt >= 0 and dlt % d == 0 and dlt <= w:
                        c += 1
                m[p, ti, :, t] = c
    return m


class Tok:
    __slots__ = ("w", "r", "serial")

    def __init__(self, fence=None):
        self.w = None
        self.r = dict(fence) if fence else {}
        self.serial = False


class Sem:
    __slots__ = ("h", "idx", "val")

    def __init__(self, h, idx):
        self.h = h
        self.idx = idx
        self.val = 0


class Eng:
    def __init__(self, name, h, sem, kind):
        self.name = name
        self.h = h
        self.sem = sem
        self.kind = kind
        self.n = 0
        self.seen = {}


class KB:
    def __init__(self, nc, es):
        self.nc = nc
        self.es = es
        self.nsem = 0
        self.pe = Eng("pe", nc.tensor, self.new_sem("s_pe"), "pe")
        self.act = Eng("act", nc.scalar, self.new_sem("s_act"), "act")
        self.dve = Eng("dve", nc.vector, self.new_sem("s_dve"), "dve")
        self.pool = Eng("pool", nc.gpsimd, self.new_sem("s_pool"), "pool")
        self.sp = Eng("sp", nc.sync, self.new_sem("s_sp"), "sp")
        self.dslots = {}
        self.di = {}
        for q in (self.sp, self.pool):
            self.dslots[q.name] = [self.new_sem("d_%s%d" % (q.name, i)) for i in range(24)]
            self.di[q.name] = 0
        self.ninstr = 0

    def new_sem(self, name):
        h = self.es.enter_context(self.nc.semaphore(name))
        s = Sem(h, self.nsem)
        self.nsem += 1
        return s

    def _wait(self, eng, ev):
        sem, val, src = ev
        if eng.seen.get(sem.idx, 0) >= val:
            return
        eng.h.wait_ge(sem.h, val)
        eng.seen[sem.idx] = val

    def _deps(self, eng, reads, writes, is_dma):
        for t in reads:
            if t.w is not None:
                src = t.w[2]
                if (not is_dma) and src is eng and eng.kind == "pe":
                    pass
                else:
                    self._wait(eng, t.w)
            if getattr(t, "serial", False):
                for e in t.r.values():
                    if e[2] is not eng:
                        self._wait(eng, e)
        for t in writes:
            if t.w is not None:
                if is_dma or t.w[2] is not eng or eng.kind != "pe":
                    self._wait(eng, t.w)
            for e in t.r.values():
                if is_dma or e[2] is not eng or eng.kind != "pe":
                    self._wait(eng, e)

    def op(self, eng, fn, reads=(), writes=(), inc=True):
        self._deps(eng, reads, writes, False)
        ins = fn()
        self.ninstr += 1
        if inc:
            eng.n += 1
            ins.then_inc(eng.sem.h, 1)
            ev = (eng.sem, eng.n, eng)
        else:
            ev = (eng.sem, eng.n + 1, eng)
        for t in writes:
            t.w = ev
            t.r = {}
        for t in reads:
            t.r[eng.name] = ev
        return ins

    def dma(self, q, out, in_, reads=(), writes=(), **kw):
        self._deps(q, reads, writes, True)
        slots = self.dslots[q.name]
        s = slots[self.di[q.name] % len(slots)]
        self.di[q.name] += 1
        if s.val > 0:
            self._wait(q, (s, s.val, None))
        ins = q.h.dma_start(out=out, in_=in_, **kw)
        s.val += 16
        ins.then_inc(s.h, 16)
        self.ninstr += 1
        ev = (s, s.val, None)
        for t in writes:
            t.w = ev
            t.r = {}
        for t in reads:
            t.r["dma%d" % s.idx] = ev
        return ins

    def finish(self):
        for q in (self.sp, self.pool):
            for s in self.dslots[q.name]:
                if s.val > 0:
                    self._wait(self.sp, (s, s.val, None))
        for e in (self.pe, self.act, self.dve, self.pool):
            if e.n > 0:
                self._wait(self.sp, (e.sem, e.n, e))


class _Stop(Exception):
    pass


class Buf:
    def __init__(self, t, fence=None):
        self.t = t
        self.toks = {}
        self.fence = fence

    def tok(self, key=0):
        if key not in self.toks:
            self.toks[key] = Tok(self.fence)
        return self.toks[key]

    def __getitem__(self, idx):
        return self.t[idx]


def build(cfg):
    nlayers = cfg.get("nlayers", NL)
    stage = cfg.get("stage", "full")
    dbg = cfg.get("dbg", {})
    nc = bass.Bass("TRN2", target_bir_lowering=False)

    def din(name, shape):
        return nc.dram_tensor(name, list(shape), F32, kind="ExternalInput").ap()

    def dout(name, shape):
        return nc.dram_tensor(name, list(shape), F32, kind="ExternalOutput").ap()

    d_x = din("xT", [128, 8, NT])
    d_ck = din("ckT", [NL, 16, 128, 1280])
    d_cv = din("cvN", [NL, 16, 128, 1280])
    d_pvec = din("pvec", [128, 136])
    d_are = din("aL_re", [128, NL, 16])
    d_aim = din("aL_im", [128, NL, 16])
    d_ldt = din("aL_ldt", [128, NL, 16])
    d_bre = din("bL_re", [128, NL, 256])
    d_bim = din("bL_im", [128, NL, 256])
    d_cre = din("cL_re", [128, NL, 256])
    d_cim = din("cL_im", [128, NL, 256])
    d_h0 = din("h0L", [128, NL, 2, 16, 4])
    d_win = din("w_in", [NL, 1024, 2048])
    d_wout = din("w_out", [NL, 1024, 1024])
    d_glu = din("glu_w", [NL, 512, 512])
    d_wg = din("wg", [NL, 1024, DFF])
    d_wu = din("wu", [NL, 1024, DFF])
    d_wd = din("wd", [NL, DFF, 1024])
    d_cid = din("c_ident", [128, 128])
    d_cmc = din("c_maskcur", [128, 512])
    d_cmp = din("c_maskprev", [128, 512])
    d_csm = din("c_smask", [128, 176])
    d_cmv = din("c_mv", [128, NMV * 16])
    d_crm = din("c_rowmask", [128, 2])

    o_y = dout("yT", [128, 8, NT])
    o_k = dout("pkT", [NL, 128, 4, NT])
    o_v = dout("pv", [NL, NP, 512])
    o_sv = dout("sv", [NL, NSM, 512])
    o_h = dout("hout", [128, NL, 2, 16, 5])
    dbg_out = {name: dout("dbg_" + name, shape) for name, shape in dbg.items()}

    es = ExitStack()
    with es:
        kb = KB(nc, es)
        PE, ACT, DVE, POOL, SP = kb.pe, kb.act, kb.dve, kb.pool, kb.sp
        V, S, T = nc.vector, nc.scalar, nc.tensor

        def sb(name, shape, dt=F32):
            return Buf(es.enter_context(nc.sbuf_tensor("s_" + name, list(shape), dt)))

        banks = [Buf(es.enter_context(nc.psum_tensor("bank%d" % i, [128, 512], F32))) for i in range(8)]
        for b_ in banks:
            b_.tok().serial = True

        xT = sb("xT", [128, 8, NT])
        pvec = sb("pvec", [128, 136])
        ident = sb("ident", [128, 128], BF16)
        identf = sb("identf", [128, 128])
        ones = sb("ones", [128, 128], BF16)
        maskcur = sb("maskcur", [128, 512], BF16)
        maskprev = sb("maskprev", [128, 512], BF16)
        smask = sb("smask", [128, 176], BF16)
        mvt = sb("mvt", [128, NMV, 16])
        rowmask = sb("rowmask", [128, 2])
        xnT = sb("xnT", [128, 8, NT], BF16)
        mixA = sb("mixA", [128, 4, NT], BF16)
        arena = es.enter_context(nc.sbuf_tensor("arena", [128, ARENA // 4], F32))

        class Phase:
            prev_bufs = []

            def __init__(self, base=0):
                fence = {}
                for b in Phase.prev_bufs:
                    for t in b.toks.values():
                        evs = list(t.r.values())
                        if t.w is not None:
                            evs.append(t.w)
                        for ev in evs:
                            k = ev[0].idx
                            if k not in fence or fence[k][1] < ev[1]:
                                fence[k] = ev
                self.fence = {("f", k): v for k, v in fence.items()}
                self.off = base
                self.bufs = []
                Phase.prev_bufs = self.bufs

            def take(self, shape, dt=F32):
                fshape = list(shape[1:])
                n = 1
                for s_ in fshape:
                    n *= s_
                nbytes = n * (4 if dt == F32 else 2)
                nwords = (nbytes + 3) // 4
                ap = arena[:, self.off:self.off + nwords]
                self.off += nwords
                assert self.off * 4 <= ARENA, ("arena overflow", self.off * 4)
                Phase.maxoff = max(getattr(Phase, "maxoff", 0), self.off * 4)
                if dt != F32:
                    ap = ap.bitcast(dt)[:, 0:n]
                if len(fshape) > 1:
                    names = "abcdefgh"[:len(fshape)]
                    pat = "p (" + " ".join(names) + ") -> p " + " ".join(names)
                    ap = ap.rearrange(pat, **{names[i]: fshape[i] for i in range(len(fshape))})
                b = Buf(ap, self.fence)
                self.bufs.append(b)
                return b

        for ci, (c0, cn) in enumerate(CTS):
            kb.dma(SP, xT[:, :, c0:c0 + cn], d_x[:, :, c0:c0 + cn], writes=[xT.tok(ci)])
        kb.dma(SP, pvec[:], d_pvec[:, :], writes=[pvec.tok()])
        kb.dma(SP, mvt[:], d_cmv.rearrange("p (m g) -> p m g", g=16), writes=[mvt.tok()])
        kb.dma(SP, rowmask[:], d_crm[:, :], writes=[rowmask.tok()])
        kb.dma(POOL, ident[:], d_cid[:, :], writes=[ident.tok()])
        kb.dma(SP, identf[:], d_cid[:, :], writes=[identf.tok()])
        kb.dma(POOL, maskcur[:], d_cmc[:, :], writes=[maskcur.tok()])
        kb.dma(POOL, maskprev[:], d_cmp[:, :], writes=[maskprev.tok()])
        kb.dma(POOL, smask[:], d_csm[:, :], writes=[smask.tok()])
        kb.op(DVE, lambda: V.memset(ones[:], 1.0), writes=[ones.tok()])

        def pv_col(l, which, c):
            base = {"n1": 0, "n2": 8, "ag": 16, "sg": 20, "gb": 24, "sd": 28}[which]
            o = 32 * l + base + c
            return pvec[:, o:o + 1]

        rr = {"b": 0}

        def next_bank(group):
            i = group[rr["b"] % len(group)]
            rr["b"] += 1
            return banks[i]

        def cut(n):
            if cfg.get("cut") == n:
                raise _Stop()

        def dump(name, ap, reads):
            if name in dbg_out:
                kb.dma(POOL, dbg_out[name], ap, reads=reads, max_dma_last_dim=2048)

        def rmsnorm_to_bf16(src, src_tok, nchunk, dst, dst_tok, gcol, width, ci, sq, rt, bank_group):
            c0, cn = CTS[ci]
            bk = next_bank(bank_group)
            for c in range(nchunk):
                sqb = sq[c % 2]
                kb.op(ACT, lambda c=c, sqb=sqb: S.activation(out=sqb[:, 0:cn], in_=src[:, c, c0:c0 + cn], func=AF.Square),
                      reads=[src_tok], writes=[sqb.tok()])
                kb.op(PE, lambda c=c, sqb=sqb: T.matmul(bk[:, 0:cn], ones[:], sqb[:, 0:cn], start=(c == 0), stop=(c == nchunk - 1)),
                      reads=[ones.tok(), sqb.tok()], writes=[bk.tok()], inc=True)
            kb.op(ACT, lambda: S.activation(out=rt[:, 0:cn], in_=bk[:, 0:cn], func=AF.Sqrt, scale=1.0 / width, bias=EPS),
                  reads=[bk.tok()], writes=[rt.tok()])
            kb.op(DVE, lambda: V.reciprocal(out=rt[:, 0:cn], in_=rt[:, 0:cn]), reads=[rt.tok()], writes=[rt.tok()])
            for c in range(nchunk):
                kb.op(DVE, lambda c=c: V.scalar_tensor_tensor(out=dst[:, c, c0:c0 + cn], in0=src[:, c, c0:c0 + cn], scalar=gcol(c),
                                                              in1=rt[:, 0:cn], op0=ALU.mult, op1=ALU.mult),
                      reads=[src_tok, rt.tok(), pvec.tok()], writes=[dst_tok])

        if stage == "load":
            dump("x0", xT[:, 0, :], [xT.tok(c) for c in range(5)])
            nlayers = 0
        try:
            for l in range(nlayers):
                ph = Phase()
                sq = [ph.take([128, 512], BF16), ph.take([128, 512], BF16)]
                rt = ph.take([128, 512])
                qz = ph.take([128, 2, NT], BF16)
                kTp = ph.take([128, NT], BF16)
                vaug = ph.take([128, 3, 16, 192], BF16)
                vS = ph.take([128, 4, 128], BF16)
                wsl = [ph.take([128, 8, 3, 128], BF16), ph.take([128, 8, 3, 128], BF16)]
                pT = [ph.take([128, 512], BF16) for _ in range(3)]
                rcp = [ph.take([128, 512]) for _ in range(1)]
                kst = [ph.take([128, 512]) for _ in range(2)]
                vst = [ph.take([128, 4, 128]) for _ in range(1)]
                svst = ph.take([128, 128])
                kcs = [ph.take([128, 1280], BF16) for _ in range(4)]
                vcs = [ph.take([128, 10, 128], BF16) for _ in range(4)]
                pS = ph.take([128, 176], BF16)
                pSn = ph.take([128, 16], BF16)
                rS = ph.take([128, 16])

                for ci in range(5):
                    rmsnorm_to_bf16(xT, xT.tok(ci), 8, xnT, xnT.tok(ci), lambda c: pv_col(l, "n1", c), 1024.0, ci, sq, rt, [6, 7])
                if stage == "norm":
                    dump("xnT", xnT[:, :, :], [xnT.tok(c) for c in range(5)])
                    break
                kb.op(POOL, lambda: nc.gpsimd.memset(qz[:], 0.0), writes=[qz.tok(("z", 0)), qz.tok(("z", 1))])
                kb.op(POOL, lambda: nc.gpsimd.memset(vaug[:, :, :, 64:128], 1.0), writes=[vaug.tok("ones")])
                kb.op(POOL, lambda: nc.gpsimd.memset(vS[:], 0.0), writes=[vS.tok()])
                kb.op(POOL, lambda: nc.gpsimd.memset(pSn[:], 0.0), writes=[pSn.tok()])

                cut(1)
                win_v = d_win[l].rearrange("(kc p) n -> p kc n", p=128)
                IPB = [6, 7, 0, 1, 2, 3, 4, 5]
                def load_pair_w(hp_):
                    w_ = wsl[hp_ % 2]
                    for j in range(3):
                        kb.dma(POOL, w_[:, :, j, :], win_v[:, :, 512 * j + 128 * hp_: 512 * j + 128 * hp_ + 128], writes=[w_.tok(j)])
                load_pair_w(0)
                load_pair_w(1)
                for hp in range(4):
                    w = wsl[hp % 2]
                    cut(21)
                    for ci, (c0, cn) in enumerate(CTS):
                        bk = next_bank(IPB)
                        for kc in range(8):
                            kb.op(PE, lambda kc=kc: T.matmul(bk[:, 0:cn], w[:, kc, 0, :], xnT[:, kc, c0:c0 + cn], start=(kc == 0), stop=(kc == 7)),
                                  reads=[w.tok(0), xnT.tok(ci)], writes=[bk.tok()], inc=(kc == 7))
                        if ci == 1:
                            cut(31)
                        kb.op(ACT, lambda: S.copy(out=qz[0:64, 0, c0:c0 + cn], in_=bk[0:64, 0:cn]),
                              reads=[bk.tok(), qz.tok(("z", 0))], writes=[qz.tok((0, ci))])
                        if ci == 1:
                            cut(32)
                        kb.op(DVE, lambda: V.tensor_copy(out=qz[64:128, 1, c0:c0 + cn], in_=bk[64:128, 0:cn]),
                              reads=[bk.tok(), qz.tok(("z", 1))], writes=[qz.tok((1, ci))])
                        if ci == 1:
                            cut(33)
                        cut(22)
                        bk = next_bank(IPB)
                        for kc in range(8):
                            kb.op(PE, lambda kc=kc: T.matmul(bk[:, 0:cn], w[:, kc, 1, :], xnT[:, kc, c0:c0 + cn], start=(kc == 0), stop=(kc == 7)),
                                  reads=[w.tok(1), xnT.tok(ci)], writes=[bk.tok()], inc=(kc == 7))
                        ks = kst[ci % 2]
                        if ci == 1:
                            cut(34)
                        kb.op(ACT, lambda: S.copy(out=kTp[:, c0:c0 + cn], in_=bk[:, 0:cn]), reads=[bk.tok()], writes=[kTp.tok(ci)])
                        if ci == 1:
                            cut(35)
                        if cfg.get("kcopy", "dve") == "dve":
                            kb.op(DVE, lambda: V.tensor_copy(out=ks[:, 0:cn], in_=bk[:, 0:cn]), reads=[bk.tok()], writes=[ks.tok()])
                        elif cfg.get("kcopy") == "act":
                            kb.op(ACT, lambda: S.copy(out=ks[:, 0:cn], in_=bk[:, 0:cn]), reads=[bk.tok()], writes=[ks.tok()])
                        cut(23)
                        if ci == 1:
                            cut(36)
                        kb.dma(SP, o_k[l, :, hp, c0:c0 + cn], ks[:, 0:cn], reads=[ks.tok()])
                        cut(24)
                        if ci == 1:
                            cut(25)
                        if ci == 3:
                            cut(26)
                    cut(2)
                    def tok_ap(o, ti, kc):
                        if o == 0:
                            return xnT[:, kc, 128 * ti:128 * ti + 128]
                        if o == 1:
                            G, r = ti // 4, ti % 4
                            return xnT[:, kc, 512 * G + r:512 * G + 512:4]
                        return xnT[:, kc, ti:2048:16]
                    for o in range(3):
                        cut(3 + o)
                        for tb in range(4):
                            bk = next_bank(IPB)
                            for j in range(4):
                                ti = 4 * tb + j
                                for kc in range(8):
                                    kb.op(PE, lambda kc=kc, ti=ti, j=j: T.matmul(bk[:, 128 * j:128 * j + 128], tok_ap(o, ti, kc), w[:, kc, 2, :],
                                                                                  start=(kc == 0), stop=(kc == 7)),
                                          reads=[w.tok(2)] + [xnT.tok(c) for c in range(4)], writes=[bk.tok()], inc=(kc == 7 and j == 3))
                            bv = bk[:].rearrange("p (j c) -> p j c", c=128)
                            kb.op(ACT, lambda: S.copy(out=vaug[:, o, 4 * tb:4 * tb + 4, 0:64], in_=bv[:, :, 0:64]),
                                  reads=[bk.tok()], writes=[vaug.tok((o, tb, 0))])
                            kb.op(DVE, lambda: V.tensor_copy(out=vaug[:, o, 4 * tb:4 * tb + 4, 128:192], in_=bv[:, :, 64:128]),
                                  reads=[bk.tok()], writes=[vaug.tok((o, tb, 1))])
                            if o == 0:
                                vs_ = vst[0]
                                kb.op(DVE, lambda: V.tensor_copy(out=vs_[:], in_=bv), reads=[bk.tok()], writes=[vs_.tok()])
                                kb.dma(SP, o_v[l, 512 * tb:512 * tb + 512, 128 * hp:128 * hp + 128].rearrange("(j p) c -> p j c", p=128), vs_[:],
                                       reads=[vs_.tok()])
                    cut(6)
                    bk = next_bank([6, 7])
                    for kc in range(8):
                        kb.op(PE, lambda kc=kc: T.matmul(bk[0:32, 0:128], xnT[:, kc, 2048:2080], w[:, kc, 2, :], start=(kc == 0), stop=(kc == 7)),
                              reads=[w.tok(2), xnT.tok(4)], writes=[bk.tok()], inc=(kc == 7))
                    kb.op(DVE, lambda: V.tensor_copy(out=svst[0:32, :], in_=bk[0:32, 0:128]), reads=[bk.tok()], writes=[svst.tok()])
                    kb.dma(SP, o_sv[l, :, 128 * hp:128 * hp + 128], svst[0:32, :], reads=[svst.tok()])
                    bk = next_bank([6, 7])
                    for b in range(4):
                        for kc in range(8):
                            kb.op(PE, lambda kc=kc, b=b: T.matmul(bk[0:8, 128 * b:128 * b + 128], xnT[:, kc, 2048 + 8 * b:2056 + 8 * b], w[:, kc, 2, :],
                                                                  start=(kc == 0), stop=(kc == 7)),
                                  reads=[w.tok(2), xnT.tok(4)], writes=[bk.tok()], inc=(kc == 7 and b == 3))
                    kb.op(DVE, lambda: V.tensor_copy(out=vS[0:8, :, :], in_=bk[0:8, :].rearrange("p (b c) -> p b c", c=128)),
                          reads=[bk.tok()], writes=[vS.tok()])

                    cut(7)
                    if hp + 2 < 4:
                        load_pair_w(hp + 2)
                    if stage == "inproj" and hp == 0:
                        dump("qz", qz[:], [qz.tok((0, c)) for c in range(5)] + [qz.tok((1, c)) for c in range(5)])
                        dump("kTp", kTp[:], [kTp.tok(c) for c in range(5)])
                        dump("vaug", vaug[:].rearrange("p o t c -> p (o t c)"), [vaug.tok((o, tb, a)) for o in range(3) for tb in range(4) for a in range(2)] + [vaug.tok("ones")])
                        dump("vS", vS[:].rearrange("p b c -> p (b c)"), [vS.tok()])
                        break

                    qz_reads = lambda a: [qz.tok((a, c)) for c in range(4)] + [qz.tok(("z", 1 - a))]
                    kT_reads = [kTp.tok(c) for c in range(4)]
                    for a in range(2):
                        tiles = []
                        for c in range(16):
                            tiles.append(("cur", slice(128 * c, 128 * c + 128), slice(128 * c, 128 * c + 128), (0, c),
                                          [(c // 4, slice(128 * (c % 4), 128 * (c % 4) + 128), slice(0, 128))]))
                        for r in range(4):
                            for j in range(4):
                                s_ = slice(512 * j + r, 512 * j + 512, 4)
                                tiles.append(("cur", s_, s_, (1, 4 * j + r), [(j, slice(r, 512, 4), slice(0, 128))]))
                        for r in range(16):
                            s_ = slice(r, 2048, 16)
                            tiles.append(("cur", s_, s_, (2, r), [(G, slice(r, 512, 16), slice(32 * G, 32 * G + 32)) for G in range(4)]))
                        for c in range(1, 16):
                            tiles.append(("prev", slice(128 * (c - 1), 128 * c), slice(128 * c, 128 * c + 128), (0, c - 1),
                                          [(c // 4, slice(128 * (c % 4), 128 * (c % 4) + 128), slice(0, 128))]))
                        for r in range(4):
                            for j in range(1, 4):
                                ks_ = slice(512 * (j - 1) + r, 512 * j, 4)
                                qs_ = slice(512 * j + r, 512 * j + 512, 4)
                                tiles.append(("prev", ks_, qs_, (1, 4 * (j - 1) + r), [(j, slice(r, 512, 4), slice(0, 128))]))
                        started = [False] * 4
                        nb_ = 0
                        i0 = 0

                        def emit_pv(batch, pt_):
                            nbt = len(batch)
                            for j, tl in enumerate(batch):
                                o, vt = tl[3]
                                nd = len(tl[4])
                                for di, (G, ocols, pcols) in enumerate(tl[4]):
                                    xb = banks[G]
                                    st = not started[G]
                                    started[G] = True
                                    pc = slice(128 * j + pcols.start, 128 * j + pcols.stop)
                                    lastpv = (j == nbt - 1 and di == nd - 1)
                                    kb.op(PE, lambda o=o, vt=vt, ocols=ocols, pc=pc, xb=xb, st=st:
                                          T.matmul(xb[:, ocols], vaug[:, o, vt, 64 * a:64 * a + 128], pt_[:, pc], start=st, stop=False, skip_group_check=True),
                                          reads=[pt_.tok(), vaug.tok((o, vt // 4, a)), vaug.tok("ones")], writes=[xb.tok()], inc=lastpv)
                        pending = None
                        while i0 < len(tiles):
                            mk = tiles[i0][0]
                            batch = [tiles[i0]]
                            while len(batch) < 4 and i0 + len(batch) < len(tiles) and tiles[i0 + len(batch)][0] == mk:
                                batch.append(tiles[i0 + len(batch)])
                            i0 += len(batch)
                            nbt = len(batch)
                            sb_ = banks[4 + (nb_ % 2)]
                            pt_ = pT[nb_ % 3]
                            nb_ += 1
                            mt = maskcur if mk == "cur" else maskprev
                            kb.op(PE, lambda: T.matmul(sb_[:, 0:128 * nbt], ident[:], mt[:, 0:128 * nbt], start=True, stop=False),
                                  reads=[ident.tok(), mt.tok()], writes=[sb_.tok()], inc=False)
                            for j, tl in enumerate(batch):
                                kb.op(PE, lambda j=j, tl=tl: T.matmul(sb_[:, 128 * j:128 * j + 128], kTp[:, tl[1]], qz[:, a, tl[2]], start=False, stop=(j == nbt - 1)),
                                      reads=kT_reads + qz_reads(a), writes=[sb_.tok()], inc=(j == nbt - 1))
                            kb.op(ACT, lambda: S.activation(out=pt_[:, 0:128 * nbt], in_=sb_[:, 0:128 * nbt], func=AF.Exp, scale=0.125),
                                  reads=[sb_.tok()], writes=[pt_.tok()])
                            if pending is not None:
                                emit_pv(*pending)
                            pending = (batch, pt_)
                        emit_pv(*pending)
                        for G in range(4):
                            xb = banks[G]
                            rc = rcp[0]
                            if a == 0:
                                kb.op(DVE, lambda: V.reciprocal(out=rc[0:64, :], in_=xb[64:128, :]), reads=[xb.tok()], writes=[rc.tok()])
                                kb.op(DVE, lambda: V.tensor_tensor(out=mixA[0:64, hp, 512 * G:512 * G + 512], in0=xb[0:64, :], in1=rc[0:64, :], op=ALU.mult),
                                      reads=[xb.tok(), rc.tok()], writes=[mixA.tok((hp, G, 0))])
                            else:
                                kb.op(DVE, lambda: V.reciprocal(out=rc[64:128, :], in_=xb[0:64, :]), reads=[xb.tok()], writes=[rc.tok()])
                                kb.op(DVE, lambda: V.tensor_tensor(out=mixA[64:128, hp, 512 * G:512 * G + 512], in0=xb[64:128, :], in1=rc[64:128, :], op=ALU.mult),
                                      reads=[xb.tok(), rc.tok()], writes=[mixA.tok((hp, G, 1))])

                    for b in range(4):
                        kc_ = kcs[b]
                        vc_ = vcs[b]
                        kb.dma(POOL, kc_[:], d_ck[l, 4 * b + hp, :, :], writes=[kc_.tok()])
                        kb.dma(POOL, vc_[:], d_cv[l, 4 * b + hp, :, :].rearrange("p (t c) -> p t c", c=128), writes=[vc_.tok()])
                        sbk = next_bank([4, 5])
                        qs = qz[:, :, 2048 + 8 * b:2056 + 8 * b]
                        qrd = [qz.tok((0, 4)), qz.tok((1, 4)), qz.tok(("z", 0)), qz.tok(("z", 1))]
                        for ti in range(10):
                            kb.op(PE, lambda ti=ti: T.matmul(sbk[:, 16 * ti:16 * ti + 16].rearrange("p (a q) -> p a q", a=2), kc_[:, 128 * ti:128 * ti + 128], qs,
                                                             start=True, stop=True),
                                  reads=[kc_.tok()] + qrd, writes=[sbk.tok()], inc=False)
                        kb.op(PE, lambda: T.matmul(sbk[0:8, 160:176].rearrange("p (a q) -> p a q", a=2), kTp[:, 2048 + 8 * b:2056 + 8 * b], qs, start=True, stop=True),
                              reads=[kTp.tok(4)] + qrd, writes=[sbk.tok()], inc=True)
                        kb.op(ACT, lambda: S.activation(out=pS[:, 0:160], in_=sbk[:, 0:160], func=AF.Exp, scale=0.125), reads=[sbk.tok()], writes=[pS.tok()])
                        kb.op(ACT, lambda: S.activation(out=pSn[0:8, :], in_=sbk[0:8, 160:176], func=AF.Exp, scale=0.125), reads=[sbk.tok()], writes=[pSn.tok()])
                        kb.op(DVE, lambda: V.tensor_tensor(out=pS[:, 0:160], in0=pS[:, 0:160], in1=smask[:, 0:160], op=ALU.mult),
                              reads=[pS.tok(), smask.tok()], writes=[pS.tok()])
                        kb.op(DVE, lambda: V.tensor_tensor(out=pSn[0:8, :], in0=pSn[0:8, :], in1=smask[0:8, 160:176], op=ALU.mult),
                              reads=[pSn.tok(), smask.tok()], writes=[pSn.tok()])
                        nb = next_bank([6, 7])
                        db = next_bank([6, 7])
                        for ti in range(10):
                            kb.op(PE, lambda ti=ti: T.matmul(nb[:, 0:16], vc_[:, ti, :], pS[:, 16 * ti:16 * ti + 16], start=(ti == 0), stop=False),
                                  reads=[vc_.tok(), pS.tok()], writes=[nb.tok()], inc=False)
                        kb.op(PE, lambda: T.matmul(nb[:, 0:16], vS[:, b, :], pSn[:, :], start=False, stop=True), reads=[vS.tok(), pSn.tok()], writes=[nb.tok()], inc=True)
                        for ti in range(10):
                            kb.op(PE, lambda ti=ti: T.matmul(db[:, 0:16], ones[:], pS[:, 16 * ti:16 * ti + 16], start=(ti == 0), stop=False),
                                  reads=[ones.tok(), pS.tok()], writes=[db.tok()], inc=False)
                        kb.op(PE, lambda: T.matmul(db[:, 0:16], ones[:], pSn[:, :], start=False, stop=True), reads=[ones.tok(), pSn.tok()], writes=[db.tok()], inc=True)
                        kb.op(DVE, lambda: V.reciprocal(out=rS[:, :], in_=db[:, 0:16]), reads=[db.tok()], writes=[rS.tok()])
                        kb.op(DVE, lambda: V.tensor_tensor(out=mixA[0:64, hp, 2048 + 8 * b:2056 + 8 * b], in0=nb[0:64, 0:8], in1=rS[0:64, 0:8], op=ALU.mult),
                              reads=[nb.tok(), rS.tok()], writes=[mixA.tok((hp, 4, 0))])
                        kb.op(DVE, lambda: V.tensor_tensor(out=mixA[64:128, hp, 2048 + 8 * b:2056 + 8 * b], in0=nb[64:128, 8:16], in1=rS[64:128, 8:16], op=ALU.mult),
                              reads=[nb.tok(), rS.tok()], writes=[mixA.tok((hp, 4, 1))])
                if stage == "inproj":
                    break
                if stage == "attn":
                    dump("mixA", mixA[:].rearrange("p c t -> p (c t)"), [mixA.tok((hp, G, a)) for hp in range(4) for G in range(5) for a in range(2)])
                    break

                for ci in range(5):
                    ma_toks = [mixA.tok((hp_, ci, a_)) for hp_ in range(4) for a_ in range(2)]
                    mat = mixA.tok(("n", ci))
                    kb.op(DVE, lambda: V.tensor_copy(out=rt[:, 0:1], in_=rt[:, 0:1]), reads=ma_toks + [rt.tok()], writes=[mat, rt.tok()])
                    rmsnorm_to_bf16(mixA, mat, 4, mixA, mat, lambda c: pv_col(l, "ag", c), 512.0, ci, sq, rt, [6, 7])

                UZW = (4 * NT * 2) // 4
                ph = Phase()
                uz = ph.take([128, 4, NT], BF16)
                wub = ph.take([128, 8, 512], BF16)
                kb.dma(POOL, wub[:], win_v[:, :, 1536:2048], writes=[wub.tok()])
                flip = 0
                for oc in range(4):
                    for ci, (c0, cn) in enumerate(CTS):
                        bk = next_bank([0, 1, 2, 3])
                        for kc in range(8):
                            kb.op(PE, lambda kc=kc: T.matmul(bk[:, 0:cn], wub[:, kc, 128 * oc:128 * oc + 128], xnT[:, kc, c0:c0 + cn], start=(kc == 0), stop=(kc == 7)),
                                  reads=[wub.tok(), xnT.tok(ci)], writes=[bk.tok()], inc=(kc == 7))
                        if flip % 2 == 0:
                            kb.op(ACT, lambda: S.copy(out=uz[:, oc, c0:c0 + cn], in_=bk[:, 0:cn]), reads=[bk.tok()], writes=[uz.tok((oc, ci))])
                        else:
                            kb.op(DVE, lambda: V.tensor_copy(out=uz[:, oc, c0:c0 + cn], in_=bk[:, 0:cn]), reads=[bk.tok()], writes=[uz.tok((oc, ci))])
                        flip += 1
                uz_all = [uz.tok((oc, ci)) for oc in range(4) for ci in range(5)]

                for hh in range(2):
                    ph2 = Phase(base=UZW)
                    ph2.bufs.append(uz)
                    g0 = 8 * hh

                    def tk(shape, dt=F32):
                        return ph2.take(shape, dt)
                    are = tk([128, 8]); aim = tk([128, 8]); ldt = tk([128, 8])
                    bre = tk([128, 8, 16]); bim = tk([128, 8, 16]); cre = tk([128, 8, 16]); cim = tk([128, 8, 16])
                    h0 = tk([128, 2, 8, 4])
                    kb.dma(SP, are[:], d_are[:, l, g0:g0 + 8], writes=[are.tok()])
                    kb.dma(SP, aim[:], d_aim[:, l, g0:g0 + 8], writes=[aim.tok()])
                    kb.dma(SP, ldt[:], d_ldt[:, l, g0:g0 + 8], writes=[ldt.tok()])
                    kb.dma(SP, bre[:], d_bre[:, l, 16 * g0:16 * g0 + 128].rearrange("p (g c) -> p g c", c=16), writes=[bre.tok()])
                    kb.dma(SP, bim[:], d_bim[:, l, 16 * g0:16 * g0 + 128].rearrange("p (g c) -> p g c", c=16), writes=[bim.tok()])
                    kb.dma(SP, cre[:], d_cre[:, l, 16 * g0:16 * g0 + 128].rearrange("p (g c) -> p g c", c=16), writes=[cre.tok()])
                    kb.dma(SP, cim[:], d_cim[:, l, 16 * g0:16 * g0 + 128].rearrange("p (g c) -> p g c", c=16), writes=[cim.tok()])
                    kb.dma(SP, h0[:], d_h0[:, l, :, g0:g0 + 8, :], writes=[h0.tok()])
                    dtt = tk([128, 8]); lr = tk([128, 8]); rho = tk([128, 8]); th = tk([128, 8])
                    t_a = tk([128, NMV, 8]); t_b = tk([128, NMV, 8]); t_c = tk([128, NMV, 8]); t_d = tk([128, NMV, 8])
                    Er = tk([128, NMV, 8]); Ei = tk([128, NMV, 8])
                    s1 = tk([128, 8]); s2 = tk([128, 8]); s3 = tk([128, 8]); s4 = tk([128, 8]); fr = tk([128, 8]); fi = tk([128, 8])
                    bbr = tk([128, 8, 16]); bbi = tk([128, 8, 16]); tb1 = tk([128, 8, 16])
                    scr = tk([128, 2, 1152])
                    RW = tk([128, 3072])
                    Xr = tk([128, 9, 8, 16], BF16); XiN = tk([128, 9, 8, 16], BF16)
                    Kblk = tk([128, 2, 8, 128], BF16)
                    Harr = tk([128, 2, 8, 257])
                    Ss = tk([128, 2, 8, 4]); Hs = tk([128, 2, 8, 4]); tS = tk([128, 2, 8, 4])
                    tl1 = tk([128, 8, 16]); tl2 = tk([128, 8, 16]); tl3 = tk([128, 8, 16]); tl4 = tk([128, 8, 16])
                    Hb = [tk([128, 2, 8, 64], BF16) for _ in range(2)]
                    ytmp = [tk([128, 512]) for _ in range(2)]
                    rwb = RW[:].bitcast(BF16)
                    WT = [rwb[:, 2048 * e:2048 * e + 2048].rearrange("p (m r c) -> p m r c", m=8, r=2) for e in range(2)]
                    BbPad = rwb[:, 4096:6144].rearrange("p (g r c) -> p g r c", g=8, r=2)
                    PT = rwb[:, 0:4096].rearrange("p (t g i c) -> p t g i c", t=8, g=2, i=8)
                    rw = RW.tok()

                    def bc_m(x):
                        return x[:].unsqueeze(1).broadcast_to([128, NMV, 8])

                    def dv(fn, reads, writes):
                        kb.op(DVE, fn, reads=reads, writes=writes)

                    kb.op(ACT, lambda: S.activation(out=dtt[:], in_=ldt[:], func=AF.Exp), reads=[ldt.tok()], writes=[dtt.tok()])
                    dv(lambda: V.tensor_scalar(out=lr[:], in0=are[:], scalar1=-1e-4, scalar2=None, op0=ALU.min), [are.tok()], [lr.tok()])
                    dv(lambda: V.tensor_tensor(out=rho[:], in0=lr[:], in1=dtt[:], op=ALU.mult), [lr.tok(), dtt.tok()], [rho.tok()])
                    dv(lambda: V.tensor_tensor(out=th[:], in0=aim[:], in1=dtt[:], op=ALU.mult), [aim.tok(), dtt.tok()], [th.tok()])
                    dv(lambda: V.tensor_tensor(out=t_a[:], in0=bc_m(rho), in1=mvt[:, :, 0:8], op=ALU.mult), [rho.tok(), mvt.tok()], [t_a.tok()])
                    dv(lambda: V.tensor_tensor(out=t_b[:], in0=bc_m(th), in1=mvt[:, :, 0:8], op=ALU.mult), [th.tok(), mvt.tok()], [t_b.tok()])
                    kb.op(ACT, lambda: S.activation(out=t_a[:], in_=t_a[:], func=AF.Exp), reads=[t_a.tok()], writes=[t_a.tok()])
                    dv(lambda: V.tensor_scalar(out=t_c[:], in0=t_b[:], scalar1=1.0 / TWO_PI, scalar2=MAGIC, op0=ALU.mult, op1=ALU.add), [t_b.tok()], [t_c.tok()])
                    dv(lambda: V.tensor_scalar(out=t_c[:], in0=t_c[:], scalar1=-MAGIC, scalar2=None, op0=ALU.add), [t_c.tok()], [t_c.tok()])
                    dv(lambda: V.scalar_tensor_tensor(out=t_b[:], in0=t_c[:], scalar=-CW1, in1=t_b[:], op0=ALU.mult, op1=ALU.add), [t_c.tok(), t_b.tok()], [t_b.tok()])
                    dv(lambda: V.scalar_tensor_tensor(out=t_b[:], in0=t_c[:], scalar=-CW2, in1=t_b[:], op0=ALU.mult, op1=ALU.add), [t_c.tok(), t_b.tok()], [t_b.tok()])
                    dv(lambda: V.tensor_scalar(out=t_b[:], in0=t_b[:], scalar1=-np.pi, scalar2=np.pi, op0=ALU.max, op1=ALU.min), [t_b.tok()], [t_b.tok()])
                    kb.op(ACT, lambda: S.activation(out=t_c[:], in_=t_b[:], func=AF.Sin), reads=[t_b.tok()], writes=[t_c.tok()])
                    kb.op(ACT, lambda: S.activation(out=t_d[:], in_=t_b[:], func=AF.Sin, scale=0.5), reads=[t_b.tok()], writes=[t_d.tok()])
                    dv(lambda: V.tensor_tensor(out=t_d[:], in0=t_d[:], in1=t_d[:], op=ALU.mult), [t_d.tok()], [t_d.tok()])
                    dv(lambda: V.tensor_scalar(out=t_d[:], in0=t_d[:], scalar1=-2.0, scalar2=1.0, op0=ALU.mult, op1=ALU.add), [t_d.tok()], [t_d.tok()])
                    dv(lambda: V.tensor_tensor(out=Er[:], in0=t_a[:], in1=t_d[:], op=ALU.mult), [t_a.tok(), t_d.tok()], [Er.tok()])
                    dv(lambda: V.tensor_tensor(out=Ei[:], in0=t_a[:], in1=t_c[:], op=ALU.mult), [t_a.tok(), t_c.tok()], [Ei.tok()])
                    dv(lambda: V.tensor_scalar(out=s1[:], in0=Er[:, 1, :], scalar1=-1.0, scalar2=None, op0=ALU.add), [Er.tok()], [s1.tok()])
                    dv(lambda: V.tensor_tensor(out=s2[:], in0=lr[:], in1=lr[:], op=ALU.mult), [lr.tok()], [s2.tok()])
                    dv(lambda: V.tensor_tensor(out=s3[:], in0=aim[:], in1=aim[:], op=ALU.mult), [aim.tok()], [s3.tok()])
                    dv(lambda: V.tensor_tensor(out=s2[:], in0=s2[:], in1=s3[:], op=ALU.add), [s2.tok(), s3.tok()], [s2.tok()])
                    dv(lambda: V.reciprocal(out=s2[:], in_=s2[:]), [s2.tok()], [s2.tok()])
                    dv(lambda: V.tensor_tensor(out=fr[:], in0=s1[:], in1=lr[:], op=ALU.mult), [s1.tok(), lr.tok()], [fr.tok()])
                    dv(lambda: V.tensor_tensor(out=s3[:], in0=Ei[:, 1, :], in1=aim[:], op=ALU.mult), [Ei.tok(), aim.tok()], [s3.tok()])
                    dv(lambda: V.tensor_tensor(out=fr[:], in0=fr[:], in1=s3[:], op=ALU.add), [fr.tok(), s3.tok()], [fr.tok()])
                    dv(lambda: V.tensor_tensor(out=fr[:], in0=fr[:], in1=s2[:], op=ALU.mult), [fr.tok(), s2.tok()], [fr.tok()])
                    dv(lambda: V.tensor_tensor(out=fi[:], in0=Ei[:, 1, :], in1=lr[:], op=ALU.mult), [Ei.tok(), lr.tok()], [fi.tok()])
                    dv(lambda: V.tensor_tensor(out=s3[:], in0=s1[:], in1=aim[:], op=ALU.mult), [s1.tok(), aim.tok()], [s3.tok()])
                    dv(lambda: V.tensor_tensor(out=fi[:], in0=fi[:], in1=s3[:], op=ALU.subtract), [fi.tok(), s3.tok()], [fi.tok()])
                    dv(lambda: V.tensor_tensor(out=fi[:], in0=fi[:], in1=s2[:], op=ALU.mult), [fi.tok(), s2.tok()], [fi.tok()])

                    def bc_c(x):
                        return x[:].unsqueeze(2).broadcast_to([128, 8, 16])
                    dv(lambda: V.tensor_tensor(out=bbr[:], in0=bc_c(fr), in1=bre[:], op=ALU.mult), [fr.tok(), bre.tok()], [bbr.tok()])
                    dv(lambda: V.tensor_tensor(out=tb1[:], in0=bc_c(fi), in1=bim[:], op=ALU.mult), [fi.tok(), bim.tok()], [tb1.tok()])
                    dv(lambda: V.tensor_tensor(out=bbr[:], in0=bbr[:], in1=tb1[:], op=ALU.subtract), [bbr.tok(), tb1.tok()], [bbr.tok()])
                    dv(lambda: V.tensor_tensor(out=bbi[:], in0=bc_c(fr), in1=bim[:], op=ALU.mult), [fr.tok(), bim.tok()], [bbi.tok()])
                    dv(lambda: V.tensor_tensor(out=tb1[:], in0=bc_c(fi), in1=bre[:], op=ALU.mult), [fi.tok(), bre.tok()], [tb1.tok()])
                    dv(lambda: V.tensor_tensor(out=bbi[:], in0=bbi[:], in1=tb1[:], op=ALU.add), [bbi.tok(), tb1.tok()], [bbi.tok()])

                    def Em(E_, n_m):
                        return E_[:, 0:n_m, :].unsqueeze(3).broadcast_to([128, n_m, 8, 16])

                    def Bm(b_, n_m):
                        return b_[:].unsqueeze(1).broadcast_to([128, n_m, 8, 16])
                    Wr = scr[:, 0, 0:1024].rearrange("p (m g c) -> p m g c", m=8, g=8)
                    Wi = scr[:, 1, 0:1024].rearrange("p (m g c) -> p m g c", m=8, g=8)
                    tmpW = Harr[:, 0, :, 0:128].rearrange("p g (m c) -> p m g c", m=8)
                    st = scr.tok()
                    ht = Harr.tok()
                    dv(lambda: V.tensor_tensor(out=Wr, in0=Em(Er, 8), in1=Bm(bbr, 8), op=ALU.mult), [Er.tok(), bbr.tok()], [st])
                    dv(lambda: V.tensor_tensor(out=tmpW, in0=Em(Ei, 8), in1=Bm(bbi, 8), op=ALU.mult), [Ei.tok(), bbi.tok()], [ht])
                    dv(lambda: V.tensor_tensor(out=Wr, in0=Wr, in1=tmpW, op=ALU.subtract), [st, ht], [st])
                    dv(lambda: V.tensor_tensor(out=Wi, in0=Em(Er, 8), in1=Bm(bbi, 8), op=ALU.mult), [Er.tok(), bbi.tok()], [st])
                    dv(lambda: V.tensor_tensor(out=tmpW, in0=Em(Ei, 8), in1=Bm(bbr, 8), op=ALU.mult), [Ei.tok(), bbr.tok(), st], [ht])
                    dv(lambda: V.tensor_tensor(out=Wi, in0=Wi, in1=tmpW, op=ALU.add), [st, ht], [st])
                    for ri in range(2):
                        Wsrc = Wr if ri == 0 else Wi
                        for mb in range(2):
                            bk = next_bank([0, 1, 2, 3])
                            for mm in range(4):
                                m_ = 4 * mb + mm
                                kb.op(PE, lambda m_=m_, mm=mm: T.transpose(bk[:, 128 * mm:128 * mm + 128], Wsrc[:, m_, :, :].rearrange("p g c -> p (g c)"), identf[:]),
                                      reads=[st, identf.tok()], writes=[bk.tok()], inc=(mm == 3))
                            for e in range(2):
                                kb.op(DVE, lambda e=e: V.tensor_scalar(out=WT[e][:, 4 * mb:4 * mb + 4, ri, :], in0=bk[:].rearrange("p (m c) -> p m c", c=128),
                                                                       scalar1=rowmask[:, e:e + 1], scalar2=None, op0=ALU.mult),
                                      reads=[bk.tok(), rowmask.tok()], writes=[rw])
                    X1 = scr[:, 0, 0:1152].rearrange("p (m g c) -> p m g c", m=9, g=8)
                    X2 = scr[:, 1, 0:1152].rearrange("p (m g c) -> p m g c", m=9, g=8)
                    dv(lambda: V.tensor_tensor(out=X1, in0=Em(Er, 9), in1=Bm(cre, 9), op=ALU.mult), [Er.tok(), cre.tok()], [st])
                    dv(lambda: V.tensor_tensor(out=X2, in0=Em(Ei, 9), in1=Bm(cim, 9), op=ALU.mult), [Ei.tok(), cim.tok()], [st])
                    dv(lambda: V.tensor_tensor(out=Xr[:], in0=X1, in1=X2, op=ALU.subtract), [st], [Xr.tok()])
                    dv(lambda: V.tensor_tensor(out=X1, in0=Em(Ei, 9), in1=Bm(cre, 9), op=ALU.mult), [Ei.tok(), cre.tok(), Xr.tok()], [st])
                    dv(lambda: V.tensor_tensor(out=X2, in0=Em(Er, 9), in1=Bm(cim, 9), op=ALU.mult), [Er.tok(), cim.tok()], [st])
                    dv(lambda: V.tensor_tensor(out=X1, in0=X1, in1=X2, op=ALU.add), [st], [st])
                    dv(lambda: V.tensor_scalar(out=XiN[:], in0=X1, scalar1=-1.0, scalar2=None, op0=ALU.mult), [st], [XiN.tok()])
                    bpt = Tok(ph2.fence)
                    kb.op(POOL, lambda: nc.gpsimd.memset(BbPad, 0.0), reads=[], writes=[bpt])
                    for i in range(8):
                        kb.op(ACT, lambda i=i: S.copy(out=BbPad[:, i, 0, 16 * i:16 * i + 16], in_=bbr[:, i, :]), reads=[bbr.tok()], writes=[bpt])
                        kb.op(ACT, lambda i=i: S.copy(out=BbPad[:, i, 1, 16 * i:16 * i + 16], in_=bbi[:, i, :]), reads=[bbi.tok()], writes=[bpt])
                    for g2 in range(2):
                        for lh in range(2):
                            bk = next_bank([0, 1, 2, 3])
                            bv = bk[:].rearrange("p (m c) -> p m c", c=128)
                            for i in range(8):
                                for lq in range(4):
                                    kb.op(PE, lambda i=i, lq=lq: T.matmul(bv[:, lq, 16 * i:16 * i + 16], BbPad[64 * g2:64 * g2 + 64, i, 0, :], Xr[64 * g2:64 * g2 + 64, 4 * lh + lq, i, :],
                                                                          start=True, stop=False, tile_position=(64 * g2, 0)),
                                          reads=[bpt, Xr.tok()], writes=[bk.tok()], inc=False)
                                    kb.op(PE, lambda i=i, lq=lq: T.matmul(bv[:, lq, 16 * i:16 * i + 16], BbPad[64 * g2:64 * g2 + 64, i, 1, :], XiN[64 * g2:64 * g2 + 64, 4 * lh + lq, i, :],
                                                                          start=False, stop=True, tile_position=(64 * g2, 0)),
                                          reads=[bpt, XiN.tok()], writes=[bk.tok()], inc=(i == 7 and lq == 3))
                            kb.op(ACT, lambda: S.copy(out=Kblk[:, g2, 4 * lh:4 * lh + 4, :], in_=bv), reads=[bk.tok()], writes=[Kblk.tok()])
                    grp = 0
                    for ri in range(2):
                        for e_ in range(2):
                            bset = [banks[4 * (grp % 2) + j_] for j_ in range(4)]
                            grp += 1
                            for g2 in range(2):
                                oc = 2 * g2 + hh
                                for sg in range(8):
                                    for j_ in range(4):
                                        bk = bset[j_]
                                        uv = uz[32 * j_:32 * j_ + 32, oc, :].rearrange("p (k s) -> p k s", s=8)
                                        kb.op(PE, lambda sg=sg, g2=g2, uv=uv, bk=bk, j_=j_: T.matmul(bk[64 * g2:64 * g2 + 64, 0:NCH], WT[e_][32 * j_:32 * j_ + 32, 7 - sg, ri, 64 * g2:64 * g2 + 64],
                                                                                                      uv[:, :, sg], start=(sg == 0), stop=(sg == 7), tile_position=(32 * j_, 64 * g2),
                                                                                                      skip_group_check=True),
                                              reads=[rw] + uz_all, writes=[bk.tok()], inc=(sg == 7 and g2 == 1))
                            for j_ in range(4):
                                i = 2 * j_ + e_
                                bk = bset[j_]
                                kb.op(ACT, lambda: S.copy(out=Harr[:, ri, i, 1:257], in_=bk[:, 0:256]), reads=[bk.tok(), ht], writes=[Harr.tok((ri, i))])
                                kb.op(DVE, lambda: V.tensor_copy(out=Ss[:, ri, i, :], in_=bk[:, 256:260]), reads=[bk.tok()], writes=[Ss.tok()])
                    hall = [Harr.tok((ri, i)) for ri in range(2) for i in range(8)]
                    hsc = Harr.tok("scan")
                    dv(lambda: V.memset(Harr[:, :, :, 0:1], 0.0), hall + [ht], [hsc])
                    kb.op(POOL, lambda: nc.gpsimd.memset(rwb[:, 0:4096], 0.0), reads=[], writes=[rw])
                    for par in range(2):
                        for ri in range(2):
                            Xs = Xr if ri == 0 else XiN
                            for g2 in range(2):
                                kb.op(ACT, lambda g2=g2: S.copy(out=PT[64 * ri:64 * ri + 64, :, g2, par:8:2, 16 * par:16 * par + 16], in_=Xs[64 * g2:64 * g2 + 64, 1:9, par:8:2, :]),
                                      reads=[Xs.tok()], writes=[rw])
                    A8r = Er[:, MIDX[8], :].unsqueeze(2).broadcast_to([128, 8, 16])
                    A8i = Ei[:, MIDX[8], :].unsqueeze(2).broadcast_to([128, 8, 16])
                    et = [Er.tok(), Ei.tok()]

                    def Hv(ri, j):
                        return Harr[:, ri, :, 1 + j:257:16]
                    for j in range(1, 16):
                        dv(lambda: V.tensor_tensor(out=tl1[:], in0=A8r, in1=Hv(0, j - 1), op=ALU.mult), et + [hsc], [tl1.tok()])
                        dv(lambda: V.tensor_tensor(out=tl2[:], in0=A8i, in1=Hv(1, j - 1), op=ALU.mult), et + [hsc], [tl2.tok()])
                        dv(lambda: V.tensor_tensor(out=tl3[:], in0=A8r, in1=Hv(1, j - 1), op=ALU.mult), et + [hsc], [tl3.tok()])
                        dv(lambda: V.tensor_tensor(out=tl4[:], in0=A8i, in1=Hv(0, j - 1), op=ALU.mult), et + [hsc], [tl4.tok()])
                        dv(lambda: V.tensor_tensor(out=tl1[:], in0=tl1[:], in1=tl2[:], op=ALU.subtract), [tl1.tok(), tl2.tok()], [tl1.tok()])
                        dv(lambda: V.tensor_tensor(out=tl3[:], in0=tl3[:], in1=tl4[:], op=ALU.add), [tl3.tok(), tl4.tok()], [tl3.tok()])
                        dv(lambda: V.tensor_tensor(out=Hv(0, j), in0=Hv(0, j), in1=tl1[:], op=ALU.add), [tl1.tok(), hsc], [hsc])
                        dv(lambda: V.tensor_tensor(out=Hv(1, j), in0=Hv(1, j), in1=tl3[:], op=ALU.add), [tl3.tok(), hsc], [hsc])
                    A128r = Er[:, MIDX[128], :]
                    A128i = Ei[:, MIDX[128], :]

                    def Ce(ri, b):
                        return Harr[:, ri, :, 16 * b + 16]
                    for sft in (1, 2, 4, 8):
                        nn = 16 - sft
                        Ar_ = Er[:, MIDX[128 * sft], :].unsqueeze(2).broadcast_to([128, 8, nn])
                        Ai_ = Ei[:, MIDX[128 * sft], :].unsqueeze(2).broadcast_to([128, 8, nn])

                        def Plo(ri):
                            return Harr[:, ri, :, 16:16 + 16 * nn:16]

                        def Phi(ri):
                            return Harr[:, ri, :, 16 + 16 * sft:257:16]
                        dv(lambda: V.tensor_tensor(out=tl1[:, :, 0:nn], in0=Ar_, in1=Plo(0), op=ALU.mult), et + [hsc], [tl1.tok()])
                        dv(lambda: V.tensor_tensor(out=tl2[:, :, 0:nn], in0=Ai_, in1=Plo(1), op=ALU.mult), et + [hsc], [tl2.tok()])
                        dv(lambda: V.tensor_tensor(out=tl3[:, :, 0:nn], in0=Ar_, in1=Plo(1), op=ALU.mult), et + [hsc], [tl3.tok()])
                        dv(lambda: V.tensor_tensor(out=tl4[:, :, 0:nn], in0=Ai_, in1=Plo(0), op=ALU.mult), et + [hsc], [tl4.tok()])
                        dv(lambda: V.tensor_tensor(out=tl1[:, :, 0:nn], in0=tl1[:, :, 0:nn], in1=tl2[:, :, 0:nn], op=ALU.subtract), [tl1.tok(), tl2.tok()], [tl1.tok()])
                        dv(lambda: V.tensor_tensor(out=tl3[:, :, 0:nn], in0=tl3[:, :, 0:nn], in1=tl4[:, :, 0:nn], op=ALU.add), [tl3.tok(), tl4.tok()], [tl3.tok()])
                        dv(lambda: V.tensor_tensor(out=Phi(0), in0=Phi(0), in1=tl1[:, :, 0:nn], op=ALU.add), [tl1.tok(), hsc], [hsc])
                        dv(lambda: V.tensor_tensor(out=Phi(1), in0=Phi(1), in1=tl3[:, :, 0:nn], op=ALU.add), [tl3.tok(), hsc], [hsc])
                    F1 = scr[:, 0, 0:1800].rearrange("p (g b j) -> p g b j", g=8, b=15) if False else None
                    fx = scr[:].rearrange("p a w -> p (a w)")
                    F1 = fx[:, 0:1800].rearrange("p (g b j) -> p g b j", g=8, b=15)

                    def Ep(E_):
                        return E_[:, 8:23, :].rearrange("p j g -> p g j").unsqueeze(2).broadcast_to([128, 8, 15, 15])

                    def Cb(ri):
                        return Harr[:, ri, :, 16:241:16].unsqueeze(3).broadcast_to([128, 8, 15, 15])

                    def Hf(ri):
                        return Harr[:, ri, :, 17:257].rearrange("p g (b j) -> p g b j", j=16)[:, :, :, 0:15]
                    dv(lambda: V.tensor_tensor(out=F1, in0=Ep(Er), in1=Cb(0), op=ALU.mult), et + [hsc], [st])
                    dv(lambda: V.tensor_tensor(out=Hf(0), in0=Hf(0), in1=F1, op=ALU.add), [st, hsc], [hsc])
                    dv(lambda: V.tensor_tensor(out=F1, in0=Ep(Ei), in1=Cb(1), op=ALU.mult), et + [hsc], [st])
                    dv(lambda: V.tensor_tensor(out=Hf(0), in0=Hf(0), in1=F1, op=ALU.subtract), [st, hsc], [hsc])
                    dv(lambda: V.tensor_tensor(out=F1, in0=Ep(Er), in1=Cb(1), op=ALU.mult), et + [hsc], [st])
                    dv(lambda: V.tensor_tensor(out=Hf(1), in0=Hf(1), in1=F1, op=ALU.add), [st, hsc], [hsc])
                    dv(lambda: V.tensor_tensor(out=F1, in0=Ep(Ei), in1=Cb(0), op=ALU.mult), et + [hsc], [st])
                    dv(lambda: V.tensor_tensor(out=Hf(1), in0=Hf(1), in1=F1, op=ALU.add), [st, hsc], [hsc])
                    A8r4 = Er[:, MIDX[8], :].unsqueeze(2).broadcast_to([128, 8, 4])
                    A8i4 = Ei[:, MIDX[8], :].unsqueeze(2).broadcast_to([128, 8, 4])
                    dv(lambda: V.tensor_tensor(out=Hs[:, 0], in0=A8r4, in1=h0[:, 0], op=ALU.mult), et + [h0.tok()], [Hs.tok()])
                    dv(lambda: V.tensor_tensor(out=tS[:, 0], in0=A8i4, in1=h0[:, 1], op=ALU.mult), et + [h0.tok()], [tS.tok()])
                    dv(lambda: V.tensor_tensor(out=Hs[:, 0], in0=Hs[:, 0], in1=tS[:, 0], op=ALU.subtract), [Hs.tok(), tS.tok()], [Hs.tok()])
                    dv(lambda: V.tensor_tensor(out=Hs[:, 1], in0=A8r4, in1=h0[:, 1], op=ALU.mult), et + [h0.tok()], [Hs.tok()])
                    dv(lambda: V.tensor_tensor(out=tS[:, 1], in0=A8i4, in1=h0[:, 0], op=ALU.mult), et + [h0.tok()], [tS.tok()])
                    dv(lambda: V.tensor_tensor(out=Hs[:, 1], in0=Hs[:, 1], in1=tS[:, 1], op=ALU.add), [Hs.tok(), tS.tok()], [Hs.tok()])
                    dv(lambda: V.tensor_tensor(out=Hs[:], in0=Hs[:], in1=Ss[:], op=ALU.add), [Hs.tok(), Ss.tok()], [Hs.tok()])
                    for ri in range(2):
                        kb.dma(SP, o_h[:, l, ri, g0:g0 + 8, 0:4], Hs[:, ri], reads=[Hs.tok()])
                        kb.dma(SP, o_h[:, l, ri, g0:g0 + 8, 4:5], Harr[:, ri, :, 256:257], reads=[hsc], allow_slow_non_contiguous=True)
                    if stage == "s5scan":
                        dump("Harr%d" % hh, Harr[:].rearrange("p r g k -> p (r g k)"), [hsc])
                        dump("Kblk%d" % hh, Kblk[:].rearrange("p a m c -> p (a m c)"), [Kblk.tok()])
                    for ci, (c0, cn) in enumerate(CTS):
                        k0, kn = c0 // 8, cn // 8
                        hb = Hb[ci % 2]
                        for ri in range(2):
                            for g2 in range(2):
                                if ci < 4:
                                    kb.op(ACT, lambda ri=ri, g2=g2: S.copy(out=hb[64 * ri:64 * ri + 64, g2, :, 0:kn], in_=Harr[64 * g2:64 * g2 + 64, ri, :, k0:k0 + kn]),
                                          reads=[hsc], writes=[hb.tok()])
                                else:
                                    kb.op(ACT, lambda ri=ri, g2=g2: S.copy(out=hb[64 * ri:64 * ri + 64, g2, :, 0:4], in_=h0[64 * g2:64 * g2 + 64, ri, :, :]),
                                          reads=[h0.tok()], writes=[hb.tok()])
                        for g2 in range(2):
                            oc = 2 * g2 + hh
                            bk = next_bank([0, 1, 2, 3])
                            kb.op(PE, lambda: T.matmul(bk[:, 0:cn], Kblk[:, g2, 0, :], uz[:, oc, c0:c0 + cn], start=True, stop=False),
                                  reads=[Kblk.tok(), uz.tok((oc, ci))], writes=[bk.tok()], inc=False)
                            uvv = uz[:, oc, c0:c0 + cn].rearrange("p (k s) -> p k s", s=8)
                            bvv = bk[:, 0:cn].rearrange("p (k s) -> p k s", s=8)
                            for ta in range(8):
                                for i in (0, 2, 4, 6, 1, 3, 5, 7):
                                    kb.op(PE, lambda i=i, ta=ta: T.matmul(bvv[32 * (i // 2):32 * (i // 2) + 32, :, ta], PT[:, ta, g2, i, :],
                                                                          hb[:, g2, i, 0:kn], start=False, stop=False,
                                                                          tile_position=(0, 32 * (i // 2))),
                                          reads=[rw, hb.tok()], writes=[bk.tok()], inc=False)
                            for lg in range(1, 8):
                                for ta in range(lg, 8):
                                    kb.op(PE, lambda lg=lg, ta=ta: T.matmul(bvv[:, :, ta], Kblk[:, g2, lg, :], uvv[:, :, ta - lg], start=False, stop=(lg == 7 and ta == 7)),
                                          reads=[Kblk.tok(), uz.tok((oc, ci))], writes=[bk.tok()], inc=(lg == 7 and ta == 7))
                            yt = ytmp[(2 * ci + g2) % 2]
                            dv(lambda: V.scalar_tensor_tensor(out=yt[:, 0:cn], in0=uz[:, oc, c0:c0 + cn], scalar=pv_col(l, "sd", oc), in1=bk[:, 0:cn], op0=ALU.mult, op1=ALU.add),
                               [uz.tok((oc, ci)), bk.tok(), pvec.tok()], [yt.tok()])
                            kb.op(ACT, lambda: S.activation(out=uz[:, oc, c0:c0 + cn], in_=yt[:, 0:cn], func=AF.Gelu_apprx_tanh), reads=[yt.tok()], writes=[uz.tok((oc, ci))])
                if stage == "s5scan":
                    break
                if stage == "s5":
                    dump("zT", uz[:].rearrange("p c t -> p (c t)"), uz_all)
                    break

                ph = Phase(base=UZW)
                ph.bufs.append(uz)
                gw = ph.take([128, 4, 512], BF16)
                sgt = [ph.take([128, 512]) for _ in range(2)]
                sq = [ph.take([128, 512], BF16), ph.take([128, 512], BF16)]
                rt = ph.take([128, 512])
                wo = ph.take([128, 8, 1024], BF16)
                kb.dma(POOL, gw[:], d_glu[l].rearrange("(kc p) n -> p kc n", p=128), writes=[gw.tok()])
                wout_v = d_wout[l].rearrange("(kc p) n -> p kc n", p=128)
                for hhalf in range(2):
                    kb.dma(POOL, wo[:, :, 512 * hhalf:512 * hhalf + 512], wout_v[:, :, 512 * hhalf:512 * hhalf + 512], writes=[wo.tok(hhalf)])
                mixS = Buf(xnT.t[:, 0:4, :])
                def stageA(ci):
                    c0, cn = CTS[ci]
                    for oc in range(4):
                        bk = next_bank([0, 1, 2, 3])
                        for kc in range(4):
                            kb.op(PE, lambda kc=kc: T.matmul(bk[:, 0:cn], gw[:, kc, 128 * oc:128 * oc + 128], uz[:, kc, c0:c0 + cn], start=(kc == 0), stop=(kc == 3)),
                                  reads=[gw.tok(), uz.tok((kc, ci))], writes=[bk.tok()], inc=(kc == 3))
                        sg_ = sgt[oc % 2]
                        kb.op(ACT, lambda: S.activation(out=sg_[:, 0:cn], in_=bk[:, 0:cn], func=AF.Sigmoid, bias=pv_col(l, "gb", oc)), reads=[bk.tok(), pvec.tok()], writes=[sg_.tok()])
                        kb.op(DVE, lambda: V.tensor_tensor(out=mixS[:, oc, c0:c0 + cn], in0=uz[:, oc, c0:c0 + cn], in1=sg_[:, 0:cn], op=ALU.mult),
                              reads=[uz.tok((oc, ci)), sg_.tok()], writes=[xnT.tok(ci)])
                    rmsnorm_to_bf16(mixS, xnT.tok(ci), 4, mixS, xnT.tok(ci), lambda c: pv_col(l, "sg", c), 512.0, ci, sq, rt, [4, 5])

                def stageB(ci):
                    c0, cn = CTS[ci]
                    mat = mixA.tok(("n", ci))
                    for dc in range(8):
                        bk = next_bank([0, 1, 2, 3, 6, 7])
                        for kc in range(8):
                            src = mixA[:, kc, c0:c0 + cn] if kc < 4 else mixS[:, kc - 4, c0:c0 + cn]
                            stok = mat if kc < 4 else xnT.tok(ci)
                            kb.op(PE, lambda kc=kc, src=src: T.matmul(bk[:, 0:cn], wo[:, kc, 128 * dc:128 * dc + 128], src, start=(kc == 0), stop=(kc == 7)),
                                  reads=[wo.tok(dc // 4), stok], writes=[bk.tok()], inc=(kc == 7))
                        kb.op(DVE, lambda: V.tensor_tensor(out=xT[:, dc, c0:c0 + cn], in0=bk[:, 0:cn], in1=xT[:, dc, c0:c0 + cn], op=ALU.add),
                              reads=[bk.tok(), xT.tok(ci)], writes=[xT.tok(ci)])
                stageA(0)
                for ci in range(1, 5):
                    stageA(ci)
                    stageB(ci - 1)
                stageB(4)
                if stage == "mix":
                    dump("hT", xT[:].rearrange("p c t -> p (c t)"), [xT.tok(c) for c in range(5)])
                    break

                ph = Phase()
                sq = [ph.take([128, 512], BF16), ph.take([128, 512], BF16)]
                rt = ph.take([128, 512])
                wgu = [ph.take([128, 8, 2, 512], BF16) for _ in range(2)]
                wdn = [ph.take([128, 4, 1024], BF16) for _ in range(2)]
                actb = [ph.take([128, 4, 512], BF16) for _ in range(2)]
                sil = [ph.take([128, 512]) for _ in range(2)]
                for ci in range(5):
                    rmsnorm_to_bf16(xT, xT.tok(ci), 8, xnT, xnT.tok(ci), lambda c: pv_col(l, "n2", c), 1024.0, ci, sq, rt, [4, 5])
                wg_v = d_wg[l].rearrange("(kc p) n -> p kc n", p=128)
                wu_v = d_wu[l].rearrange("(kc p) n -> p kc n", p=128)
                fgs = [(0, 4), (4, 4), (8, 4), (12, 4), (16, 4), (20, 2)]
                na = 0
                for gi, (f0, fn_) in enumerate(fgs):
                    wA = wgu[gi % 2]
                    wD = wdn[gi % 2]
                    kb.dma(POOL, wA[:, :, 0, 0:128 * fn_], wg_v[:, :, 128 * f0:128 * (f0 + fn_)], writes=[wA.tok(0)])
                    kb.dma(POOL, wA[:, :, 1, 0:128 * fn_], wu_v[:, :, 128 * f0:128 * (f0 + fn_)], writes=[wA.tok(1)])
                    kb.dma(POOL, wD[:, 0:fn_, :], d_wd[l, 128 * f0:128 * (f0 + fn_), :].rearrange("(ft p) n -> p ft n", p=128), writes=[wD.tok()])
                    for ci, (c0, cn) in enumerate(CTS):
                        ab = actb[na % 2]
                        na += 1
                        for ft in range(fn_):
                            bg = next_bank([0, 1, 2, 3])
                            bu = next_bank([0, 1, 2, 3])
                            for kc in range(8):
                                kb.op(PE, lambda kc=kc: T.matmul(bg[:, 0:cn], wA[:, kc, 0, 128 * ft:128 * ft + 128], xnT[:, kc, c0:c0 + cn], start=(kc == 0), stop=(kc == 7)),
                                      reads=[wA.tok(0), xnT.tok(ci)], writes=[bg.tok()], inc=(kc == 7))
                            for kc in range(8):
                                kb.op(PE, lambda kc=kc: T.matmul(bu[:, 0:cn], wA[:, kc, 1, 128 * ft:128 * ft + 128], xnT[:, kc, c0:c0 + cn], start=(kc == 0), stop=(kc == 7)),
                                      reads=[wA.tok(1), xnT.tok(ci)], writes=[bu.tok()], inc=(kc == 7))
                            sl = sil[ft % 2]
                            kb.op(ACT, lambda: S.activation(out=sl[:, 0:cn], in_=bg[:, 0:cn], func=AF.Silu), reads=[bg.tok()], writes=[sl.tok()])
                            kb.op(DVE, lambda: V.tensor_tensor(out=ab[:, ft, 0:cn], in0=bu[:, 0:cn], in1=sl[:, 0:cn], op=ALU.mult),
                                  reads=[bu.tok(), sl.tok()], writes=[ab.tok(ft)])
                        for dc in range(8):
                            bk = next_bank([4, 5, 6, 7])
                            for ft in range(fn_):
                                kb.op(PE, lambda ft=ft: T.matmul(bk[:, 0:cn], wD[:, ft, 128 * dc:128 * dc + 128], ab[:, ft, 0:cn], start=(ft == 0), stop=(ft == fn_ - 1)),
                                      reads=[wD.tok(), ab.tok(ft)], writes=[bk.tok()], inc=(ft == fn_ - 1))
                            kb.op(DVE, lambda: V.tensor_tensor(out=xT[:, dc, c0:c0 + cn], in0=bk[:, 0:cn], in1=xT[:, dc, c0:c0 + cn], op=ALU.add),
                                  reads=[bk.tok(), xT.tok(ci)], writes=[xT.tok(ci)])
                if stage == "layer":
                    dump("yT1", xT[:].rearrange("p c t -> p (c t)"), [xT.tok(c) for c in range(5)])
                    break

            if stage == "full":
                ph = Phase()
                sq = [ph.take([128, 512], BF16), ph.take([128, 512], BF16)]
                rt = ph.take([128, 512])
                yst = [ph.take([128, 8, 512]) for _ in range(2)]
                for ci, (c0, cn) in enumerate(CTS):
                    ys = yst[ci % 2]
                    ysv = Buf(ys.t[:, :, 0:cn])

                    class _Shift:
                        def __getitem__(self, idx):
                            p, c, cols = idx
                            return ys.t[p, c, cols.start - c0:cols.stop - c0]
                    rmsnorm_to_bf16(xT, xT.tok(ci), 8, _Shift(), ys.tok(), lambda c: pvec[:, 128 + c:129 + c], 1024.0, ci, sq, rt, [4, 5])
                    kb.dma(SP, o_y[:, :, c0:c0 + cn], ys[:, :, 0:cn], reads=[ys.tok()])

        except _Stop:
            pass
        kb.finish()
    return nc


def _consts():
    c = {}
    c["c_ident"] = np.eye(128, dtype=np.float32)
    k = np.arange(128)[:, None]
    q = np.arange(128)[None, :]
    cur = np.where(q >= k, 0.0, MASKNEG).astype(np.float32)
    prev = np.where(k >= q, 0.0, MASKNEG).astype(np.float32)
    c["c_maskcur"] = np.tile(cur, (1, 4))
    c["c_maskprev"] = np.tile(prev, (1, 4))
    c["c_smask"] = _sample_mult().reshape(128, 176)
    mv = np.zeros((128, NMV, 16), np.float32)
    mv[:, :, :] = np.array(MVALS, np.float32)[None, :, None]
    c["c_mv"] = mv.reshape(128, NMV * 16)
    rm = np.zeros((128, 2), np.float32)
    par = (np.arange(128) // 16) % 2
    rm[:, 0] = (par == 0)
    rm[:, 1] = (par == 1)
    c["c_rowmask"] = rm
    return c


def _vecT(v):
    L_, F_ = v.shape
    return v.reshape(L_, F_ // 128, 128).transpose(2, 0, 1)


def prep_core(inp, core, consts):
    m = dict(consts)
    xp = inp["x_prompt"][core]
    xs = inp["x_sample"][4 * core:4 * core + 4].reshape(32, 1024)
    x = np.concatenate([xp, xs], axis=0)
    m["xT"] = np.ascontiguousarray(x.reshape(NT, 8, 128).transpose(2, 1, 0))
    ck = inp["cache_attn_k"][:, 4 * core:4 * core + 4]
    cv = inp["cache_attn_v"][:, 4 * core:4 * core + 4]
    ckg = ck[:, :, SPOS].reshape(NL, 4, 1280, 4, 128)
    m["ckT"] = np.ascontiguousarray(ckg.transpose(0, 1, 3, 4, 2).reshape(NL, 16, 128, 1280))
    cvg = cv[:, :, SPOS].reshape(NL, 4, 10, 128, 4, 128)
    m["cvN"] = np.ascontiguousarray(cvg.transpose(0, 1, 4, 3, 2, 5).reshape(NL, 16, 128, 1280))
    pv = np.zeros((128, 136), np.float32)
    pvl = pv[:, :128].reshape(128, NL, 32)
    pvl[:, :, 0:8] = _vecT(inp["norm1_g"])
    pvl[:, :, 8:16] = _vecT(inp["norm2_g"])
    pvl[:, :, 16:20] = _vecT(inp["attn_out_g"])
    pvl[:, :, 20:24] = _vecT(inp["ssm_out_g"])
    pvl[:, :, 24:28] = _vecT(inp["ssm_glu_b"])
    pvl[:, :, 28:32] = _vecT(inp["ssm_d"].reshape(NL, 512))
    pv[:, 128:136] = inp["final_norm_g"].reshape(8, 128).T
    m["pvec"] = pv

    def gn(a):
        return np.ascontiguousarray(a.reshape(NL, 2, 16, 64).transpose(1, 3, 0, 2).reshape(128, NL, 16))
    m["aL_re"] = gn(inp["ssm_a_re"])
    m["aL_im"] = gn(inp["ssm_a_im"])
    m["aL_ldt"] = gn(np.broadcast_to(inp["ssm_log_dt"][:, :, None], (NL, 32, 64)))
    m["bL_re"] = np.ascontiguousarray(inp["ssm_b_re"].reshape(NL, 2, 16, 64, 16).transpose(1, 3, 0, 2, 4).reshape(128, NL, 256))
    m["bL_im"] = np.ascontiguousarray(inp["ssm_b_im"].reshape(NL, 2, 16, 64, 16).transpose(1, 3, 0, 2, 4).reshape(128, NL, 256))
    m["cL_re"] = np.ascontiguousarray(inp["ssm_c_re"].reshape(NL, 2, 16, 16, 64).transpose(1, 4, 0, 2, 3).reshape(128, NL, 256))
    m["cL_im"] = np.ascontiguousarray(inp["ssm_c_im"].reshape(NL, 2, 16, 16, 64).transpose(1, 4, 0, 2, 3).reshape(128, NL, 256))
    h0 = np.stack([inp["state_ssm_re"][:, 4 * core:4 * core + 4], inp["state_ssm_im"][:, 4 * core:4 * core + 4]], axis=0)
    m["h0L"] = np.ascontiguousarray(h0.reshape(2, NL, 4, 2, 16, 64).transpose(3, 5, 1, 0, 4, 2).reshape(128, NL, 2, 16, 4))
    m["w_in"] = inp["w_in"]
    m["w_out"] = inp["w_out"]
    m["glu_w"] = inp["ssm_glu_w"]
    m["wg"] = inp["ffn_w_gate"]
    m["wu"] = inp["ffn_w_up"]
    m["wd"] = inp["ffn_w_down"]
    return m


_CACHE = {}


def kernel(**inputs):
    inp = {k: np.asarray(v) for k, v in inputs.items()}
    if "nc" not in _CACHE:
        _CACHE["nc"] = build({})
    nc = _CACHE["nc"]
    consts = _consts()
    in_maps = [prep_core(inp, c, consts) for c in range(8)]
    res = run_bass_kernel_spmd(nc, in_maps, core_ids=list(range(8)))
    R = res.results
    y_p = np.zeros((8, NP, 1024), np.float32)
    y_s = np.zeros((32, 8, 1024), np.float32)
    pk = np.zeros((NL, 8, NP, 8, 64), np.float32)
    pvv = np.zeros((NL, 8, NP, 8, 64), np.float32)
    pr = np.zeros((NL, 8, 32, 64), np.float32)
    pi = np.zeros((NL, 8, 32, 64), np.float32)
    sk = np.zeros((NL, 32, 8, 8, 64), np.float32)
    sv = np.zeros((NL, 32, 8, 8, 64), np.float32)
    sr = np.zeros((NL, 32, 32, 64), np.float32)
    si = np.zeros((NL, 32, 32, 64), np.float32)
    for c in range(8):
        r = R[c]
        y = r["yT"].transpose(2, 1, 0).reshape(NT, 1024)
        y_p[c] = y[:NP]
        y_s[4 * c:4 * c + 4] = y[NP:].reshape(4, 8, 1024)
        kT = r["pkT"]
        kk = kT.transpose(0, 3, 2, 1).reshape(NL, NT, 512)
        pk[:, c] = kk[:, :NP].reshape(NL, NP, 8, 64)
        sk[:, 4 * c:4 * c + 4] = kk[:, NP:].reshape(NL, 4, 8, 8, 64)
        pvv[:, c] = r["pv"].reshape(NL, NP, 8, 64)
        sv[:, 4 * c:4 * c + 4] = r["sv"].reshape(NL, 4, 8, 8, 64)
        h = r["hout"].reshape(2, 64, NL, 2, 16, 5)
        hh = h.transpose(2, 3, 5, 0, 4, 1).reshape(NL, 2, 5, 32, 64)
        sr[:, 4 * c:4 * c + 4] = hh[:, 0, 0:4]
        si[:, 4 * c:4 * c + 4] = hh[:, 1, 0:4]
        pr[:, c] = hh[:, 0, 4]
        pi[:, c] = hh[:, 1, 4]
    return (y_p, y_s, pk, pvv, pr, pi, sk, sv, sr, si)
```
